# Optimizing a Trainium2 kernel written in Bass

```python
import jax, jax.numpy as jnp
from jax import lax
import numpy as np

D_MODEL = 1024
BATCH = 32
SEQ = 2048
DEPTH = 4
DEC_BATCH = 16
DEC_SEQ = 2048
PAST_LEN = 128

N_MIXERS = 3
N_LAYERS_A = (DEPTH + 2) // 3
N_LAYERS_B = (DEPTH + 1) // 3
N_LAYERS_C = DEPTH // 3

ROPE_THETA = 10000.0
NORM_EPS = 1e-6
NEG_BIG = -1e30

MLA_HEADS = 8
MLA_Q_LORA = 384
MLA_KV_LORA = 256
MLA_NOPE = 128
MLA_ROPE = 64
MLA_V = 128
MLA_QK = MLA_NOPE + MLA_ROPE
MLA_Q_BLOCK = 128

SWA_Q_HEADS = 16
SWA_KV_HEADS = 4
SWA_GROUP = SWA_Q_HEADS // SWA_KV_HEADS
SWA_HEAD_DIM = 64
SWA_HALF_WINDOW = 128
SWA_BLOCK = 128

DIL_CONFIGS = ((128, 1), (512, 4), (2048, 16))
DIL_GROUPS = len(DIL_CONFIGS)
DIL_HEADS_PER_GROUP = 8
DIL_HEAD_DIM = 64
DIL_BLOCK = 64

D_FF = -(-8 * D_MODEL // (3 * 256)) * 256

kernel_name = 'hybrid_mla_swa_dilated_encoder'


def rms_norm(x, g):
    xf = x.astype(jnp.float32)
    y = xf * lax.rsqrt(jnp.mean(xf * xf, axis=-1, keepdims=True) + NORM_EPS)
    return (y * g.astype(jnp.float32)).astype(x.dtype)


def rope_tables(seq, dim):
    inv = 1.0 / (ROPE_THETA ** (jnp.arange(0, dim, 2, dtype=jnp.float32) / dim))
    ang = jnp.arange(seq, dtype=jnp.float32)[:, None] * inv[None, :]
    return jnp.cos(ang), jnp.sin(ang)


def apply_rope(x, cos, sin):
    half = x.shape[-1] // 2
    xf = x.astype(jnp.float32)
    x1, x2 = xf[..., :half], xf[..., half:]
    c = cos[None, :, None, :]
    s = sin[None, :, None, :]
    return jnp.concatenate([x1 * c - x2 * s, x2 * c + x1 * s], axis=-1).astype(x.dtype)


def _pad_axis1(t, lo, hi):
    return jnp.pad(t, [(0, 0), (lo, hi)] + [(0, 0)] * (t.ndim - 2))


def banded_attention(q, k, v, half_window, block, sink=None):
    b, n, kvh, g, d = q.shape
    nb = -(-n // block)
    n_pad = nb * block
    qp = _pad_axis1(q, 0, n_pad - n)
    kp = _pad_axis1(k, block, n_pad - n + block)
    vp = _pad_axis1(v, block, n_pad - n + block)
    scale = d ** -0.5
    offs_q = jnp.arange(block)
    offs_k = jnp.arange(3 * block) - block

    def one_block(j):
        start = j * block
        qb = lax.dynamic_slice_in_dim(qp, start, block, axis=1).astype(jnp.float32)
        kb = lax.dynamic_slice_in_dim(kp, start, 3 * block, axis=1).astype(jnp.float32)
        vb = lax.dynamic_slice_in_dim(vp, start, 3 * block, axis=1).astype(jnp.float32)
        qi = start + offs_q
        ki = start + offs_k
        mask = (jnp.abs(qi[:, None] - ki[None, :]) <= half_window) & ((ki >= 0) & (ki < n))[None, :]
        s = jnp.einsum('bqhgd,bshd->bhgqs', qb, kb) * scale
        s = jnp.where(mask, s, NEG_BIG)
        m = jnp.max(s, axis=-1)
        if sink is not None:
            sk = sink.astype(jnp.float32)[None, :, :, None]
            m = jnp.maximum(m, sk)
        p = jnp.exp(s - m[..., None])
        den = jnp.sum(p, axis=-1)
        if sink is not None:
            den = den + jnp.exp(sk - m)
        den_t = jnp.transpose(den, (0, 3, 1, 2))
        o = jnp.einsum('bhgqs,bshd->bqhgd', p, vb) / den_t[..., None]
        lse = jnp.transpose(m + jnp.log(den), (0, 3, 1, 2))
        return o, lse

    o, lse = lax.map(one_block, jnp.arange(nb))
    o = jnp.moveaxis(o, 0, 1).reshape(b, n_pad, kvh, g, d)[:, :n]
    lse = jnp.moveaxis(lse, 0, 1).reshape(b, n_pad, kvh, g)[:, :n]
    return o, lse


def blocked_dense_attention(q, k, v, block):
    b, s, h, dk = q.shape
    dv = v.shape[-1]
    nb = s // block
    scale = dk ** -0.5
    kf = k.astype(jnp.float32)
    vf = v.astype(jnp.float32)
    qb = jnp.moveaxis(q.reshape(b, nb, block, h, dk), 1, 0)

    def one_block(qblk):
        sc = jnp.einsum('bqhd,bkhd->bhqk', qblk.astype(jnp.float32), kf) * scale
        p = jax.nn.softmax(sc, axis=-1)
        return jnp.einsum('bhqk,bkhd->bqhd', p, vf)

    o = lax.map(one_block, qb)
    return jnp.moveaxis(o, 0, 1).reshape(b, s, h, dv)


def mla_mixer(x, wq_a, q_a_norm, wq_b, wkv_a, kv_a_norm, wkv_b, q_norm, k_norm, wo):
    b, s, _ = x.shape
    cq = rms_norm(x @ wq_a, q_a_norm)
    q = (cq @ wq_b).reshape(b, s, MLA_HEADS, MLA_QK)
    kv_a = x @ wkv_a
    ckv = rms_norm(kv_a[..., :MLA_KV_LORA], kv_a_norm)
    k_rope = jnp.broadcast_to(kv_a[..., None, MLA_KV_LORA:], (b, s, MLA_HEADS, MLA_ROPE))
    kv = (ckv @ wkv_b).reshape(b, s, MLA_HEADS, MLA_NOPE + MLA_V)
    k = jnp.concatenate([kv[..., :MLA_NOPE], k_rope], axis=-1)
    v = kv[..., MLA_NOPE:]
    q = rms_norm(q, q_norm)
    k = rms_norm(k, k_norm)
    cos, sin = rope_tables(s, MLA_ROPE)
    q = jnp.concatenate([q[..., :MLA_NOPE], apply_rope(q[..., MLA_NOPE:], cos, sin)], axis=-1)
    k = jnp.concatenate([k[..., :MLA_NOPE], apply_rope(k[..., MLA_NOPE:], cos, sin)], axis=-1)
    o = blocked_dense_attention(q, k, v, MLA_Q_BLOCK)
    return o.reshape(b, s, MLA_HEADS * MLA_V).astype(x.dtype) @ wo


def swa_mixer(x, wqkv, q_norm, k_norm, sink, wo):
    b, s, _ = x.shape
    qkv = x @ wqkv
    nq = SWA_Q_HEADS * SWA_HEAD_DIM
    nk = SWA_KV_HEADS * SWA_HEAD_DIM
    q = qkv[..., :nq].reshape(b, s, SWA_Q_HEADS, SWA_HEAD_DIM)
    k = qkv[..., nq:nq + nk].reshape(b, s, SWA_KV_HEADS, SWA_HEAD_DIM)
    v = qkv[..., nq + nk:].reshape(b, s, SWA_KV_HEADS, SWA_HEAD_DIM)
    cos, sin = rope_tables(s, SWA_HEAD_DIM)
    q = apply_rope(rms_norm(q, q_norm), cos, sin)
    k = apply_rope(rms_norm(k, k_norm), cos, sin)
    q = q.reshape(b, s, SWA_KV_HEADS, SWA_GROUP, SWA_HEAD_DIM)
    o, _ = banded_attention(q, k, v, SWA_HALF_WINDOW, SWA_BLOCK, sink.reshape(SWA_KV_HEADS, SWA_GROUP))
    return o.reshape(b, s, SWA_Q_HEADS * SWA_HEAD_DIM).astype(x.dtype) @ wo


def dilated_mixer(x, wqkv, q_norm, k_norm, wo):
    b, s, _ = x.shape
    hg, d, ng = DIL_HEADS_PER_GROUP, DIL_HEAD_DIM, DIL_GROUPS
    qkv = (x @ wqkv).reshape(b, s, 3, ng * hg, d)
    cos, sin = rope_tables(s, d)
    q = apply_rope(rms_norm(qkv[:, :, 0], q_norm), cos, sin)
    k = apply_rope(rms_norm(qkv[:, :, 1], k_norm), cos, sin)
    v = qkv[:, :, 2]
    outs, lses = [], []
    for gi, (window, dil) in enumerate(DIL_CONFIGS):
        sl = slice(gi * hg, (gi + 1) * hg)
        half = window // (2 * dil)
        length = s // dil

        def to_strided(t):
            return jnp.transpose(t.reshape(b, length, dil, hg, d), (0, 2, 1, 3, 4)).reshape(b * dil, length, hg, d)

        qg = to_strided(q[:, :, sl])[:, :, :, None, :]
        o, lse = banded_attention(qg, to_strided(k[:, :, sl]), to_strided(v[:, :, sl]), half, DIL_BLOCK)
        o = jnp.transpose(o[:, :, :, 0].reshape(b, dil, length, hg, d), (0, 2, 1, 3, 4)).reshape(b, s, hg, d)
        lse = jnp.transpose(lse[..., 0].reshape(b, dil, length, hg), (0, 2, 1, 3)).reshape(b, s, hg)
        outs.append(o)
        lses.append(lse)
    o = jnp.stack(outs, axis=0)
    w = jax.nn.softmax(jnp.stack(lses, axis=0), axis=0)
    o = jnp.sum(w[..., None] * o, axis=0)
    return o.reshape(b, s, hg * d).astype(x.dtype) @ wo


def swiglu(x, w_gate, w_up, w_down):
    return (jax.nn.silu(x @ w_gate) * (x @ w_up)) @ w_down


def encoder_trunk(x, attn_norm, ffn_norm, w_gate, w_up, w_down,
                  mla_wq_a, mla_q_a_norm, mla_wq_b, mla_wkv_a, mla_kv_a_norm, mla_wkv_b,
                  mla_q_norm, mla_k_norm, mla_wo,
                  swa_wqkv, swa_q_norm, swa_k_norm, swa_sink, swa_wo,
                  dil_wqkv, dil_q_norm, dil_k_norm, dil_wo):
    for i in range(DEPTH):
        kind = i % N_MIXERS
        j = i // N_MIXERS
        h = rms_norm(x, attn_norm[i])
        if kind == 0:
            mix = mla_mixer(h, mla_wq_a[j], mla_q_a_norm[j], mla_wq_b[j], mla_wkv_a[j], mla_kv_a_norm[j],
                            mla_wkv_b[j], mla_q_norm[j], mla_k_norm[j], mla_wo[j])
        elif kind == 1:
            mix = swa_mixer(h, swa_wqkv[j], swa_q_norm[j], swa_k_norm[j], swa_sink[j], swa_wo[j])
        else:
            mix = dilated_mixer(h, dil_wqkv[j], dil_q_norm[j], dil_k_norm[j], dil_wo[j])
        x = x + mix
        x = x + swiglu(rms_norm(x, ffn_norm[i]), w_gate[i], w_up[i], w_down[i])
    return x


def setup_inputs(seed: int = 0) -> dict:
    key = jax.random.key(seed)
    ks = jax.random.split(key, 32)
    f32 = jnp.float32

    def nrm(k, shape, scale):
        return jax.random.normal(k, shape, f32) * scale

    def gain(k, shape):
        return 1.0 + 0.02 * jax.random.normal(k, shape, f32)

    na, nb_, nc = N_LAYERS_A, N_LAYERS_B, N_LAYERS_C
    return {
        'x_prompt': nrm(ks[0], (BATCH, SEQ, D_MODEL), 1.0),
        'x_sample': nrm(ks[1], (DEC_BATCH, DEC_SEQ, D_MODEL), 1.0),
        'attn_norm': gain(ks[2], (DEPTH, D_MODEL)),
        'ffn_norm': gain(ks[3], (DEPTH, D_MODEL)),
        'w_gate': nrm(ks[4], (DEPTH, D_MODEL, D_FF), D_MODEL ** -0.5),
        'w_up': nrm(ks[5], (DEPTH, D_MODEL, D_FF), D_MODEL ** -0.5),
        'w_down': nrm(ks[6], (DEPTH, D_FF, D_MODEL), D_FF ** -0.5),
        'mla_wq_a': nrm(ks[7], (na, D_MODEL, MLA_Q_LORA), D_MODEL ** -0.5),
        'mla_q_a_norm': gain(ks[8], (na, MLA_Q_LORA)),
        'mla_wq_b': nrm(ks[9], (na, MLA_Q_LORA, MLA_HEADS * MLA_QK), MLA_Q_LORA ** -0.5),
        'mla_wkv_a': nrm(ks[10], (na, D_MODEL, MLA_KV_LORA + MLA_ROPE), D_MODEL ** -0.5),
        'mla_kv_a_norm': gain(ks[11], (na, MLA_KV_LORA)),
        'mla_wkv_b': nrm(ks[12], (na, MLA_KV_LORA, MLA_HEADS * (MLA_NOPE + MLA_V)), MLA_KV_LORA ** -0.5),
        'mla_q_norm': gain(ks[13], (na, MLA_QK)),
        'mla_k_norm': gain(ks[14], (na, MLA_QK)),
        'mla_wo': nrm(ks[15], (na, MLA_HEADS * MLA_V, D_MODEL), (MLA_HEADS * MLA_V) ** -0.5),
        'swa_wqkv': nrm(ks[16], (nb_, D_MODEL, (SWA_Q_HEADS + 2 * SWA_KV_HEADS) * SWA_HEAD_DIM), D_MODEL ** -0.5),
        'swa_q_norm': gain(ks[17], (nb_, SWA_HEAD_DIM)),
        'swa_k_norm': gain(ks[18], (nb_, SWA_HEAD_DIM)),
        'swa_sink': nrm(ks[19], (nb_, SWA_Q_HEADS), 0.5),
        'swa_wo': nrm(ks[20], (nb_, SWA_Q_HEADS * SWA_HEAD_DIM, D_MODEL), (SWA_Q_HEADS * SWA_HEAD_DIM) ** -0.5),
        'dil_wqkv': nrm(ks[21], (nc, D_MODEL, 3 * DIL_GROUPS * DIL_HEADS_PER_GROUP * DIL_HEAD_DIM), D_MODEL ** -0.5),
        'dil_q_norm': gain(ks[22], (nc, DIL_HEAD_DIM)),
        'dil_k_norm': gain(ks[23], (nc, DIL_HEAD_DIM)),
        'dil_wo': nrm(ks[24], (nc, DIL_HEADS_PER_GROUP * DIL_HEAD_DIM, D_MODEL), (DIL_HEADS_PER_GROUP * DIL_HEAD_DIM) ** -0.5),
    }


def reference(x_prompt, x_sample, attn_norm, ffn_norm, w_gate, w_up, w_down,
              mla_wq_a, mla_q_a_norm, mla_wq_b, mla_wkv_a, mla_kv_a_norm, mla_wkv_b,
              mla_q_norm, mla_k_norm, mla_wo,
              swa_wqkv, swa_q_norm, swa_k_norm, swa_sink, swa_wo,
              dil_wqkv, dil_q_norm, dil_k_norm, dil_wo):
    weights = (attn_norm, ffn_norm, w_gate, w_up, w_down,
               mla_wq_a, mla_q_a_norm, mla_wq_b, mla_wkv_a, mla_kv_a_norm, mla_wkv_b,
               mla_q_norm, mla_k_norm, mla_wo,
               swa_wqkv, swa_q_norm, swa_k_norm, swa_sink, swa_wo,
               dil_wqkv, dil_q_norm, dil_k_norm, dil_wo)
    y_prompt = encoder_trunk(x_prompt, *weights)
    y_sample = encoder_trunk(x_sample, *weights)
    return (y_prompt, y_sample)
```

```python
import numpy as np
import concourse.bass as bass
import concourse.mybir as mybir
from concourse.bass_utils import run_bass_kernel_spmd
from contextlib import ExitStack

F32 = mybir.dt.float32
BF16 = mybir.dt.bfloat16
ALU = mybir.AluOpType
AF = mybir.ActivationFunctionType

D = 1024
DFF = 2816
NFB = 22
EPS = 1e-6
NEG = -30000.0
ENGS = ("pe", "act", "dve", "pool", "sp")
ARENA = 45056


class T:
    __slots__ = ("name", "last_w", "readers", "dsem", "dcount", "excl")

    def __init__(self, name, excl=False, fence=()):
        self.name = name
        self.excl = excl
        self.last_w = None
        self.readers = list(fence)
        self.dsem = None
        self.dcount = 0


class Prog:
    def __init__(self, nc):
        self.nc = nc
        self.ops = {e: [] for e in ENGS}
        self.seen = {e: {e2: -1 for e2 in ENGS} for e in ENGS}
        self.seen_d = {e: {} for e in ENGS}
        self.dma_tiles = []

    def fence(self):
        f = []
        for e in ("pe", "act", "dve", "pool"):
            for i in range(len(self.ops[e]) - 1, -1, -1):
                if self.ops[e][i]["dma"] is None:
                    f.append(("e", e, i))
                    break
        return f

    def _collect(self, eng, reads, writes):
        deps = []
        for t in reads:
            if t.last_w is not None:
                deps.append((t.last_w, "raw"))
        for t in writes:
            if t.last_w is not None:
                deps.append((t.last_w, "waw"))
            for r in t.readers:
                deps.append((r, "war"))
        waits = []
        for d, kind in deps:
            if d[0] == "e":
                _, e2, idx = d
                if e2 == eng:
                    if eng == "pe" or eng == "sp":
                        continue
                    if kind != "raw":
                        continue
                if self.seen[eng][e2] >= idx:
                    continue
                self.seen[eng][e2] = idx
                self.ops[e2][idx]["signal"] = True
                waits.append(("e", e2, idx))
            else:
                _, t, cnt = d
                if self.seen_d[eng].get(t, 0) >= cnt:
                    continue
                self.seen_d[eng][t] = cnt
                waits.append(("d", t, cnt))
        return waits

    def op(self, eng, fn, reads=(), writes=()):
        ex = [t for t in reads if t.excl]
        if ex:
            reads = [t for t in reads if not t.excl]
            writes = list(writes) + [t for t in ex if t not in writes]
        waits = self._collect(eng, reads, writes)
        idx = len(self.ops[eng])
        self.ops[eng].append(dict(fn=fn, waits=waits, signal=False, dma=None))
        me = ("e", eng, idx)
        for t in writes:
            t.last_w = me
            t.readers = []
        for t in reads:
            if t not in writes:
                t.readers.append(me)
        return idx

    def dma(self, eng, fn, reads=(), writes=(), sem_tile=None):
        waits = self._collect(eng, reads, writes)
        st = sem_tile or (writes[0] if writes else reads[0])
        if st.dsem is None:
            st.dsem = True
            self.dma_tiles.append(st)
        st.dcount += 16
        me = ("d", st, st.dcount)
        self.ops[eng].append(dict(fn=fn, waits=waits, signal=False, dma=st))
        for t in writes:
            t.last_w = me
            t.readers = []
        for t in reads:
            if t not in writes:
                t.readers.append(me)

    def emit(self, final_tiles=()):
        nc = self.nc
        with ExitStack() as es:
            esem = {e: es.enter_context(nc.semaphore("s_" + e)) for e in ENGS}
            for i, t in enumerate(self.dma_tiles):
                t.dsem = es.enter_context(nc.semaphore("d%d" % i))
            signum = {}
            for e in ENGS:
                c = 0
                for i, o in enumerate(self.ops[e]):
                    if o["signal"]:
                        c += 1
                        signum[(e, i)] = c
            block = es.enter_context(nc.Block())

            def run(e, engobj):
                for i, o in enumerate(self.ops[e]):
                    for w in o["waits"]:
                        if w[0] == "e":
                            engobj.wait_ge(esem[w[1]], signum[(w[1], w[2])])
                        else:
                            engobj.wait_ge(w[1].dsem, w[2])
                    ins = o["fn"](engobj)
                    if o["dma"] is not None:
                        ins.then_inc(o["dma"].dsem, 16)
                    if o["signal"]:
                        ins.then_inc(esem[e], 1)
                if e == "sp":
                    for t in final_tiles:
                        engobj.wait_ge(t.dsem, t.dcount)

            @block.tensor
            def _(eng):
                run("pe", eng)

            @block.scalar
            def _(eng):
                run("act", eng)

            @block.vector
            def _(eng):
                run("dve", eng)

            @block.gpsimd
            def _(eng):
                run("pool", eng)

            @block.sync
            def _(eng):
                run("sp", eng)


class Rot:
    def __init__(self, items):
        self.items = list(items)
        self.i = 0

    def next(self):
        v = self.items[self.i % len(self.items)]
        self.i += 1
        return v


def mask_specs():
    return [("swa", 128, 1), ("d1", 64, 1), ("d4", 256, 4), ("d16", 1024, 16)]


def mask_layout():
    off = 0
    lay = {}
    for name, W, dil in mask_specs():
        Wt = -(-W // 128) * 128
        OFF = 384 + Wt
        width = OFF + Wt + 512
        lay[name] = (off, OFF, Wt, width, W, dil)
        off += width
    return lay, off


def build_consts(S):
    inv = 1.0 / (10000.0 ** (np.arange(0, 64, 2, dtype=np.float32) / 64.0))
    ang = np.arange(S, dtype=np.float32)[:, None] * inv[None, :].astype(np.float32)
    cos = np.cos(ang).astype(np.float32).T
    sin = np.sin(ang).astype(np.float32).T
    p = np.arange(128)
    rope = np.zeros((128, 2, S), np.float32)
    rope[:, 0, :] = cos[p % 32]
    sgn = np.where((p % 64) < 32, -1.0, 1.0).astype(np.float32)
    rope[:, 1, :] = sin[p % 32] * sgn[:, None]
    cm = np.zeros((128, 6, 128), np.float32)
    cm[:, 0, :] = 1.0
    cm[:, 1, :] = (p[:, None] // 64 == p[None, :] // 64)
    cm[:, 2, :] = (p[:, None] < 64)
    cm[:, 3, :] = (p[:, None] >= 64)
    cm[:, 4, :] = (p[:, None] == p[None, :])
    partner = np.where((p % 64) < 32, p + 32, p - 32)
    cm[:, 5, :] = (p[:, None] == partner[None, :])
    lay, tot = mask_layout()
    mt = np.full((128, tot), NEG, np.float32)
    for name, (off, OFF, Wt, width, W, dil) in lay.items():
        c = np.arange(width)
        delta = p[:, None] - c[None, :] + OFF
        valid = (np.abs(delta) <= W) & (delta % dil == 0)
        mt[:, off:off + width] = np.where(valid, 0.0, NEG)
    return rope, cm, mt


def gain_layout():
    lay = {}
    c = 0
    for l in range(4):
        lay[("attn", l)] = c; c += 8
        lay[("ffn", l)] = c; c += 8
    for j in range(2):
        lay[("qa", j)] = c; c += 3
        lay[("kva", j)] = c; c += 2
        lay[("qn_nope", j)] = c; c += 1
        lay[("qn_rope", j)] = c; c += 1
        lay[("kn_nope", j)] = c; c += 1
        lay[("kn_rope", j)] = c; c += 1
    lay["swa_q"] = c; c += 1
    lay["swa_k"] = c; c += 1
    lay["dil_q"] = c; c += 1
    lay["dil_k"] = c; c += 1
    lay["sink"] = c; c += 16
    return lay, c


def build_gains(inp):
    lay, n = gain_layout()
    g = np.zeros((128, n), np.float32)

    def put(col, vec):
        v = np.asarray(vec, np.float32)
        k = v.shape[0] // 128
        g[:, col:col + k] = v.reshape(k, 128).T

    for l in range(4):
        put(lay[("attn", l)], inp["attn_norm"][l])
        put(lay[("ffn", l)], inp["ffn_norm"][l])
    for j in range(2):
        put(lay[("qa", j)], inp["mla_q_a_norm"][j])
        put(lay[("kva", j)], inp["mla_kv_a_norm"][j])
        qn = np.asarray(inp["mla_q_norm"][j]); kn = np.asarray(inp["mla_k_norm"][j])
        put(lay[("qn_nope", j)], qn[:128])
        put(lay[("qn_rope", j)], np.concatenate([qn[128:], qn[128:]]))
        put(lay[("kn_nope", j)], kn[:128])
        put(lay[("kn_rope", j)], np.concatenate([kn[128:], kn[128:]]))
    put(lay["swa_q"], np.tile(np.asarray(inp["swa_q_norm"][0]), 2))
    put(lay["swa_k"], np.tile(np.asarray(inp["swa_k_norm"][0]), 2))
    put(lay["dil_q"], np.tile(np.asarray(inp["dil_q_norm"][0]), 2))
    put(lay["dil_k"], np.tile(np.asarray(inp["dil_k_norm"][0]), 2))
    g[:, lay["sink"]:lay["sink"] + 16] = np.broadcast_to(np.asarray(inp["swa_sink"][0], np.float32)[None, :], (128, 16))
    return g


def prep_weights(inp):
    f = lambda a: np.ascontiguousarray(np.asarray(a, np.float32))
    w = {}
    w["w_gate"] = f(inp["w_gate"]); w["w_up"] = f(inp["w_up"]); w["w_down"] = f(inp["w_down"])
    w["mla_wq_a"] = f(inp["mla_wq_a"])
    kva = np.asarray(inp["mla_wkv_a"], np.float32)
    w["mla_wkv_a"] = f(np.concatenate([kva, kva[:, :, 256:320]], axis=2))
    qb = np.asarray(inp["mla_wq_b"], np.float32).reshape(2, 384, 8, 192)
    cols = []
    for hp in range(4):
        cols += [qb[:, :, 2 * hp, :128], qb[:, :, 2 * hp + 1, :128], qb[:, :, 2 * hp, 128:], qb[:, :, 2 * hp + 1, 128:]]
    w["mla_wq_b"] = f(np.concatenate(cols, axis=2))
    kvb = np.asarray(inp["mla_wkv_b"], np.float32).reshape(2, 256, 8, 256)
    w["mla_wkb"] = f(kvb[:, :, :, :128].reshape(2, 256, 1024))
    w["mla_wvb"] = f(kvb[:, :, :, 128:].reshape(2, 256, 1024))
    w["mla_wo"] = f(inp["mla_wo"])
    sw = np.asarray(inp["swa_wqkv"], np.float32)[0]
    w["swa_wq"] = f(sw[:, :1024])
    k = sw[:, 1024:1280].reshape(1024, 4, 64)
    w["swa_wk"] = f(np.concatenate([k, k], axis=2).reshape(1024, 512))
    w["swa_wv"] = f(sw[:, 1280:1536])
    w["swa_wo"] = f(np.asarray(inp["swa_wo"], np.float32)[0])
    w["dil_wqkv"] = f(np.asarray(inp["dil_wqkv"], np.float32)[0])
    w["dil_wo"] = f(np.asarray(inp["dil_wo"], np.float32)[0])
    return w


WSHAPES = {
    "w_gate": [4, D, DFF], "w_up": [4, D, DFF], "w_down": [4, DFF, D],
    "mla_wq_a": [2, D, 384], "mla_wkv_a": [2, D, 384], "mla_wq_b": [2, 384, 1536],
    "mla_wkb": [2, 256, 1024], "mla_wvb": [2, 256, 1024], "mla_wo": [2, D, D],
    "swa_wq": [D, 1024], "swa_wk": [D, 512], "swa_wv": [D, 256], "swa_wo": [D, D],
    "dil_wqkv": [D, 4608], "dil_wo": [512, D],
}


class Builder:
    def __init__(self, S, NSEQ, layers=(0, 1, 2, 3), do_ffn=True):
        self.S, self.NSEQ, self.layers, self.do_ffn = S, NSEQ, tuple(layers), do_ffn
        self.NC = S // 512
        self.NT = S // 128
        self.glay, self.NG = gain_layout()
        self.mlay, self.MW = mask_layout()

    def mm(self, out, lhsT, rhs, start, stop, reads, w):
        self.P.op("pe", lambda e: e.matmul(out, lhsT, rhs, start=start, stop=stop), reads=reads, writes=[w])

    def act(self, out, in_, func, reads, writes, scale=1.0, bias=None):
        if bias is None:
            self.P.op("act", lambda e: e.activation(out=out, in_=in_, func=func, scale=scale), reads=reads, writes=writes)
        else:
            self.P.op("act", lambda e: e.activation(out=out, in_=in_, func=func, scale=scale, bias=bias), reads=reads, writes=writes)

    def tt(self, out, in0, in1, op, reads, writes, eng="dve"):
        self.P.op(eng, lambda e: e.tensor_tensor(out=out, in0=in0, in1=in1, op=op), reads=reads, writes=writes)

    def stt(self, out, in0, scalar, in1, op0, op1, reads, writes):
        self.P.op("dve", lambda e: e.scalar_tensor_tensor(out=out, in0=in0, scalar=scalar, in1=in1, op0=op0, op1=op1),
                  reads=reads, writes=writes)

    def tsmul(self, out, in0, scalar, reads, writes):
        self.P.op("dve", lambda e: e.tensor_scalar_mul(out=out, in0=in0, scalar1=scalar), reads=reads, writes=writes)

    def tsadd(self, out, in0, scalar, reads, writes):
        self.P.op("dve", lambda e: e.tensor_scalar_add(out=out, in0=in0, scalar1=scalar), reads=reads, writes=writes)

    def recip(self, out, in_, reads, writes):
        self.P.op("dve", lambda e: e.reciprocal(out=out, in_=in_), reads=reads, writes=writes)

    def copy(self, eng, out, in_, reads, writes):
        if eng == "act":
            self.P.op("act", lambda e: e.activation(out=out, in_=in_, func=AF.Copy), reads=reads, writes=writes)
        else:
            self.P.op(eng, lambda e: e.tensor_copy(out=out, in_=in_), reads=reads, writes=writes)

    def load_w(self, wap2d, kc, c0, n, slot=None):
        if slot is None:
            slot = self.wrot.next()
        sap, st = slot
        dst = sap[:, 0:kc * n].rearrange("p (c n) -> p c n", c=kc)
        src = wap2d.rearrange("(c p) n -> p c n", p=128)[:, :, c0:c0 + n]
        self.P.dma("pool", lambda e: e.dma_start(out=dst, in_=src), writes=[st])
        return dst, st

    def dbg(self, ap2d, tile, n=512):
        if not getattr(self, "debug", False) or self.dbg_i >= self.NDBG:
            return
        i = self.dbg_i
        self.dbg_i += 1
        stg = self.dbg_stage[i]
        stT = T("dbgst%d" % i)
        self.P.op("dve", lambda e: e.tensor_copy(out=stg[:, 0:n], in_=ap2d), reads=[tile], writes=[stT])
        self.P.dma("sp", lambda e: e.dma_start(out=self.dbg_out[i][:, 0:n], in_=stg[:, 0:n]), reads=[stT], sem_tile=self.dbg_sem)

    def arena_reset(self):
        self.aoff = 0
        self.fence = self.P.fence()

    def abf(self, n, name):
        assert self.aoff + n <= ARENA, (name, self.aoff, n)
        ap = self.arena[:, self.aoff:self.aoff + n]
        self.aoff += n
        return ap

    def af32(self, n, name):
        assert self.aoff % 2 == 0
        assert self.aoff + 2 * n <= ARENA, (name, self.aoff, n)
        ap = self.arena[:, self.aoff:self.aoff + 2 * n].bitcast(F32)
        self.aoff += 2 * n
        return ap

    def nT(self, name):
        return T(name, fence=self.fence)

    def gcol(self, key, k=0):
        c = self.glay[key] + k
        return self.gains[:, c:c + 1]

    def norm_x(self, t, gkey, hn, hnT, sqrot, r, rT):
        S = self.S
        ps, pst = self.bank[7]
        for c in range(8):
            sq, sqt = sqrot.next()
            xa = self.x3[:, c, t * 512:(t + 1) * 512]
            self.act(sq, xa, AF.Square, [self.xT[c][t]], [sqt])
            self.mm(ps, self.cm[:, 0, :], sq, c == 0, c == 7, [sqt, self.cmT], pst)
        self.act(r, ps, AF.Sqrt, [pst, self.cT], [rT], scale=1.0 / D, bias=self.epsb)
        self.recip(r, r, [rT], [rT])
        for c in range(8):
            xa = self.x3[:, c, t * 512:(t + 1) * 512]
            self.stt(hn[:, c, :], xa, self.gcol(gkey, c), r, ALU.mult, ALU.mult, [self.xT[c][t], rT, self.cT], [hnT])

    def rstd(self, r, rT, ps, pst, n):
        self.act(r, ps, AF.Sqrt, [pst, self.cT], [rT], scale=1.0 / n, bias=self.epsb)
        self.recip(r, r, [rT], [rT])

    def rope_chunk(self, p, pT, gcol, t, kg, kgT, t1, t1T, t2, t2T):
        swb, swT = self.bank[6]
        tok = slice(t * 512, (t + 1) * 512)
        self.tsmul(kg, p, gcol, [pT, self.cT], [kgT])
        self.mm(swb, self.cm[:, 5, :], kg, True, True, [kgT, self.cmT], swT)
        self.stt(t1, p, gcol, self.rope[:, 0, tok], ALU.mult, ALU.mult, [pT, self.cT], [t1T])
        self.tt(t2, swb, self.rope[:, 1, tok], ALU.mult, [swT, self.cT], [t2T])
        self.tt(t1, t1, t2, ALU.add, [t1T, t2T], [t1T])

    def ffn(self, l):
        S, P = self.S, self.P
        HT = min(S, 1024)
        NCH = HT // 512
        for half in range(S // HT):
            self.arena_reset()
            hn = self.abf(8 * HT, "hn").rearrange("p (c s) -> p c s", c=8)
            hnT = [self.nT("hn%d" % i) for i in range(NCH)]
            ffh = self.abf(NFB * HT, "ffh").rearrange("p (f s) -> p f s", f=NFB)
            ffhT = [[self.nT("ffh") for _ in range(NCH)] for _ in range(NFB)]
            sqrot = Rot([(self.abf(512, "sq"), self.nT("sq")) for _ in range(3)])
            sgrot = Rot([(self.abf(512, "sg"), self.nT("sg")) for _ in range(2)])
            r = self.af32(512, "r"); rT = self.nT("r")
            for i in range(NCH):
                t = half * NCH + i
                self.norm_x(t, ("ffn", l), hn[:, :, i * 512:(i + 1) * 512], hnT[i], sqrot, r, rT)
            grot = Rot([self.bank[0], self.bank[1]])
            urot = Rot([self.bank[2], self.bank[3]])
            drot = Rot([self.bank[4], self.bank[5]])
            for fb in range(NFB):
                wg, wgT = self.load_w(self.W["w_gate"][l], 8, fb * 128, 128)
                wu, wuT = self.load_w(self.W["w_up"][l], 8, fb * 128, 128)
                for i in range(NCH):
                    pg, pgT = grot.next()
                    pu, puT = urot.next()
                    hs = hn[:, :, i * 512:(i + 1) * 512]
                    for c in range(8):
                        self.mm(pg, wg[:, c, :], hs[:, c, :], c == 0, c == 7, [wgT, hnT[i]], pgT)
                    for c in range(8):
                        self.mm(pu, wu[:, c, :], hs[:, c, :], c == 0, c == 7, [wuT, hnT[i]], puT)
                    sg, sgT = sgrot.next()
                    self.act(sg, pg, AF.Silu, [pgT], [sgT])
                    self.tt(ffh[:, fb, i * 512:(i + 1) * 512], pu, sg, ALU.mult, [puT, sgT], [ffhT[fb][i]])
            for dc in range(8):
                slot = self.wdrot.next()
                sap, st = slot
                dst = sap[:, :].rearrange("p (f n) -> p f n", f=NFB)
                src = self.W["w_down"][l].rearrange("(f p) n -> p f n", p=128)[:, :, dc * 128:(dc + 1) * 128]
                P.dma("pool", lambda e, dst=dst, src=src: e.dma_start(out=dst, in_=src), writes=[st])
                for i in range(NCH):
                    t = half * NCH + i
                    pd, pdT = drot.next()
                    for fb in range(NFB):
                        self.mm(pd, dst[:, fb, :], ffh[:, fb, i * 512:(i + 1) * 512], fb == 0, fb == NFB - 1,
                                [st, ffhT[fb][i]], pdT)
                    xa = self.x3[:, dc, t * 512:(t + 1) * 512]
                    self.tt(xa, pd, xa, ALU.add, [pdT], [self.xT[dc][t]])

    def mla(self, l, j):
        S, P, NC, NT = self.S, self.P, self.NC, self.NT
        W = self.W
        self.arena_reset()
        cqn = self.abf(3 * S, "cqn").rearrange("p (c s) -> p c s", c=3)
        ckvn = self.abf(2 * S, "ckvn").rearrange("p (c s) -> p c s", c=2)
        U = self.abf(S, "U")
        sqkr = self.abf(S, "sqkr")
        cqnT = [self.nT("cqn") for _ in range(NC)]
        ckvnT = [self.nT("ckvn") for _ in range(NC)]
        UT = [self.nT("U") for _ in range(NC)]
        sqkrT = [self.nT("sqkr") for _ in range(NC)]
        persist = self.aoff
        hnrot = Rot([(self.abf(8 * 512, "hn").rearrange("p (c s) -> p c s", c=8), self.nT("hn")) for _ in range(2)])
        sqrot = Rot([(self.abf(512, "sq"), self.nT("sq")) for _ in range(3)])
        rrot = Rot([(self.af32(512, "r"), self.nT("r")) for _ in range(2)])
        t1 = self.af32(512, "t1"); t1T = self.nT("t1")
        t2 = self.af32(512, "t2"); t2T = self.nT("t2")
        kg = self.abf(512, "kg"); kgT = self.nT("kg")
        prot = Rot([self.bank[i] for i in range(6)])
        for t in range(NC):
            tok = slice(t * 512, (t + 1) * 512)
            hn, hnT = hnrot.next()
            r, rT = rrot.next()
            self.norm_x(t, ("attn", l), hn, hnT, sqrot, r, rT)
            pq = []
            for jj in range(3):
                w_, wT = self.load_w(W["mla_wq_a"][j], 8, jj * 128, 128)
                p, pT = prot.next()
                for c in range(8):
                    self.mm(p, w_[:, c, :], hn[:, c, :], c == 0, c == 7, [wT, hnT], pT)
                pq.append((p, pT))
            ss, ssT = self.bank[7]
            for jj in range(3):
                sq, sqt = sqrot.next()
                self.act(sq, pq[jj][0], AF.Square, [pq[jj][1]], [sqt])
                self.mm(ss, self.cm[:, 0, :], sq, jj == 0, jj == 2, [sqt, self.cmT], ssT)
            r, rT = rrot.next()
            self.rstd(r, rT, ss, ssT, 384)
            for jj in range(3):
                self.stt(cqn[:, jj, tok], pq[jj][0], self.gcol(("qa", j), jj), r, ALU.mult, ALU.mult,
                         [pq[jj][1], rT, self.cT], [cqnT[t]])
            pk = []
            for jj in range(3):
                w_, wT = self.load_w(W["mla_wkv_a"][j], 8, jj * 128, 128)
                p, pT = prot.next()
                for c in range(8):
                    self.mm(p, w_[:, c, :], hn[:, c, :], c == 0, c == 7, [wT, hnT], pT)
                pk.append((p, pT))
            for jj in range(2):
                sq, sqt = sqrot.next()
                self.act(sq, pk[jj][0], AF.Square, [pk[jj][1]], [sqt])
                self.mm(ss, self.cm[:, 0, :], sq, jj == 0, jj == 1, [sqt, self.cmT], ssT)
            r, rT = rrot.next()
            self.rstd(r, rT, ss, ssT, 256)
            for jj in range(2):
                self.stt(ckvn[:, jj, tok], pk[jj][0], self.gcol(("kva", j), jj), r, ALU.mult, ALU.mult,
                         [pk[jj][1], rT, self.cT], [ckvnT[t]])
            self.act(sqkr[:, tok], pk[2][0], AF.Square, [pk[2][1]], [sqkrT[t]])
            self.rope_chunk(pk[2][0], pk[2][1], self.gcol(("kn_rope", j)), t, kg, kgT, t1, t1T, t2, t2T)
            self.copy("dve", U[:, tok], t1, [t1T], [UT[t]])
        self.aoff = persist
        self.fence = P.fence()
        kT = self.abf(3 * S, "kT").rearrange("p (c s) -> p c s", c=3)
        kTT = [self.nT("kT") for _ in range(NC)]
        V = self.abf(NT * 256, "V").rearrange("p (t n) -> p t n", n=256)
        VT = [self.nT("V") for _ in range(NC)]
        qrot = Rot([(self.abf(3 * 512, "qT").rearrange("p (c s) -> p c s", c=3), self.nT("qT")) for _ in range(2)])
        orot = Rot([(self.abf(2 * 512, "oT").rearrange("p (c s) -> p c s", c=2), self.nT("oT")) for _ in range(2)])
        ptrot = Rot([(self.abf(512, "pt"), self.nT("pt")) for _ in range(4)])
        sqrot = Rot([(self.abf(512, "sq"), self.nT("sq")) for _ in range(3)])
        rrot = Rot([(self.af32(512, "r"), self.nT("r")) for _ in range(4)])
        t1 = self.af32(512, "t1"); t1T = self.nT("t1")
        t2 = self.af32(512, "t2"); t2T = self.nT("t2")
        kg = self.abf(512, "kg"); kgT = self.nT("kg")
        arot = Rot([self.bank[0], self.bank[1], self.bank[2]])
        accrot = Rot([(self.bank[3], self.bank[4]), (self.bank[5], self.bank[6])])
        scale = 192.0 ** -0.5
        for hp in range(4):
            wk = [self.load_w(W["mla_wkb"][j], 2, (2 * hp + h) * 128, 128) for h in range(2)]
            wv, wvT = self.load_w(W["mla_wvb"][j], 2, 2 * hp * 128, 256)
            for t in range(NC):
                tok = slice(t * 512, (t + 1) * 512)
                rk = []
                for h in range(2):
                    p, pT = arot.next()
                    for c in range(2):
                        self.mm(p, wk[h][0][:, c, :], ckvn[:, c, tok], c == 0, c == 1, [wk[h][1], ckvnT[t]], pT)
                    sq, sqt = sqrot.next()
                    self.act(sq, p, AF.Square, [pT], [sqt])
                    ss, ssT = self.bank[7]
                    self.mm(ss, self.cm[:, 0, :], sq, True, False, [sqt, self.cmT], ssT)
                    self.mm(ss, self.cm[:, 2, :], sqkr[:, tok], False, True, [sqkrT[t], self.cmT], ssT)
                    r, rT = rrot.next()
                    self.rstd(r, rT, ss, ssT, 192)
                    self.stt(kT[:, h, tok], p, self.gcol(("kn_nope", j)), r, ALU.mult, ALU.mult,
                             [pT, rT, self.cT], [kTT[t]])
                    rk.append((r, rT))
                for h in range(2):
                    rows = slice(64 * h, 64 * h + 64)
                    self.tt(kT[rows, 2, tok], U[rows, tok], rk[h][0][rows, :], ALU.mult, [UT[t], rk[h][1]], [kTT[t]])
                for i2 in range(2):
                    p, pT = arot.next()
                    for ii in range(2):
                        tile_ = t * 4 + i2 * 2 + ii
                        for c in range(2):
                            self.mm(p[:, ii * 256:(ii + 1) * 256], ckvn[:, c, tile_ * 128:(tile_ + 1) * 128], wv[:, c, :],
                                    c == 0, c == 1, [wvT, ckvnT[t]], pT)
                    t0_ = t * 4 + i2 * 2
                    self.copy("act", V[:, t0_:t0_ + 2, :], p.rearrange("p (t n) -> p t n", n=256), [pT], [VT[t]])
            for qc in range(NC):
                tok = slice(qc * 512, (qc + 1) * 512)
                qT, qTT = qrot.next()
                pq = []
                for jj in range(3):
                    w_, wT = self.load_w(W["mla_wq_b"][j], 3, hp * 384 + jj * 128, 128)
                    p, pT = arot.next()
                    for c in range(3):
                        self.mm(p, w_[:, c, :], cqn[:, c, tok], c == 0, c == 2, [wT, cqnT[qc]], pT)
                    pq.append((p, pT))
                sqs = []
                for jj in range(3):
                    sq, sqt = sqrot.next()
                    self.act(sq, pq[jj][0], AF.Square, [pq[jj][1]], [sqt])
                    sqs.append((sq, sqt))
                rq = []
                for h in range(2):
                    ss, ssT = self.bank[7]
                    self.mm(ss, self.cm[:, 0, :], sqs[h][0], True, False, [sqs[h][1], self.cmT], ssT)
                    self.mm(ss, self.cm[:, 2 + h, :], sqs[2][0], False, True, [sqs[2][1], self.cmT], ssT)
                    r, rT = rrot.next()
                    self.rstd(r, rT, ss, ssT, 192)
                    rq.append((r, rT))
                    self.stt(qT[:, h, :], pq[h][0], self.gcol(("qn_nope", j)), r, ALU.mult, ALU.mult,
                             [pq[h][1], rT, self.cT], [qTT])
                self.rope_chunk(pq[2][0], pq[2][1], self.gcol(("qn_rope", j)), qc, kg, kgT, t1, t1T, t2, t2T)
                for h in range(2):
                    rows = slice(64 * h, 64 * h + 64)
                    self.tt(qT[rows, 2, :], t1[rows, :], rq[h][0][rows, :], ALU.mult, [t1T, rq[h][1]], [qTT])
                oT, oTT = orot.next()
                for h in range(2):
                    rows = slice(64 * h, 64 * h + 64)
                    (po, poT), (pdn, pdnT) = accrot.next()
                    pend = None
                    for kt in range(NT + 1):
                        if kt < NT:
                            ktok = slice(kt * 128, (kt + 1) * 128)
                            ps, psT = arot.next()
                            self.mm(ps, kT[:, h, ktok], qT[:, h, :], True, False, [kTT[kt // 4], qTT], psT)
                            self.mm(ps, kT[rows, 2, ktok], qT[rows, 2, :], False, True, [kTT[kt // 4], qTT], psT)
                            pt, ptT = ptrot.next()
                            self.act(pt, ps, AF.Exp, [psT], [ptT], scale=scale)
                            nxt = (kt, pt, ptT)
                        else:
                            nxt = None
                        if pend is not None:
                            k0, pt0, ptT0 = pend
                            self.mm(po, V[:, k0, h * 128:(h + 1) * 128], pt0, k0 == 0, k0 == NT - 1, [VT[k0 // 4], ptT0], poT)
                            self.mm(pdn, self.cm[:, 0, :], pt0, k0 == 0, k0 == NT - 1, [self.cmT, ptT0], pdnT)
                        pend = nxt
                    r, rT = rrot.next()
                    self.recip(r, pdn, [pdnT], [rT])
                    self.tt(oT[:, h, :], po, r, ALU.mult, [poT, rT], [oTT])
                for hf in range(2):
                    slot = self.wrot.next()
                    sap, st = slot
                    dst = sap[:, 0:1024].rearrange("p (c n) -> p c n", c=2)
                    src = W["mla_wo"][j][2 * hp * 128:(2 * hp + 2) * 128, :].rearrange("(c p) n -> p c n", p=128)[:, :, hf * 512:(hf + 1) * 512]
                    P.dma("pool", lambda e, dst=dst, src=src: e.dma_start(out=dst, in_=src), writes=[st])
                    for d4 in range(4):
                        dc = hf * 4 + d4
                        pw, pwT = arot.next()
                        for h in range(2):
                            self.mm(pw, dst[:, h, d4 * 128:(d4 + 1) * 128], oT[:, h, :], h == 0, h == 1, [st, oTT], pwT)
                        xa = self.x3[:, dc, tok]
                        self.tt(xa, pw, xa, ALU.add, [pwT], [self.xT[dc][qc]])

    def qk_chunk(self, w2d, col0, hn, hnT, gcol, t, dst, dstT, st):
        sqrot, rrot, t1, t1T, t2, t2T, kg, kgT, prot = st
        w_, wT = self.load_w(w2d, 8, col0, 128)
        p, pT = prot.next()
        for c in range(8):
            self.mm(p, w_[:, c, :], hn[:, c, :], c == 0, c == 7, [wT, hnT], pT)
        sq, sqt = sqrot.next()
        self.act(sq, p, AF.Square, [pT], [sqt])
        ss, ssT = self.bank[7]
        self.mm(ss, self.cm[:, 1, :], sq, True, True, [sqt, self.cmT], ssT)
        r, rT = rrot.next()
        self.rstd(r, rT, ss, ssT, 64)
        self.rope_chunk(p, pT, gcol, t, kg, kgT, t1, t1T, t2, t2T)
        self.tt(dst, t1, r, ALU.mult, [t1T, rT], [dstT])

    def band_attn(self, mname, qT, qTT, kfn, vfn, VT, qc, nheads, hinfo, po_epilogue, ptrot, arot, scale):
        NT = self.NT
        off, OFF, Wt, width, W, dil = self.mlay[mname]
        dt_max = -(-W // 128)
        kts = list(range(max(0, 4 * qc - Wt // 128), min(NT - 1, 4 * qc + 3 + Wt // 128) + 1))
        for hh in range(nheads):
            qch, rows, kch = hinfo(hh)
            po, poT = self.accr.next()
            contrib = {jq: [kt for kt in kts if abs(kt - (4 * qc + jq)) <= dt_max] for jq in range(4)}
            pv_list = [(kt, jq) for kt in kts for jq in range(4) if kt in contrib[jq]]
            pv_first, pv_last = pv_list[0], pv_list[-1]
            pend = None
            for kt in kts + [None]:
                if kt is not None:
                    ktok = slice(kt * 128, (kt + 1) * 128)
                    ps, psT = arot.next()
                    u0 = off + 512 * qc - 128 * kt + OFF
                    self.mm(ps, self.cm[:, 4, :], self.mtab[:, u0:u0 + 512], True, False, [self.cmT], psT)
                    kap, kTt = kfn(kch, rows, ktok, kt)
                    self.mm(ps, kap, qT[rows, qch, :], False, True, [kTt, qTT], psT)
                    pt, ptT = ptrot.next()
                    self.act(pt, ps, AF.Exp, [psT], [ptT], scale=scale)
                    nxt = (kt, pt, ptT)
                else:
                    nxt = None
                if pend is not None:
                    k0, pt0, ptT0 = pend
                    vap = vfn(hh, k0)
                    for jq in range(4):
                        cl = contrib[jq]
                        if k0 in cl:
                            self.mm(po[:, jq * 128:jq * 128 + 65], pt0[:, jq * 128:(jq + 1) * 128], vap,
                                    (k0, jq) == pv_first, (k0, jq) == pv_last, [VT[k0 // 4], ptT0], poT)
                pend = nxt
            po_epilogue(hh, po, poT)

    def swa(self, l):
        S, P, NC, NT = self.S, self.P, self.NC, self.NT
        W = self.W
        scale = 64.0 ** -0.5
        self.arena_reset()
        hnF = self.abf(8 * S, "hnF").rearrange("p (c s) -> p c s", c=8)
        hnFT = [self.nT("hnF") for _ in range(NC)]
        sq0 = Rot([(self.abf(512, "sq"), self.nT("sq")) for _ in range(3)])
        r0 = self.af32(512, "r"); r0T = self.nT("r")
        for t in range(NC):
            self.norm_x(t, ("attn", l), hnF[:, :, t * 512:(t + 1) * 512], hnFT[t], sq0, r0, r0T)
        mark = self.aoff
        for g in range(4):
            self.aoff = mark
            self.fence = P.fence()
            qT = self.abf(2 * S, "qT").rearrange("p (c s) -> p c s", c=2)
            qTT = [self.nT("qT") for _ in range(NC)]
            kT = self.abf(S, "kT")
            kTT = [self.nT("kT") for _ in range(NC)]
            Va = self.abf(NT * 80, "Va").rearrange("p (t n) -> p t n", n=80)
            VT = [self.nT("Va") for _ in range(NC)]
            otm_rot = Rot([(self.abf(4 * 256, "otm").rearrange("p (t n) -> p t n", n=256), self.nT("otm")) for _ in range(2)])
            oTrot = Rot([(self.abf(2 * 512, "oT").rearrange("p (c s) -> p c s", c=2), self.nT("oT")) for _ in range(2)])
            ptrot = Rot([(self.abf(512, "pt"), self.nT("pt")) for _ in range(4)])
            sqrot = Rot([(self.abf(512, "sq"), self.nT("sq")) for _ in range(3)])
            rrot = Rot([(self.af32(512, "r"), self.nT("r")) for _ in range(3)])
            t1 = self.af32(512, "t1"); t1T = self.nT("t1")
            t2 = self.af32(512, "t2"); t2T = self.nT("t2")
            kg = self.abf(512, "kg"); kgT = self.nT("kg")
            den = self.af32(8, "den"); denT = self.nT("den")
            prot = Rot([self.bank[0], self.bank[1], self.bank[2]])
            st = (sqrot, rrot, t1, t1T, t2, t2T, kg, kgT, prot)
            for t in range(NC):
                P.op("pool", lambda e, a=Va[:, t * 4:(t + 1) * 4, 64:65]: e.memset(a, 1.0), writes=[VT[t]])
            for t in range(NC):
                tok = slice(t * 512, (t + 1) * 512)
                hn, hnT = hnF[:, :, tok], hnFT[t]
                for jj in range(2):
                    self.qk_chunk(W["swa_wq"], (2 * g + jj) * 128, hn, hnT, self.gcol("swa_q"), t, qT[:, jj, tok], qTT[t], st)
                self.qk_chunk(W["swa_wk"], g * 128, hn, hnT, self.gcol("swa_k"), t, kT[:, tok], kTT[t], st)
                wv, wvT = self.load_w(W["swa_wv"], 8, g * 64, 64)
                p, pT = prot.next()
                for ii in range(4):
                    tl = t * 4 + ii
                    for c in range(8):
                        self.mm(p[:, ii * 64:(ii + 1) * 64], hn[:, c, ii * 128:(ii + 1) * 128], wv[:, c, :], c == 0, c == 7,
                                [wvT, hnT], pT)
                self.copy("act", Va[:, t * 4:(t + 1) * 4, 0:64], p[:, 0:256].rearrange("p (t n) -> p t n", n=64), [pT], [VT[t]])
            if g == 1:
                self.dbg(qT[:, 0, 0:512], qTT[0])
                self.dbg(qT[:, 1, 0:512], qTT[0])
                self.dbg(kT[:, 0:512], kTT[0])
                self.dbg(Va[:, 0:4, :].rearrange("p t n -> p (t n)"), VT[0], n=320)
            arot = Rot([self.bank[0], self.bank[1], self.bank[2]])
            self.accr = Rot([self.bank[3], self.bank[4], self.bank[5]])
            sinkc = self.glay["sink"]
            for qc in range(NC):
                tok = slice(qc * 512, (qc + 1) * 512)
                otm, otmT = otm_rot.next()

                def epi(hh, po, poT, otm=otm, otmT=otmT, g=g):
                    po3 = po.rearrange("p (t n) -> p t n", n=128)
                    hq = 4 * g + hh
                    self.tsadd(den[:, 0:4], po3[:, :, 64], self.esink[:, hq:hq + 1], [poT, self.cT], [denT])
                    self.recip(den[:, 0:4], den[:, 0:4], [denT], [denT])
                    self.tt(otm[:, :, hh * 64:(hh + 1) * 64], po3[:, :, 0:64],
                            den[:, 0:4].unsqueeze(2).broadcast_to([128, 4, 64]), ALU.mult, [poT, denT], [otmT])

                self.band_attn("swa", qT[:, :, tok], qTT[qc],
                               lambda kch, rows, ktok, kt: (kT[rows, ktok], kTT[kt // 4]),
                               lambda hh, k0: Va[:, k0, 0:65], VT, qc, 4,
                               lambda hh: (hh // 2, slice(64 * (hh % 2), 64 * (hh % 2) + 64), 0),
                               epi, ptrot, arot, scale)
                if qc == 0:
                    self.dbg(otm[:, 0:2, :].rearrange("p t n -> p (t n)"), otmT)
                    self.dbg(otm[:, 2:4, :].rearrange("p t n -> p (t n)"), otmT)
                self.otm_to_x(otm, otmT, 2, oTrot, W["swa_wo"], g * 256, qc, arot)

    def otm_to_x(self, otm, otmT, nfc, oTrot, wo2d, row0, qc, arot):
        P = self.P
        tok = slice(qc * 512, (qc + 1) * 512)
        oT, oTT = oTrot.next()
        pb, pbT = self.bank[7]
        pbb = pb.bitcast(BF16)
        for fc in range(nfc):
            for jq in range(4):
                P.op("pe", lambda e, o=pbb[:, fc * 512 + jq * 128: fc * 512 + (jq + 1) * 128],
                     i=otm[:, jq, fc * 128:(fc + 1) * 128]: e.transpose(o, i, self.cm[:, 4, :]),
                     reads=[otmT, self.cmT], writes=[pbT])
        self.copy("act", oT, pbb[:, 0:nfc * 512].rearrange("p (c s) -> p c s", c=nfc), [pbT], [oTT])
        if getattr(self, "debug", False) and self.dbg_i in (96, 97):
            self.dbg(oT[:, 0, :], oTT)
            self.dbg(oT[:, 1, :], oTT)
        for hf in range(2):
            slot = self.wrot.next()
            sap, st = slot
            n = 1024 // nfc
            assert n == 512
            dst = sap[:, 0:1024].rearrange("p (c n) -> p c n", c=nfc)
            src = wo2d[row0:row0 + nfc * 128, :].rearrange("(c p) n -> p c n", p=128)[:, :, hf * 512:(hf + 1) * 512]
            P.dma("pool", lambda e, dst=dst, src=src: e.dma_start(out=dst, in_=src), writes=[st])
            for d4 in range(4):
                dc = hf * 4 + d4
                pw, pwT = arot.next()
                for fc in range(nfc):
                    self.mm(pw, dst[:, fc, d4 * 128:(d4 + 1) * 128], oT[:, fc, :], fc == 0, fc == nfc - 1, [st, oTT], pwT)
                xa = self.x3[:, dc, tok]
                self.tt(xa, pw, xa, ALU.add, [pwT], [self.xT[dc][qc]])

    def dil(self, l):
        S, P, NC, NT = self.S, self.P, self.NC, self.NT
        W = self.W
        scale = 64.0 ** -0.5
        names = ["d1", "d4", "d16"]
        self.arena_reset()
        otmF = self.abf(NT * 512, "otmF").rearrange("p (t n) -> p t n", n=512)
        otmFT = [self.nT("otmF") for _ in range(NC)]
        mark = self.aoff
        for hf in range(2):
            self.aoff = mark
            self.fence = P.fence()
            hn = self.abf(8 * 512, "hn").rearrange("p (c s) -> p c s", c=8); hnT = self.nT("hn")
            qT = self.abf(2 * S, "qT").rearrange("p (c s) -> p c s", c=2)
            kT = self.abf(2 * S, "kT").rearrange("p (c s) -> p c s", c=2)
            Va = self.abf(NT * 320, "Va").rearrange("p (t h n) -> p t h n", h=4, n=80)
            nacc = self.af32(NT * 320, "nacc").rearrange("p (t h n) -> p t h n", h=4, n=80)
            naccT = [[self.nT("nacc") for _ in range(4)] for _ in range(NC)]
            ptrot = Rot([(self.abf(512, "pt"), self.nT("pt")) for _ in range(4)])
            sqrot = Rot([(self.abf(512, "sq"), self.nT("sq")) for _ in range(3)])
            rrot = Rot([(self.af32(512, "r"), self.nT("r")) for _ in range(2)])
            t1 = self.af32(512, "t1"); t1T = self.nT("t1")
            t2 = self.af32(512, "t2"); t2T = self.nT("t2")
            kg = self.abf(512, "kg"); kgT = self.nT("kg")
            den = self.af32(8, "den"); denT = self.nT("den")
            prot = Rot([self.bank[0], self.bank[1], self.bank[2]])
            st = (sqrot, rrot, t1, t1T, t2, t2T, kg, kgT, prot)
            for gi in range(3):
                qTT = [self.nT("qT") for _ in range(NC)]
                kTT = [self.nT("kT") for _ in range(NC)]
                VT = [self.nT("Va") for _ in range(NC)]
                if gi > 0:
                    f = P.fence()
                    for lst in (qTT, kTT, VT):
                        for tt_ in lst:
                            tt_.readers = list(f)
                for t in range(NC):
                    P.op("pool", lambda e, a=Va[:, t * 4:(t + 1) * 4, :, 64:65]: e.memset(a, 1.0), writes=[VT[t]])
                for t in range(NC):
                    tok = slice(t * 512, (t + 1) * 512)
                    r, rT = rrot.next()
                    self.norm_x(t, ("attn", l), hn, hnT, sqrot, r, rT)
                    for jj in range(2):
                        col = gi * 512 + hf * 256 + jj * 128
                        self.qk_chunk(W["dil_wqkv"], col, hn, hnT, self.gcol("dil_q"), t, qT[:, jj, tok], qTT[t], st)
                        self.qk_chunk(W["dil_wqkv"], 1536 + col, hn, hnT, self.gcol("dil_k"), t, kT[:, jj, tok], kTT[t], st)
                    wv, wvT = self.load_w(W["dil_wqkv"], 8, 3072 + gi * 512 + hf * 256, 128)
                    wv2, wv2T = self.load_w(W["dil_wqkv"], 8, 3072 + gi * 512 + hf * 256 + 128, 128)
                    for i2 in range(2):
                        p, pT = prot.next()
                        for ii in range(2):
                            tl = i2 * 2 + ii
                            for (wv_, wvT_, cc) in ((wv, wvT, 0), (wv2, wv2T, 128)):
                                for c in range(8):
                                    self.mm(p[:, ii * 256 + cc: ii * 256 + cc + 128], hn[:, c, tl * 128:(tl + 1) * 128], wv_[:, c, :],
                                            c == 0, c == 7, [wvT_, hnT], pT)
                        t0_ = t * 4 + i2 * 2
                        self.copy("act", Va[:, t0_:t0_ + 2, :, 0:64], p.rearrange("p (t h n) -> p t h n", h=4, n=64), [pT], [VT[t]])
                arot = Rot([self.bank[0], self.bank[1], self.bank[2]])
                self.accr = Rot([self.bank[3], self.bank[4], self.bank[5]])
                for qc in range(NC):
                    tok = slice(qc * 512, (qc + 1) * 512)

                    def epi(hh, po, poT, qc=qc, gi=gi):
                        po3 = po.rearrange("p (t n) -> p t n", n=128)[:, :, 0:65]
                        dst = nacc[:, qc * 4:(qc + 1) * 4, hh, 0:65]
                        if gi == 0:
                            self.copy("dve", dst, po3, [poT], [naccT[qc][hh]])
                        else:
                            self.tt(dst, po3, dst, ALU.add, [poT], [naccT[qc][hh]])

                    self.band_attn(names[gi], qT[:, :, tok], qTT[qc],
                                   lambda kch, rows, ktok, kt: (kT[rows, kch, ktok], kTT[kt // 4]),
                                   lambda hh, k0: Va[:, k0, hh, 0:65], VT, qc, 4,
                                   lambda hh: (hh // 2, slice(64 * (hh % 2), 64 * (hh % 2) + 64), hh // 2),
                                   epi, ptrot, arot, scale)
            for qc in range(NC):
                for hh in range(4):
                    nq = nacc[:, qc * 4:(qc + 1) * 4, hh, :]
                    c0 = hf * 256 + hh * 64
                    self.recip(den[:, 0:4], nq[:, :, 64], [naccT[qc][hh]], [denT])
                    self.tt(otmF[:, qc * 4:(qc + 1) * 4, c0:c0 + 64], nq[:, :, 0:64],
                            den[:, 0:4].unsqueeze(2).broadcast_to([128, 4, 64]), ALU.mult, [naccT[qc][hh], denT], [otmFT[qc]])
        self.aoff = mark
        self.fence = P.fence()
        oTrot = Rot([(self.abf(2 * 512, "oT").rearrange("p (c s) -> p c s", c=2), self.nT("oT")) for _ in range(2)])
        arot = Rot([self.bank[0], self.bank[1], self.bank[2]])
        for qc in range(NC):
            for hf in range(2):
                self.otm_to_x(otmF[:, qc * 4:(qc + 1) * 4, hf * 256:(hf + 1) * 256], otmFT[qc], 2, oTrot, W["dil_wo"], hf * 256, qc, arot)

    def build(self):
        S, NSEQ = self.S, self.NSEQ
        nc = bass.Bass("TRN2", target_bir_lowering=False)
        self.nc = nc
        xin = nc.dram_tensor("xT", [NSEQ, 8, 128, S], F32, kind="ExternalInput").ap()
        yout = nc.dram_tensor("yT", [NSEQ, 8, 128, S], F32, kind="ExternalOutput").ap()
        rope_h = nc.dram_tensor("rope", [128, 2, S], F32, kind="ExternalInput").ap()
        cm_h = nc.dram_tensor("cmat", [128, 6, 128], F32, kind="ExternalInput").ap()
        mt_h = nc.dram_tensor("mtab", [128, self.MW], F32, kind="ExternalInput").ap()
        g_h = nc.dram_tensor("gains", [128, self.NG], F32, kind="ExternalInput").ap()
        self.W = {k: nc.dram_tensor(k, shp, F32, kind="ExternalInput").ap() for k, shp in WSHAPES.items()}
        self.NDBG = 12
        self.dbg_i = 0
        if getattr(self, "debug", False):
            self.dbg_out = nc.dram_tensor("dbg", [self.NDBG, 128, 512], F32, kind="ExternalOutput").ap()
            self.dbg_sem = T("dbgsem")
        with ExitStack() as es:
            sb = lambda name, shape, dt: es.enter_context(nc.sbuf_tensor(name, shape, dt))
            xs = sb("xs", [128, 8 * S], F32)
            self.x3 = xs[:, :].rearrange("p (c s) -> p c s", c=8)
            self.rope = sb("rope_sb", [128, 2 * S], BF16)[:, :].rearrange("p (c s) -> p c s", c=2)
            self.cm = sb("cm_sb", [128, 6 * 128], BF16)[:, :].rearrange("p (c s) -> p c s", c=6)
            self.mtab = sb("mtab_sb", [128, self.MW], BF16)[:, :]
            self.gains = sb("gains_sb", [128, self.NG], F32)[:, :]
            self.esink = sb("esink", [128, 16], F32)[:, :]
            self.epsb = sb("epsb", [128, 1], F32)[:, :]
            self.arena = sb("arena", [128, ARENA], BF16)[:, :]
            wslots = [sb("ws%d" % i, [128, 1024], BF16)[:, :] for i in range(8)]
            wdslots = [sb("wd%d" % i, [128, NFB * 128], BF16)[:, :] for i in range(2)]
            banks = [es.enter_context(nc.psum_tensor("pb%d" % i, [128, 512], F32))[:, :] for i in range(8)]
            if getattr(self, "debug", False):
                self.dbg_stage = [sb("dbgst%d" % i, [128, 512], F32)[:, :] for i in range(self.NDBG)]
            P = Prog(nc)
            self.P = P
            self.bank = [(banks[i], T("bank%d" % i, excl=True)) for i in range(8)]
            self.wrot = Rot([(wslots[i], T("ws%d" % i)) for i in range(8)])
            self.wdrot = Rot([(wdslots[i], T("wd%d" % i)) for i in range(2)])
            self.cT = T("consts")
            self.cmT = T("cm")
            self.xT = [[T("x%d_%d" % (c, t)) for t in range(self.NC)] for c in range(8)]
            xsem = [T("xsem%d" % c) for c in range(8)]
            self.fence = []
            P.dma("pool", lambda e: e.dma_start(out=self.rope, in_=rope_h), writes=[self.cT])
            P.dma("sp", lambda e: e.dma_start(out=self.gains, in_=g_h), writes=[self.cT])
            P.dma("pool", lambda e: e.dma_start(out=self.cm, in_=cm_h), writes=[self.cmT])
            P.dma("pool", lambda e: e.dma_start(out=self.mtab, in_=mt_h), writes=[self.cmT])
            P.op("dve", lambda e: e.memset(self.epsb, EPS), writes=[self.cT])
            sc = self.glay["sink"]
            self.act(self.esink, self.gains[:, sc:sc + 16], AF.Exp, [self.cT], [self.cT])
            for s in range(NSEQ):
                for c in range(8):
                    P.dma("sp", lambda e, c=c, s=s: e.dma_start(out=self.x3[:, c, :], in_=xin[s, c]),
                          writes=self.xT[c], sem_tile=xsem[c])
                for l in self.layers:
                    kind = l % 3
                    if kind == 0:
                        self.mla(l, l // 3)
                    elif kind == 1:
                        self.swa(l)
                    else:
                        self.dil(l)
                    if self.do_ffn:
                        self.ffn(l)
                for c in range(8):
                    P.dma("sp", lambda e, c=c, s=s: e.dma_start(out=yout[s, c], in_=self.x3[:, c, :]),
                          reads=self.xT[c], sem_tile=xsem[c])
            fin = list(xsem)
            if getattr(self, "debug", False) and self.dbg_i > 0:
                fin.append(self.dbg_sem)
            P.emit(final_tiles=fin)
        return nc


_CACHE = {}


def run_device(xT_cores, inp, S, NSEQ, layers=(0, 1, 2, 3), do_ffn=True):
    key = (S, NSEQ, tuple(layers), do_ffn)
    if key not in _CACHE:
        _CACHE[key] = Builder(S, NSEQ, layers, do_ffn).build()
    nc = _CACHE[key]
    rope, cm, mt = build_consts(S)
    gains = build_gains(inp)
    w = prep_weights(inp)
    in_maps = []
    for xc in xT_cores:
        m = {"xT": xc, "rope": rope, "cmat": cm, "mtab": mt, "gains": gains}
        m.update(w)
        in_maps.append(m)
    res = run_bass_kernel_spmd(nc, in_maps, core_ids=list(range(len(xT_cores))))
    return [r["yT"] for r in res.results]


def kernel(**inputs):
    xp = np.asarray(inputs["x_prompt"], np.float32)
    xs_ = np.asarray(inputs["x_sample"], np.float32)
    B, S, _ = xp.shape
    Bd = xs_.shape[0]
    allx = np.concatenate([xp, xs_], axis=0)
    n = allx.shape[0]
    NSEQ = n // 8
    order = np.arange(n).reshape(8, NSEQ)
    cores = []
    for c in range(8):
        xc = allx[order[c]]
        xT = np.ascontiguousarray(xc.transpose(0, 2, 1)).reshape(NSEQ, 8, 128, S)
        cores.append(xT)
    outs = run_device(cores, inputs, S, NSEQ)
    y = np.empty_like(allx)
    for c in range(8):
        yT = np.asarray(outs[c], np.float32).reshape(NSEQ, D, S)
        y[order[c]] = yT.transpose(0, 2, 1)
    return (np.ascontiguousarray(y[:B]), np.ascontiguousarray(y[B:]))
```

```python
import numpy as np
import concourse.bass as bass
import concourse.mybir as mybir
from concourse.bass_utils import run_bass_kernel_spmd
from contextlib import ExitStack

F32 = mybir.dt.float32
BF16 = mybir.dt.bfloat16
ALU = mybir.AluOpType
AF = mybir.ActivationFunctionType

D = 1024
DFF = 2816
NFB = 22
EPS = 1e-6
NEG = -30000.0
ENGS = ("pe", "act", "dve", "pool", "sp")
ARENA = 46080


class T:
    __slots__ = ("name", "last_w", "readers", "dsem", "dcount", "excl")

    def __init__(self, name, excl=False, fence=()):
        self.name = name
        self.excl = excl
        self.last_w = None
        self.readers = list(fence)
        self.dsem = None
        self.dcount = 0


class Prog:
    def __init__(self, nc):
        self.nc = nc
        self.ops = {e: [] for e in ENGS}
        self.seen = {e: {e2: -1 for e2 in ENGS} for e in ENGS}
        self.seen_d = {e: {} for e in ENGS}
        self.dma_tiles = []

    def fence(self):
        f = []
        for e in ("pe", "act", "dve", "pool"):
            for i in range(len(self.ops[e]) - 1, -1, -1):
                if self.ops[e][i]["dma"] is None:
                    f.append(("e", e, i))
                    break
        return f

    def _collect(self, eng, reads, writes):
        deps = []
        for t in reads:
            if t.last_w is not None:
                deps.append((t.last_w, "raw"))
        for t in writes:
            if t.last_w is not None:
                deps.append((t.last_w, "waw"))
            for r in t.readers:
                deps.append((r, "war"))
        waits = []
        for d, kind in deps:
            if d[0] == "e":
                _, e2, idx = d
                if e2 == eng:
                    if eng == "pe" or eng == "sp":
                        continue
                    if kind != "raw":
                        continue
                if self.seen[eng][e2] >= idx:
                    continue
                self.seen[eng][e2] = idx
                self.ops[e2][idx]["signal"] = True
                waits.append(("e", e2, idx))
            else:
                _, t, cnt = d
                if self.seen_d[eng].get(t, 0) >= cnt:
                    continue
                self.seen_d[eng][t] = cnt
                waits.append(("d", t, cnt))
        return waits

    def op(self, eng, fn, reads=(), writes=()):
        ex = [t for t in reads if t.excl]
        if ex:
            reads = [t for t in reads if not t.excl]
            writes = list(writes) + [t for t in ex if t not in writes]
        waits = self._collect(eng, reads, writes)
        idx = len(self.ops[eng])
        self.ops[eng].append(dict(fn=fn, waits=waits, signal=False, dma=None))
        me = ("e", eng, idx)
        for t in writes:
            t.last_w = me
            t.readers = []
        for t in reads:
            if t not in writes:
                t.readers.append(me)
        return idx

    def dma(self, eng, fn, reads=(), writes=(), sem_tile=None):
        waits = self._collect(eng, reads, writes)
        st = sem_tile or (writes[0] if writes else reads[0])
        if st.dsem is None:
            st.dsem = True
            self.dma_tiles.append(st)
        st.dcount += 16
        me = ("d", st, st.dcount)
        self.ops[eng].append(dict(fn=fn, waits=waits, signal=False, dma=st))
        for t in writes:
            t.last_w = me
            t.readers = []
        for t in reads:
            if t not in writes:
                t.readers.append(me)

    def emit(self, final_tiles=()):
        nc = self.nc
        with ExitStack() as es:
            esem = {e: es.enter_context(nc.semaphore("s_" + e)) for e in ENGS}
            for i, t in enumerate(self.dma_tiles):
                t.dsem = es.enter_context(nc.semaphore("d%d" % i))
            signum = {}
            for e in ENGS:
                c = 0
                for i, o in enumerate(self.ops[e]):
                    if o["signal"]:
                        c += 1
                        signum[(e, i)] = c
            block = es.enter_context(nc.Block())

            def run(e, engobj):
                for i, o in enumerate(self.ops[e]):
                    for w in o["waits"]:
                        if w[0] == "e":
                            engobj.wait_ge(esem[w[1]], signum[(w[1], w[2])])
                        else:
                            engobj.wait_ge(w[1].dsem, w[2])
                    ins = o["fn"](engobj)
                    if o["dma"] is not None:
                        ins.then_inc(o["dma"].dsem, 16)
                    if o["signal"]:
                        ins.then_inc(esem[e], 1)
                if e == "sp":
                    for t in final_tiles:
                        engobj.wait_ge(t.dsem, t.dcount)

            @block.tensor
            def _(eng):
                run("pe", eng)

            @block.scalar
            def _(eng):
                run("act", eng)

            @block.vector
            def _(eng):
                run("dve", eng)

            @block.gpsimd
            def _(eng):
                run("pool", eng)

            @block.sync
            def _(eng):
                run("sp", eng)


class Rot:
    def __init__(self, items):
        self.items = list(items)
        self.i = 0

    def next(self):
        v = self.items[self.i % len(self.items)]
        self.i += 1
        return v


def mask_specs():
    return [("swa", 128, 1), ("d1", 64, 1), ("d4", 256, 4), ("d16", 1024, 16)]


def mask_layout():
    off = 0
    lay = {}
    for name, W, dil in mask_specs():
        Wt = -(-W // 128) * 128
        OFF = 384 + Wt
        width = OFF + Wt + 512
        lay[name] = (off, OFF, Wt, width, W, dil)
        off += width
    return lay, off


def build_consts(S):
    inv = 1.0 / (10000.0 ** (np.arange(0, 64, 2, dtype=np.float32) / 64.0))
    ang = np.arange(S, dtype=np.float32)[:, None] * inv[None, :].astype(np.float32)
    cos = np.cos(ang).astype(np.float32).T
    sin = np.sin(ang).astype(np.float32).T
    p = np.arange(128)
    rope = np.zeros((128, 2, S), np.float32)
    rope[:, 0, :] = cos[p % 32]
    sgn = np.where((p % 64) < 32, -1.0, 1.0).astype(np.float32)
    rope[:, 1, :] = sin[p % 32] * sgn[:, None]
    cm = np.zeros((128, 6, 128), np.float32)
    cm[:, 0, :] = 1.0
    cm[:, 1, :] = (p[:, None] // 64 == p[None, :] // 64)
    cm[:, 2, :] = (p[:, None] < 64)
    cm[:, 3, :] = (p[:, None] >= 64)
    cm[:, 4, :] = (p[:, None] == p[None, :])
    partner = np.where((p % 64) < 32, p + 32, p - 32)
    cm[:, 5, :] = (p[:, None] == partner[None, :])
    lay, tot = mask_layout()
    mt = np.full((128, tot), NEG, np.float32)
    for name, (off, OFF, Wt, width, W, dil) in lay.items():
        c = np.arange(width)
        delta = p[:, None] - c[None, :] + OFF
        valid = (np.abs(delta) <= W) & (delta % dil == 0)
        mt[:, off:off + width] = np.where(valid, 0.0, NEG)
    return rope, cm, mt


def gain_layout():
    lay = {}
    c = 0
    for l in range(4):
        lay[("attn", l)] = c; c += 8
        lay[("ffn", l)] = c; c += 8
    for j in range(2):
        lay[("qa", j)] = c; c += 3
        lay[("kva", j)] = c; c += 2
        lay[("qn_nope", j)] = c; c += 1
        lay[("qn_rope", j)] = c; c += 1
        lay[("kn_nope", j)] = c; c += 1
        lay[("kn_rope", j)] = c; c += 1
    lay["swa_q"] = c; c += 1
    lay["swa_k"] = c; c += 1
    lay["dil_q"] = c; c += 1
    lay["dil_k"] = c; c += 1
    lay["sink"] = c; c += 16
    return lay, c


def build_gains(inp):
    lay, n = gain_layout()
    g = np.zeros((128, n), np.float32)

    def put(col, vec):
        v = np.asarray(vec, np.float32)
        k = v.shape[0] // 128
        g[:, col:col + k] = v.reshape(k, 128).T

    for l in range(4):
        put(lay[("attn", l)], inp["attn_norm"][l])
        put(lay[("ffn", l)], inp["ffn_norm"][l])
    for j in range(2):
        put(lay[("qa", j)], inp["mla_q_a_norm"][j])
        put(lay[("kva", j)], inp["mla_kv_a_norm"][j])
        qn = np.asarray(inp["mla_q_norm"][j]); kn = np.asarray(inp["mla_k_norm"][j])
        put(lay[("qn_nope", j)], qn[:128])
        put(lay[("qn_rope", j)], np.concatenate([qn[128:], qn[128:]]))
        put(lay[("kn_nope", j)], kn[:128])
        put(lay[("kn_rope", j)], np.concatenate([kn[128:], kn[128:]]))
    put(lay["swa_q"], np.tile(np.asarray(inp["swa_q_norm"][0]), 2))
    put(lay["swa_k"], np.tile(np.asarray(inp["swa_k_norm"][0]), 2))
    put(lay["dil_q"], np.tile(np.asarray(inp["dil_q_norm"][0]), 2))
    put(lay["dil_k"], np.tile(np.asarray(inp["dil_k_norm"][0]), 2))
    g[:, lay["sink"]:lay["sink"] + 16] = np.broadcast_to(np.asarray(inp["swa_sink"][0], np.float32)[None, :], (128, 16))
    return g


def prep_weights(inp):
    f = lambda a: np.ascontiguousarray(np.asarray(a, np.float32))
    w = {}
    w["w_gate"] = f(inp["w_gate"]); w["w_up"] = f(inp["w_up"]); w["w_down"] = f(inp["w_down"])
    w["mla_wq_a"] = f(inp["mla_wq_a"])
    kva = np.asarray(inp["mla_wkv_a"], np.float32)
    w["mla_wkv_a"] = f(np.concatenate([kva, kva[:, :, 256:320]], axis=2))
    qb = np.asarray(inp["mla_wq_b"], np.float32).reshape(2, 384, 8, 192)
    cols = []
    for hp in range(4):
        cols += [qb[:, :, 2 * hp, :128], qb[:, :, 2 * hp + 1, :128], qb[:, :, 2 * hp, 128:], qb[:, :, 2 * hp + 1, 128:]]
    w["mla_wq_b"] = f(np.concatenate(cols, axis=2))
    kvb = np.asarray(inp["mla_wkv_b"], np.float32).reshape(2, 256, 8, 256)
    w["mla_wkb"] = f(kvb[:, :, :, :128].reshape(2, 256, 1024))
    w["mla_wvb"] = f(kvb[:, :, :, 128:].reshape(2, 256, 1024))
    w["mla_wo"] = f(inp["mla_wo"])
    sw = np.asarray(inp["swa_wqkv"], np.float32)[0]
    w["swa_wq"] = f(sw[:, :1024])
    k = sw[:, 1024:1280].reshape(1024, 4, 64)
    w["swa_wk"] = f(np.concatenate([k, k], axis=2).reshape(1024, 512))
    w["swa_wv"] = f(sw[:, 1280:1536])
    w["swa_wo"] = f(np.asarray(inp["swa_wo"], np.float32)[0])
    w["dil_wqkv"] = f(np.asarray(inp["dil_wqkv"], np.float32)[0])
    w["dil_wo"] = f(np.asarray(inp["dil_wo"], np.float32)[0])
    return w


WSHAPES = {
    "w_gate": [4, D, DFF], "w_up": [4, D, DFF], "w_down": [4, DFF, D],
    "mla_wq_a": [2, D, 384], "mla_wkv_a": [2, D, 384], "mla_wq_b": [2, 384, 1536],
    "mla_wkb": [2, 256, 1024], "mla_wvb": [2, 256, 1024], "mla_wo": [2, D, D],
    "swa_wq": [D, 1024], "swa_wk": [D, 512], "swa_wv": [D, 256], "swa_wo": [D, D],
    "dil_wqkv": [D, 4608], "dil_wo": [512, D],
}


class Builder:
    def __init__(self, S, NSEQ, layers=(0, 1, 2, 3), do_ffn=True):
        self.S, self.NSEQ, self.layers, self.do_ffn = S, NSEQ, tuple(layers), do_ffn
        self.NC = S // 512
        self.NT = S // 128
        self.glay, self.NG = gain_layout()
        self.mlay, self.MW = mask_layout()

    def mm(self, out, lhsT, rhs, start, stop, reads, w):
        self.P.op("pe", lambda e: e.matmul(out, lhsT, rhs, start=start, stop=stop), reads=reads, writes=[w])

    def act(self, out, in_, func, reads, writes, scale=1.0, bias=None):
        if bias is None:
            self.P.op("act", lambda e: e.activation(out=out, in_=in_, func=func, scale=scale), reads=reads, writes=writes)
        else:
            self.P.op("act", lambda e: e.activation(out=out, in_=in_, func=func, scale=scale, bias=bias), reads=reads, writes=writes)

    def tt(self, out, in0, in1, op, reads, writes, eng="dve"):
        self.P.op(eng, lambda e: e.tensor_tensor(out=out, in0=in0, in1=in1, op=op), reads=reads, writes=writes)

    def stt(self, out, in0, scalar, in1, op0, op1, reads, writes):
        self.P.op("dve", lambda e: e.scalar_tensor_tensor(out=out, in0=in0, scalar=scalar, in1=in1, op0=op0, op1=op1),
                  reads=reads, writes=writes)

    def tsmul(self, out, in0, scalar, reads, writes):
        self.P.op("dve", lambda e: e.tensor_scalar_mul(out=out, in0=in0, scalar1=scalar), reads=reads, writes=writes)

    def tsadd(self, out, in0, scalar, reads, writes):
        self.P.op("dve", lambda e: e.tensor_scalar_add(out=out, in0=in0, scalar1=scalar), reads=reads, writes=writes)

    def recip(self, out, in_, reads, writes):
        self.P.op("dve", lambda e: e.reciprocal(out=out, in_=in_), reads=reads, writes=writes)

    def copy(self, eng, out, in_, reads, writes):
        if eng == "act":
            self.P.op("act", lambda e: e.activation(out=out, in_=in_, func=AF.Copy), reads=reads, writes=writes)
        else:
            self.P.op(eng, lambda e: e.tensor_copy(out=out, in_=in_), reads=reads, writes=writes)

    def load_w(self, wap2d, kc, c0, n, slot=None):
        if slot is None:
            slot = self.wrot.next()
        sap, st = slot
        dst = sap[:, 0:kc * n].rearrange("p (c n) -> p c n", c=kc)
        src = wap2d.rearrange("(c p) n -> p c n", p=128)[:, :, c0:c0 + n]
        self.P.dma("pool", lambda e: e.dma_start(out=dst, in_=src), writes=[st])
        return dst, st

    def dbg(self, ap2d, tile, n=512):
        if not getattr(self, "debug", False) or self.dbg_i >= self.NDBG:
            return
        i = self.dbg_i
        self.dbg_i += 1
        stg = self.dbg_stage[i]
        stT = T("dbgst%d" % i)
        self.P.op("dve", lambda e: e.tensor_copy(out=stg[:, 0:n], in_=ap2d), reads=[tile], writes=[stT])
        self.P.dma("sp", lambda e: e.dma_start(out=self.dbg_out[i][:, 0:n], in_=stg[:, 0:n]), reads=[stT], sem_tile=self.dbg_sem)

    def arena_reset(self):
        self.aoff = 0
        self.fence = self.P.fence()

    def abf(self, n, name):
        assert self.aoff + n <= ARENA, (name, self.aoff, n)
        ap = self.arena[:, self.aoff:self.aoff + n]
        self.aoff += n
        return ap

    def af32(self, n, name):
        assert self.aoff % 2 == 0
        assert self.aoff + 2 * n <= ARENA, (name, self.aoff, n)
        ap = self.arena[:, self.aoff:self.aoff + 2 * n].bitcast(F32)
        self.aoff += 2 * n
        return ap

    def nT(self, name):
        return T(name, fence=self.fence)

    def gcol(self, key, k=0):
        c = self.glay[key] + k
        return self.gains[:, c:c + 1]

    def norm_x(self, t, gkey, hn, hnT, sqrot, r, rT):
        S = self.S
        ps, pst = self.bank[7]
        for c in range(8):
            sq, sqt = sqrot.next()
            xa = self.x3[:, c, t * 512:(t + 1) * 512]
            self.act(sq, xa, AF.Square, [self.xT[c][t]], [sqt])
            self.mm(ps, self.cm[:, 0, :], sq, c == 0, c == 7, [sqt, self.cmT], pst)
        self.act(r, ps, AF.Ln, [pst, self.cT], [rT], scale=1.0 / D, bias=self.epsb)
        self.act(r, r, AF.Exp, [rT], [rT], scale=-0.5)
        for c in range(8):
            xa = self.x3[:, c, t * 512:(t + 1) * 512]
            self.stt(hn[:, c, :], xa, self.gcol(gkey, c), r, ALU.mult, ALU.mult, [self.xT[c][t], rT, self.cT], [hnT])

    def rstd(self, r, rT, ps, pst, n):
        self.act(r, ps, AF.Ln, [pst, self.cT], [rT], scale=1.0 / n, bias=self.epsb)
        self.act(r, r, AF.Exp, [rT], [rT], scale=-0.5)

    def rope_chunk(self, p, pT, gcol, t, kg, kgT, t1, t1T, t2, t2T):
        swb, swT = self.bank[6]
        tok = slice(t * 512, (t + 1) * 512)
        self.tsmul(kg, p, gcol, [pT, self.cT], [kgT])
        self.mm(swb, self.cm[:, 5, :], kg, True, True, [kgT, self.cmT], swT)
        self.stt(t1, p, gcol, self.rope[:, 0, tok], ALU.mult, ALU.mult, [pT, self.cT], [t1T])
        self.tt(t2, swb, self.rope[:, 1, tok], ALU.mult, [swT, self.cT], [t2T])
        self.tt(t1, t1, t2, ALU.add, [t1T, t2T], [t1T])

    def ffn(self, l):
        S, P = self.S, self.P
        HT = min(S, 1024)
        NCH = HT // 512
        for half in range(S // HT):
            self.arena_reset()
            hn = self.abf(8 * HT, "hn").rearrange("p (c s) -> p c s", c=8)
            hnT = [self.nT("hn%d" % i) for i in range(NCH)]
            ffh = self.abf(NFB * HT, "ffh").rearrange("p (f s) -> p f s", f=NFB)
            ffhT = [[self.nT("ffh") for _ in range(NCH)] for _ in range(NFB)]
            sqrot = Rot([(self.abf(512, "sq"), self.nT("sq")) for _ in range(3)])
            sgrot = Rot([(self.abf(512, "sg"), self.nT("sg")) for _ in range(2)])
            r = self.af32(512, "r"); rT = self.nT("r")
            for i in range(NCH):
                t = half * NCH + i
                self.norm_x(t, ("ffn", l), hn[:, :, i * 512:(i + 1) * 512], hnT[i], sqrot, r, rT)
            grot = Rot([self.bank[0], self.bank[1]])
            urot = Rot([self.bank[2], self.bank[3]])
            drot = Rot([self.bank[4], self.bank[5]])
            for fb in range(NFB):
                wg, wgT = self.load_w(self.W["w_gate"][l], 8, fb * 128, 128)
                wu, wuT = self.load_w(self.W["w_up"][l], 8, fb * 128, 128)
                for i in range(NCH):
                    pg, pgT = grot.next()
                    pu, puT = urot.next()
                    hs = hn[:, :, i * 512:(i + 1) * 512]
                    for c in range(8):
                        self.mm(pg, wg[:, c, :], hs[:, c, :], c == 0, c == 7, [wgT, hnT[i]], pgT)
                    for c in range(8):
                        self.mm(pu, wu[:, c, :], hs[:, c, :], c == 0, c == 7, [wuT, hnT[i]], puT)
                    sg, sgT = sgrot.next()
                    self.act(sg, pg, AF.Silu, [pgT], [sgT])
                    self.tt(ffh[:, fb, i * 512:(i + 1) * 512], pu, sg, ALU.mult, [puT, sgT], [ffhT[fb][i]])
            for dc in range(8):
                slot = self.wdrot.next()
                sap, st = slot
                dst = sap[:, :].rearrange("p (f n) -> p f n", f=NFB)
                src = self.W["w_down"][l].rearrange("(f p) n -> p f n", p=128)[:, :, dc * 128:(dc + 1) * 128]
                P.dma("pool", lambda e, dst=dst, src=src: e.dma_start(out=dst, in_=src), writes=[st])
                for i in range(NCH):
                    t = half * NCH + i
                    pd, pdT = drot.next()
                    for fb in range(NFB):
                        self.mm(pd, dst[:, fb, :], ffh[:, fb, i * 512:(i + 1) * 512], fb == 0, fb == NFB - 1,
                                [st, ffhT[fb][i]], pdT)
                    xa = self.x3[:, dc, t * 512:(t + 1) * 512]
                    self.tt(xa, pd, xa, ALU.add, [pdT], [self.xT[dc][t]])

    def mla(self, l, j):
        S, P, NC, NT = self.S, self.P, self.NC, self.NT
        W = self.W
        self.arena_reset()
        cqn = self.abf(3 * S, "cqn").rearrange("p (c s) -> p c s", c=3)
        ckvn = self.abf(2 * S, "ckvn").rearrange("p (c s) -> p c s", c=2)
        U = self.abf(S, "U")
        sqkr = self.abf(S, "sqkr")
        cqnT = [self.nT("cqn") for _ in range(NC)]
        ckvnT = [self.nT("ckvn") for _ in range(NC)]
        UT = [self.nT("U") for _ in range(NC)]
        sqkrT = [self.nT("sqkr") for _ in range(NC)]
        persist = self.aoff
        hnrot = Rot([(self.abf(8 * 512, "hn").rearrange("p (c s) -> p c s", c=8), self.nT("hn")) for _ in range(2)])
        sqrot = Rot([(self.abf(512, "sq"), self.nT("sq")) for _ in range(3)])
        rrot = Rot([(self.af32(512, "r"), self.nT("r")) for _ in range(2)])
        t1 = self.af32(512, "t1"); t1T = self.nT("t1")
        t2 = self.af32(512, "t2"); t2T = self.nT("t2")
        kg = self.abf(512, "kg"); kgT = self.nT("kg")
        prot = Rot([self.bank[i] for i in range(6)])
        for t in range(NC):
            tok = slice(t * 512, (t + 1) * 512)
            hn, hnT = hnrot.next()
            r, rT = rrot.next()
            self.norm_x(t, ("attn", l), hn, hnT, sqrot, r, rT)
            pq = []
            for jj in range(3):
                w_, wT = self.load_w(W["mla_wq_a"][j], 8, jj * 128, 128)
                p, pT = prot.next()
                for c in range(8):
                    self.mm(p, w_[:, c, :], hn[:, c, :], c == 0, c == 7, [wT, hnT], pT)
                pq.append((p, pT))
            ss, ssT = self.bank[7]
            for jj in range(3):
                sq, sqt = sqrot.next()
                self.act(sq, pq[jj][0], AF.Square, [pq[jj][1]], [sqt])
                self.mm(ss, self.cm[:, 0, :], sq, jj == 0, jj == 2, [sqt, self.cmT], ssT)
            r, rT = rrot.next()
            self.rstd(r, rT, ss, ssT, 384)
            for jj in range(3):
                self.stt(cqn[:, jj, tok], pq[jj][0], self.gcol(("qa", j), jj), r, ALU.mult, ALU.mult,
                         [pq[jj][1], rT, self.cT], [cqnT[t]])
            pk = []
            for jj in range(3):
                w_, wT = self.load_w(W["mla_wkv_a"][j], 8, jj * 128, 128)
                p, pT = prot.next()
                for c in range(8):
                    self.mm(p, w_[:, c, :], hn[:, c, :], c == 0, c == 7, [wT, hnT], pT)
                pk.append((p, pT))
            for jj in range(2):
                sq, sqt = sqrot.next()
                self.act(sq, pk[jj][0], AF.Square, [pk[jj][1]], [sqt])
                self.mm(ss, self.cm[:, 0, :], sq, jj == 0, jj == 1, [sqt, self.cmT], ssT)
            r, rT = rrot.next()
            self.rstd(r, rT, ss, ssT, 256)
            for jj in range(2):
                self.stt(ckvn[:, jj, tok], pk[jj][0], self.gcol(("kva", j), jj), r, ALU.mult, ALU.mult,
                         [pk[jj][1], rT, self.cT], [ckvnT[t]])
            self.act(sqkr[:, tok], pk[2][0], AF.Square, [pk[2][1]], [sqkrT[t]])
            self.rope_chunk(pk[2][0], pk[2][1], self.gcol(("kn_rope", j)), t, kg, kgT, t1, t1T, t2, t2T)
            self.copy("dve", U[:, tok], t1, [t1T], [UT[t]])
        self.aoff = persist
        self.fence = P.fence()
        kT = self.abf(3 * S, "kT").rearrange("p (c s) -> p c s", c=3)
        kTT = [self.nT("kT") for _ in range(NC)]
        V = self.abf(NT * 256, "V").rearrange("p (t n) -> p t n", n=256)
        VT = [self.nT("V") for _ in range(NC)]
        qrot = Rot([(self.abf(4 * 512, "qT").rearrange("p (c s) -> p c s", c=4), self.nT("qT")) for _ in range(2)])
        for (qb, qbT) in qrot.items:
            P.op("pool", lambda e, a=qb[64:128, 2, :]: e.memset(a, 0.0), writes=[qbT])
            P.op("pool", lambda e, a=qb[0:64, 3, :]: e.memset(a, 0.0), writes=[qbT])
        orot = Rot([(self.abf(2 * 512, "oT").rearrange("p (c s) -> p c s", c=2), self.nT("oT")) for _ in range(2)])
        ptrot = Rot([(self.abf(512, "pt"), self.nT("pt")) for _ in range(4)])
        sqrot = Rot([(self.abf(512, "sq"), self.nT("sq")) for _ in range(3)])
        rrot = Rot([(self.af32(512, "r"), self.nT("r")) for _ in range(4)])
        t1 = self.af32(512, "t1"); t1T = self.nT("t1")
        t2 = self.af32(512, "t2"); t2T = self.nT("t2")
        kg = self.abf(512, "kg"); kgT = self.nT("kg")
        arot = Rot([self.bank[0], self.bank[1], self.bank[2]])
        accrot = Rot([(self.bank[3], self.bank[4]), (self.bank[5], self.bank[6])])
        scale = 192.0 ** -0.5
        for hp in range(4):
            wk = [self.load_w(W["mla_wkb"][j], 2, (2 * hp + h) * 128, 128) for h in range(2)]
            wv, wvT = self.load_w(W["mla_wvb"][j], 2, 2 * hp * 128, 256)
            for t in range(NC):
                tok = slice(t * 512, (t + 1) * 512)
                rk = []
                for h in range(2):
                    p, pT = arot.next()
                    for c in range(2):
                        self.mm(p, wk[h][0][:, c, :], ckvn[:, c, tok], c == 0, c == 1, [wk[h][1], ckvnT[t]], pT)
                    sq, sqt = sqrot.next()
                    self.act(sq, p, AF.Square, [pT], [sqt])
                    ss, ssT = self.bank[7]
                    self.mm(ss, self.cm[:, 0, :], sq, True, False, [sqt, self.cmT], ssT)
                    self.mm(ss, self.cm[:, 2, :], sqkr[:, tok], False, True, [sqkrT[t], self.cmT], ssT)
                    r, rT = rrot.next()
                    self.rstd(r, rT, ss, ssT, 192)
                    self.stt(kT[:, h, tok], p, self.gcol(("kn_nope", j)), r, ALU.mult, ALU.mult,
                             [pT, rT, self.cT], [kTT[t]])
                    rk.append((r, rT))
                for h in range(2):
                    rows = slice(64 * h, 64 * h + 64)
                    self.tt(kT[rows, 2, tok], U[rows, tok], rk[h][0][rows, :], ALU.mult, [UT[t], rk[h][1]], [kTT[t]])
                for i2 in range(2):
                    p, pT = arot.next()
                    for ii in range(2):
                        tile_ = t * 4 + i2 * 2 + ii
                        for c in range(2):
                            self.mm(p[:, ii * 256:(ii + 1) * 256], ckvn[:, c, tile_ * 128:(tile_ + 1) * 128], wv[:, c, :],
                                    c == 0, c == 1, [wvT, ckvnT[t]], pT)
                    t0_ = t * 4 + i2 * 2
                    self.copy("act", V[:, t0_:t0_ + 2, :], p.rearrange("p (t n) -> p t n", n=256), [pT], [VT[t]])
            for qc in range(NC):
                tok = slice(qc * 512, (qc + 1) * 512)
                qT, qTT = qrot.next()
                pq = []
                for jj in range(3):
                    w_, wT = self.load_w(W["mla_wq_b"][j], 3, hp * 384 + jj * 128, 128)
                    p, pT = arot.next()
                    for c in range(3):
                        self.mm(p, w_[:, c, :], cqn[:, c, tok], c == 0, c == 2, [wT, cqnT[qc]], pT)
                    pq.append((p, pT))
                sqs = []
                for jj in range(3):
                    sq, sqt = sqrot.next()
                    self.act(sq, pq[jj][0], AF.Square, [pq[jj][1]], [sqt])
                    sqs.append((sq, sqt))
                rq = []
                for h in range(2):
                    ss, ssT = self.bank[7]
                    self.mm(ss, self.cm[:, 0, :], sqs[h][0], True, False, [sqs[h][1], self.cmT], ssT)
                    self.mm(ss, self.cm[:, 2 + h, :], sqs[2][0], False, True, [sqs[2][1], self.cmT], ssT)
                    r, rT = rrot.next()
                    self.rstd(r, rT, ss, ssT, 192)
                    rq.append((r, rT))
                    self.stt(qT[:, h, :], pq[h][0], self.gcol(("qn_nope", j)), r, ALU.mult, ALU.mult,
                             [pq[h][1], rT, self.cT], [qTT])
                self.rope_chunk(pq[2][0], pq[2][1], self.gcol(("qn_rope", j)), qc, kg, kgT, t1, t1T, t2, t2T)
                for h in range(2):
                    rows = slice(64 * h, 64 * h + 64)
                    self.tt(qT[rows, 2 + h, :], t1[rows, :], rq[h][0][rows, :], ALU.mult, [t1T, rq[h][1]], [qTT])
                oT, oTT = orot.next()
                for h in range(2):
                    rows = slice(64 * h, 64 * h + 64)
                    (po, poT), (pdn, pdnT) = accrot.next()
                    pend = None
                    for kt in range(NT + 1):
                        if kt < NT:
                            ktok = slice(kt * 128, (kt + 1) * 128)
                            ps, psT = arot.next()
                            self.mm(ps, kT[:, h, ktok], qT[:, h, :], True, False, [kTT[kt // 4], qTT], psT)
                            self.mm(ps, kT[:, 2, ktok], qT[:, 2 + h, :], False, True, [kTT[kt // 4], qTT], psT)
                            pt, ptT = ptrot.next()
                            self.act(pt, ps, AF.Exp, [psT], [ptT], scale=scale)
                            nxt = (kt, pt, ptT)
                        else:
                            nxt = None
                        if pend is not None:
                            k0, pt0, ptT0 = pend
                            self.mm(po, V[:, k0, h * 128:(h + 1) * 128], pt0, k0 == 0, k0 == NT - 1, [VT[k0 // 4], ptT0], poT)
                            self.mm(pdn, self.cm[:, 0, :], pt0, k0 == 0, k0 == NT - 1, [self.cmT, ptT0], pdnT)
                        pend = nxt
                    r, rT = rrot.next()
                    self.act(r, pdn, AF.Ln, [pdnT], [rT])
                    self.act(r, r, AF.Exp, [rT], [rT], scale=-1.0)
                    self.tt(oT[:, h, :], po, r, ALU.mult, [poT, rT], [oTT])
                for hf in range(2):
                    slot = self.wrot.next()
                    sap, st = slot
                    dst = sap[:, 0:1024].rearrange("p (c n) -> p c n", c=2)
                    src = W["mla_wo"][j][2 * hp * 128:(2 * hp + 2) * 128, :].rearrange("(c p) n -> p c n", p=128)[:, :, hf * 512:(hf + 1) * 512]
                    P.dma("pool", lambda e, dst=dst, src=src: e.dma_start(out=dst, in_=src), writes=[st])
                    for d4 in range(4):
                        dc = hf * 4 + d4
                        pw, pwT = arot.next()
                        for h in range(2):
                            self.mm(pw, dst[:, h, d4 * 128:(d4 + 1) * 128], oT[:, h, :], h == 0, h == 1, [st, oTT], pwT)
                        xa = self.x3[:, dc, tok]
                        self.tt(xa, pw, xa, ALU.add, [pwT], [self.xT[dc][qc]])

    def qk_chunk(self, w2d, col0, hn, hnT, gcol, t, dst, dstT, st):
        sqrot, rrot, t1, t1T, t2, t2T, kg, kgT, prot = st
        w_, wT = self.load_w(w2d, 8, col0, 128)
        p, pT = prot.next()
        for c in range(8):
            self.mm(p, w_[:, c, :], hn[:, c, :], c == 0, c == 7, [wT, hnT], pT)
        sq, sqt = sqrot.next()
        self.act(sq, p, AF.Square, [pT], [sqt])
        ss, ssT = self.bank[7]
        self.mm(ss, self.cm[:, 1, :], sq, True, True, [sqt, self.cmT], ssT)
        r, rT = rrot.next()
        self.rstd(r, rT, ss, ssT, 64)
        self.rope_chunk(p, pT, gcol, t, kg, kgT, t1, t1T, t2, t2T)
        self.tt(dst, t1, r, ALU.mult, [t1T, rT], [dstT])

    def alloc_zq(self):
        self.zq = []
        for par in range(2):
            z = self.abf(512, "zq"); zT = self.nT("zq")
            zr = slice(64, 128) if par == 0 else slice(0, 64)
            self.P.op("pool", lambda e, a=z[zr, :]: e.memset(a, 0.0), writes=[zT])
            self.zq.append((z, zT))

    def band_attn(self, mname, qT, qTT, kfn, vfn, VT, qc, nheads, hinfo, po_epilogue, ptrot, arot, scale):
        NT = self.NT
        off, OFF, Wt, width, W, dil = self.mlay[mname]
        dt_max = -(-W // 128)
        kts = list(range(max(0, 4 * qc - Wt // 128), min(NT - 1, 4 * qc + 3 + Wt // 128) + 1))
        for hh in range(nheads):
            qch, rows, kch = hinfo(hh)
            zq, zqT = self.zq[hh % 2]
            self.copy("pool", zq[rows, :], qT[rows, qch, :], [qTT], [zqT])
            po, poT = self.accr.next()
            contrib = {jq: [kt for kt in kts if abs(kt - (4 * qc + jq)) <= dt_max] for jq in range(4)}
            pv_list = [(kt, jq) for kt in kts for jq in range(4) if kt in contrib[jq]]
            pv_first, pv_last = pv_list[0], pv_list[-1]
            pend = None
            for kt in kts + [None]:
                if kt is not None:
                    ktok = slice(kt * 128, (kt + 1) * 128)
                    ps, psT = arot.next()
                    u0 = off + 512 * qc - 128 * kt + OFF
                    self.mm(ps, self.cm[:, 4, :], self.mtab[:, u0:u0 + 512], True, False, [self.cmT], psT)
                    kap, kTt = kfn(kch, slice(0, 128), ktok, kt)
                    self.mm(ps, kap, zq, False, True, [kTt, zqT], psT)
                    pt, ptT = ptrot.next()
                    self.act(pt, ps, AF.Exp, [psT], [ptT], scale=scale)
                    nxt = (kt, pt, ptT)
                else:
                    nxt = None
                if pend is not None:
                    k0, pt0, ptT0 = pend
                    vap = vfn(hh, k0)
                    for jq in range(4):
                        cl = contrib[jq]
                        if k0 in cl:
                            self.mm(po[:, jq * 128:jq * 128 + 65], pt0[:, jq * 128:(jq + 1) * 128], vap,
                                    (k0, jq) == pv_first, (k0, jq) == pv_last, [VT[k0 // 4], ptT0], poT)
                pend = nxt
            po_epilogue(hh, po, poT)

    def swa(self, l):
        S, P, NC, NT = self.S, self.P, self.NC, self.NT
        W = self.W
        scale = 64.0 ** -0.5
        self.arena_reset()
        hnF = self.abf(8 * S, "hnF").rearrange("p (c s) -> p c s", c=8)
        hnFT = [self.nT("hnF") for _ in range(NC)]
        sq0 = Rot([(self.abf(512, "sq"), self.nT("sq")) for _ in range(3)])
        r0 = self.af32(512, "r"); r0T = self.nT("r")
        for t in range(NC):
            self.norm_x(t, ("attn", l), hnF[:, :, t * 512:(t + 1) * 512], hnFT[t], sq0, r0, r0T)
        mark = self.aoff
        for g in range(4):
            self.aoff = mark
            self.fence = P.fence()
            qT = self.abf(2 * S, "qT").rearrange("p (c s) -> p c s", c=2)
            qTT = [self.nT("qT") for _ in range(NC)]
            kT = self.abf(S, "kT")
            kTT = [self.nT("kT") for _ in range(NC)]
            Va = self.abf(NT * 80, "Va").rearrange("p (t n) -> p t n", n=80)
            VT = [self.nT("Va") for _ in range(NC)]
            otm_rot = Rot([(self.abf(4 * 256, "otm").rearrange("p (t n) -> p t n", n=256), self.nT("otm")) for _ in range(2)])
            oTrot = Rot([(self.abf(2 * 512, "oT").rearrange("p (c s) -> p c s", c=2), self.nT("oT")) for _ in range(2)])
            ptrot = Rot([(self.abf(512, "pt"), self.nT("pt")) for _ in range(4)])
            sqrot = Rot([(self.abf(512, "sq"), self.nT("sq")) for _ in range(3)])
            rrot = Rot([(self.af32(512, "r"), self.nT("r")) for _ in range(3)])
            t1 = self.af32(512, "t1"); t1T = self.nT("t1")
            t2 = self.af32(512, "t2"); t2T = self.nT("t2")
            kg = self.abf(512, "kg"); kgT = self.nT("kg")
            den = self.af32(8, "den"); denT = self.nT("den")
            self.alloc_zq()
            prot = Rot([self.bank[0], self.bank[1], self.bank[2]])
            st = (sqrot, rrot, t1, t1T, t2, t2T, kg, kgT, prot)
            for t in range(NC):
                P.op("pool", lambda e, a=Va[:, t * 4:(t + 1) * 4, 64:65]: e.memset(a, 1.0), writes=[VT[t]])
            for t in range(NC):
                tok = slice(t * 512, (t + 1) * 512)
                hn, hnT = hnF[:, :, tok], hnFT[t]
                for jj in range(2):
                    self.qk_chunk(W["swa_wq"], (2 * g + jj) * 128, hn, hnT, self.gcol("swa_q"), t, qT[:, jj, tok], qTT[t], st)
                self.qk_chunk(W["swa_wk"], g * 128, hn, hnT, self.gcol("swa_k"), t, kT[:, tok], kTT[t], st)
                wv, wvT = self.load_w(W["swa_wv"], 8, g * 64, 64)
                p, pT = prot.next()
                for ii in range(4):
                    tl = t * 4 + ii
                    for c in range(8):
                        self.mm(p[:, ii * 64:(ii + 1) * 64], hn[:, c, ii * 128:(ii + 1) * 128], wv[:, c, :], c == 0, c == 7,
                                [wvT, hnT], pT)
                self.copy("act", Va[:, t * 4:(t + 1) * 4, 0:64], p[:, 0:256].rearrange("p (t n) -> p t n", n=64), [pT], [VT[t]])
            if g == 1:
                self.dbg(qT[:, 0, 0:512], qTT[0])
                self.dbg(qT[:, 1, 0:512], qTT[0])
                self.dbg(kT[:, 0:512], kTT[0])
                self.dbg(Va[:, 0:4, :].rearrange("p t n -> p (t n)"), VT[0], n=320)
            arot = Rot([self.bank[0], self.bank[1], self.bank[2]])
            self.accr = Rot([self.bank[3], self.bank[4], self.bank[5]])
            sinkc = self.glay["sink"]
            for qc in range(NC):
                tok = slice(qc * 512, (qc + 1) * 512)
                otm, otmT = otm_rot.next()

                def epi(hh, po, poT, otm=otm, otmT=otmT, g=g):
                    po3 = po.rearrange("p (t n) -> p t n", n=128)
                    hq = 4 * g + hh
                    self.tsadd(den[:, 0:4], po3[:, :, 64], self.esink[:, hq:hq + 1], [poT, self.cT], [denT])
                    self.recip(den[:, 0:4], den[:, 0:4], [denT], [denT])
                    self.tt(otm[:, :, hh * 64:(hh + 1) * 64], po3[:, :, 0:64],
                            den[:, 0:4].unsqueeze(2).broadcast_to([128, 4, 64]), ALU.mult, [poT, denT], [otmT])

                self.band_attn("swa", qT[:, :, tok], qTT[qc],
                               lambda kch, rows, ktok, kt: (kT[rows, ktok], kTT[kt // 4]),
                               lambda hh, k0: Va[:, k0, 0:65], VT, qc, 4,
                               lambda hh: (hh // 2, slice(64 * (hh % 2), 64 * (hh % 2) + 64), 0),
                               epi, ptrot, arot, scale)
                if qc == 0:
                    self.dbg(otm[:, 0:2, :].rearrange("p t n -> p (t n)"), otmT)
                    self.dbg(otm[:, 2:4, :].rearrange("p t n -> p (t n)"), otmT)
                self.otm_to_x(otm, otmT, 2, oTrot, W["swa_wo"], g * 256, qc, arot)

    def otm_to_x(self, otm, otmT, nfc, oTrot, wo2d, row0, qc, arot):
        P = self.P
        tok = slice(qc * 512, (qc + 1) * 512)
        oT, oTT = oTrot.next()
        pb, pbT = self.bank[7]
        pbb = pb.bitcast(BF16)
        for fc in range(nfc):
            for jq in range(4):
                P.op("pe", lambda e, o=pbb[:, fc * 512 + jq * 128: fc * 512 + (jq + 1) * 128],
                     i=otm[:, jq, fc * 128:(fc + 1) * 128]: e.transpose(o, i, self.cm[:, 4, :]),
                     reads=[otmT, self.cmT], writes=[pbT])
        self.copy("act", oT, pbb[:, 0:nfc * 512].rearrange("p (c s) -> p c s", c=nfc), [pbT], [oTT])
        if getattr(self, "debug", False) and self.dbg_i in (96, 97):
            self.dbg(oT[:, 0, :], oTT)
            self.dbg(oT[:, 1, :], oTT)
        for hf in range(2):
            slot = self.wrot.next()
            sap, st = slot
            n = 1024 // nfc
            assert n == 512
            dst = sap[:, 0:1024].rearrange("p (c n) -> p c n", c=nfc)
            src = wo2d[row0:row0 + nfc * 128, :].rearrange("(c p) n -> p c n", p=128)[:, :, hf * 512:(hf + 1) * 512]
            P.dma("pool", lambda e, dst=dst, src=src: e.dma_start(out=dst, in_=src), writes=[st])
            for d4 in range(4):
                dc = hf * 4 + d4
                pw, pwT = arot.next()
                for fc in range(nfc):
                    self.mm(pw, dst[:, fc, d4 * 128:(d4 + 1) * 128], oT[:, fc, :], fc == 0, fc == nfc - 1, [st, oTT], pwT)
                xa = self.x3[:, dc, tok]
                self.tt(xa, pw, xa, ALU.add, [pwT], [self.xT[dc][qc]])

    def dil(self, l):
        S, P, NC, NT = self.S, self.P, self.NC, self.NT
        W = self.W
        scale = 64.0 ** -0.5
        names = ["d1", "d4", "d16"]
        self.arena_reset()
        otmF = self.abf(NT * 512, "otmF").rearrange("p (t n) -> p t n", n=512)
        otmFT = [self.nT("otmF") for _ in range(NC)]
        mark = self.aoff
        for hf in range(2):
            self.aoff = mark
            self.fence = P.fence()
            hn = self.abf(8 * 512, "hn").rearrange("p (c s) -> p c s", c=8); hnT = self.nT("hn")
            qT = self.abf(2 * S, "qT").rearrange("p (c s) -> p c s", c=2)
            kT = self.abf(2 * S, "kT").rearrange("p (c s) -> p c s", c=2)
            Va = self.abf(NT * 320, "Va").rearrange("p (t h n) -> p t h n", h=4, n=80)
            nacc = self.af32(NT * 320, "nacc").rearrange("p (t h n) -> p t h n", h=4, n=80)
            naccT = [[self.nT("nacc") for _ in range(4)] for _ in range(NC)]
            ptrot = Rot([(self.abf(512, "pt"), self.nT("pt")) for _ in range(4)])
            sqrot = Rot([(self.abf(512, "sq"), self.nT("sq")) for _ in range(3)])
            rrot = Rot([(self.af32(512, "r"), self.nT("r")) for _ in range(2)])
            t1 = self.af32(512, "t1"); t1T = self.nT("t1")
            t2 = self.af32(512, "t2"); t2T = self.nT("t2")
            kg = self.abf(512, "kg"); kgT = self.nT("kg")
            den = self.af32(8, "den"); denT = self.nT("den")
            self.alloc_zq()
            prot = Rot([self.bank[0], self.bank[1], self.bank[2]])
            st = (sqrot, rrot, t1, t1T, t2, t2T, kg, kgT, prot)
            for gi in range(3):
                qTT = [self.nT("qT") for _ in range(NC)]
                kTT = [self.nT("kT") for _ in range(NC)]
                VT = [self.nT("Va") for _ in range(NC)]
                if gi > 0:
                    f = P.fence()
                    for lst in (qTT, kTT, VT):
                        for tt_ in lst:
                            tt_.readers = list(f)
                for t in range(NC):
                    P.op("pool", lambda e, a=Va[:, t * 4:(t + 1) * 4, :, 64:65]: e.memset(a, 1.0), writes=[VT[t]])
                for t in range(NC):
                    tok = slice(t * 512, (t + 1) * 512)
                    r, rT = rrot.next()
                    self.norm_x(t, ("attn", l), hn, hnT, sqrot, r, rT)
                    for jj in range(2):
                        col = gi * 512 + hf * 256 + jj * 128
                        self.qk_chunk(W["dil_wqkv"], col, hn, hnT, self.gcol("dil_q"), t, qT[:, jj, tok], qTT[t], st)
                        self.qk_chunk(W["dil_wqkv"], 1536 + col, hn, hnT, self.gcol("dil_k"), t, kT[:, jj, tok], kTT[t], st)
                    wv, wvT = self.load_w(W["dil_wqkv"], 8, 3072 + gi * 512 + hf * 256, 128)
                    wv2, wv2T = self.load_w(W["dil_wqkv"], 8, 3072 + gi * 512 + hf * 256 + 128, 128)
                    for i2 in range(2):
                        p, pT = prot.next()
                        for ii in range(2):
                            tl = i2 * 2 + ii
                            for (wv_, wvT_, cc) in ((wv, wvT, 0), (wv2, wv2T, 128)):
                                for c in range(8):
                                    self.mm(p[:, ii * 256 + cc: ii * 256 + cc + 128], hn[:, c, tl * 128:(tl + 1) * 128], wv_[:, c, :],
                                            c == 0, c == 7, [wvT_, hnT], pT)
                        t0_ = t * 4 + i2 * 2
                        self.copy("act", Va[:, t0_:t0_ + 2, :, 0:64], p.rearrange("p (t h n) -> p t h n", h=4, n=64), [pT], [VT[t]])
                arot = Rot([self.bank[0], self.bank[1], self.bank[2]])
                self.accr = Rot([self.bank[3], self.bank[4], self.bank[5]])
                for qc in range(NC):
                    tok = slice(qc * 512, (qc + 1) * 512)

                    def epi(hh, po, poT, qc=qc, gi=gi):
                        po3 = po.rearrange("p (t n) -> p t n", n=128)[:, :, 0:65]
                        dst = nacc[:, qc * 4:(qc + 1) * 4, hh, 0:65]
                        if gi == 0:
                            self.copy("dve", dst, po3, [poT], [naccT[qc][hh]])
                        else:
                            self.tt(dst, po3, dst, ALU.add, [poT], [naccT[qc][hh]])

                    self.band_attn(names[gi], qT[:, :, tok], qTT[qc],
                                   lambda kch, rows, ktok, kt: (kT[rows, kch, ktok], kTT[kt // 4]),
                                   lambda hh, k0: Va[:, k0, hh, 0:65], VT, qc, 4,
                                   lambda hh: (hh // 2, slice(64 * (hh % 2), 64 * (hh % 2) + 64), hh // 2),
                                   epi, ptrot, arot, scale)
            for qc in range(NC):
                for hh in range(4):
                    nq = nacc[:, qc * 4:(qc + 1) * 4, hh, :]
                    c0 = hf * 256 + hh * 64
                    self.recip(den[:, 0:4], nq[:, :, 64], [naccT[qc][hh]], [denT])
                    self.tt(otmF[:, qc * 4:(qc + 1) * 4, c0:c0 + 64], nq[:, :, 0:64],
                            den[:, 0:4].unsqueeze(2).broadcast_to([128, 4, 64]), ALU.mult, [naccT[qc][hh], denT], [otmFT[qc]])
        self.aoff = mark
        self.fence = P.fence()
        oTrot = Rot([(self.abf(2 * 512, "oT").rearrange("p (c s) -> p c s", c=2), self.nT("oT")) for _ in range(2)])
        arot = Rot([self.bank[0], self.bank[1], self.bank[2]])
        for qc in range(NC):
            for hf in range(2):
                self.otm_to_x(otmF[:, qc * 4:(qc + 1) * 4, hf * 256:(hf + 1) * 256], otmFT[qc], 2, oTrot, W["dil_wo"], hf * 256, qc, arot)

    def build(self):
        S, NSEQ = self.S, self.NSEQ
        nc = bass.Bass("TRN2", target_bir_lowering=False)
        self.nc = nc
        xin = nc.dram_tensor("xT", [NSEQ, 8, 128, S], F32, kind="ExternalInput").ap()
        yout = nc.dram_tensor("yT", [NSEQ, 8, 128, S], F32, kind="ExternalOutput").ap()
        rope_h = nc.dram_tensor("rope", [128, 2, S], F32, kind="ExternalInput").ap()
        cm_h = nc.dram_tensor("cmat", [128, 6, 128], F32, kind="ExternalInput").ap()
        mt_h = nc.dram_tensor("mtab", [128, self.MW], F32, kind="ExternalInput").ap()
        g_h = nc.dram_tensor("gains", [128, self.NG], F32, kind="ExternalInput").ap()
        self.W = {k: nc.dram_tensor(k, shp, F32, kind="ExternalInput").ap() for k, shp in WSHAPES.items()}
        self.NDBG = 12
        self.dbg_i = 0
        if getattr(self, "debug", False):
            self.dbg_out = nc.dram_tensor("dbg", [self.NDBG, 128, 512], F32, kind="ExternalOutput").ap()
            self.dbg_sem = T("dbgsem")
        with ExitStack() as es:
            sb = lambda name, shape, dt: es.enter_context(nc.sbuf_tensor(name, shape, dt))
            xs = sb("xs", [128, 8 * S], F32)
            self.x3 = xs[:, :].rearrange("p (c s) -> p c s", c=8)
            self.rope = sb("rope_sb", [128, 2 * S], BF16)[:, :].rearrange("p (c s) -> p c s", c=2)
            self.cm = sb("cm_sb", [128, 6 * 128], BF16)[:, :].rearrange("p (c s) -> p c s", c=6)
            self.mtab = sb("mtab_sb", [128, self.MW], BF16)[:, :]
            self.gains = sb("gains_sb", [128, self.NG], F32)[:, :]
            self.esink = sb("esink", [128, 16], F32)[:, :]
            self.epsb = sb("epsb", [128, 1], F32)[:, :]
            self.arena = sb("arena", [128, ARENA], BF16)[:, :]
            wslots = [sb("ws%d" % i, [128, 1024], BF16)[:, :] for i in range(8)]
            wdslots = [sb("wd%d" % i, [128, NFB * 128], BF16)[:, :] for i in range(2)]
            banks = [es.enter_context(nc.psum_tensor("pb%d" % i, [128, 512], F32))[:, :] for i in range(8)]
            if getattr(self, "debug", False):
                self.dbg_stage = [sb("dbgst%d" % i, [128, 512], F32)[:, :] for i in range(self.NDBG)]
            P = Prog(nc)
            self.P = P
            self.bank = [(banks[i], T("bank%d" % i, excl=True)) for i in range(8)]
            self.wrot = Rot([(wslots[i], T("ws%d" % i)) for i in range(8)])
            self.wdrot = Rot([(wdslots[i], T("wd%d" % i)) for i in range(2)])
            self.cT = T("consts")
            self.cmT = T("cm")
            self.xT = [[T("x%d_%d" % (c, t)) for t in range(self.NC)] for c in range(8)]
            xsem = [T("xsem%d" % c) for c in range(8)]
            self.fence = []
            P.dma("pool", lambda e: e.dma_start(out=self.rope, in_=rope_h), writes=[self.cT])
            P.dma("sp", lambda e: e.dma_start(out=self.gains, in_=g_h), writes=[self.cT])
            P.dma("pool", lambda e: e.dma_start(out=self.cm, in_=cm_h), writes=[self.cmT])
            P.dma("pool", lambda e: e.dma_start(out=self.mtab, in_=mt_h), writes=[self.cmT])
            P.op("dve", lambda e: e.memset(self.epsb, EPS), writes=[self.cT])
            sc = self.glay["sink"]
            self.act(self.esink, self.gains[:, sc:sc + 16], AF.Exp, [self.cT], [self.cT])
            for s in range(NSEQ):
                for c in range(8):
                    P.dma("sp", lambda e, c=c, s=s: e.dma_start(out=self.x3[:, c, :], in_=xin[s, c]),
                          writes=self.xT[c], sem_tile=xsem[c])
                for l in self.layers:
                    kind = l % 3
                    if kind == 0:
                        self.mla(l, l // 3)
                    elif kind == 1:
                        self.swa(l)
                    else:
                        self.dil(l)
                    if self.do_ffn:
                        self.ffn(l)
                for c in range(8):
                    P.dma("sp", lambda e, c=c, s=s: e.dma_start(out=yout[s, c], in_=self.x3[:, c, :]),
                          reads=self.xT[c], sem_tile=xsem[c])
            fin = list(xsem)
            if getattr(self, "debug", False) and self.dbg_i > 0:
                fin.append(self.dbg_sem)
            P.emit(final_tiles=fin)
        return nc


_CACHE = {}


def run_device(xT_cores, inp, S, NSEQ, layers=(0, 1, 2, 3), do_ffn=True):
    key = (S, NSEQ, tuple(layers), do_ffn)
    if key not in _CACHE:
        _CACHE[key] = Builder(S, NSEQ, layers, do_ffn).build()
    nc = _CACHE[key]
    rope, cm, mt = build_consts(S)
    gains = build_gains(inp)
    w = prep_weights(inp)
    in_maps = []
    for xc in xT_cores:
        m = {"xT": xc, "rope": rope, "cmat": cm, "mtab": mt, "gains": gains}
        m.update(w)
        in_maps.append(m)
    res = run_bass_kernel_spmd(nc, in_maps, core_ids=list(range(len(xT_cores))))
    return [r["yT"] for r in res.results]


def kernel(**inputs):
    xp = np.asarray(inputs["x_prompt"], np.float32)
    xs_ = np.asarray(inputs["x_sample"], np.float32)
    B, S, _ = xp.shape
    Bd = xs_.shape[0]
    allx = np.concatenate([xp, xs_], axis=0)
    n = allx.shape[0]
    NSEQ = n // 8
    order = np.arange(n).reshape(8, NSEQ)
    cores = []
    for c in range(8):
        xc = allx[order[c]]
        xT = np.ascontiguousarray(xc.transpose(0, 2, 1)).reshape(NSEQ, 8, 128, S)
        cores.append(xT)
    outs = run_device(cores, inputs, S, NSEQ)
    y = np.empty_like(allx)
    for c in range(8):
        yT = np.asarray(outs[c], np.float32).reshape(NSEQ, D, S)
        y[order[c]] = yT.transpose(0, 2, 1)
    return (np.ascontiguousarray(y[:B]), np.ascontiguousarray(y[B:]))
```

```python
import numpy as np
import concourse.bass as bass
import concourse.mybir as mybir
from concourse.bass_utils import run_bass_kernel_spmd
from contextlib import ExitStack

F32 = mybir.dt.float32
BF16 = mybir.dt.bfloat16
ALU = mybir.AluOpType
AF = mybir.ActivationFunctionType

D = 1024
DFF = 2816
NFB = 22
EPS = 1e-6
NEG = -30000.0
ENGS = ("pe", "act", "dve", "pool", "sp")
ARENA = 46080
PREFETCH_Q = False


class T:
    __slots__ = ("name", "last_w", "readers", "dsem", "dcount", "excl")

    def __init__(self, name, excl=False, fence=()):
        self.name = name
        self.excl = excl
        self.last_w = None
        self.readers = list(fence)
        self.dsem = None
        self.dcount = 0


class Prog:
    def __init__(self, nc):
        self.nc = nc
        self.ops = {e: [] for e in ENGS}
        self.seen = {e: {e2: -1 for e2 in ENGS} for e in ENGS}
        self.seen_d = {e: {} for e in ENGS}
        self.dma_tiles = []

    def fence(self):
        f = []
        for e in ("pe", "act", "dve", "pool"):
            for i in range(len(self.ops[e]) - 1, -1, -1):
                if self.ops[e][i]["dma"] is None:
                    f.append(("e", e, i))
                    break
        return f

    def _collect(self, eng, reads, writes):
        deps = []
        for t in reads:
            if t.last_w is not None:
                deps.append((t.last_w, "raw"))
        for t in writes:
            if t.last_w is not None:
                deps.append((t.last_w, "waw"))
            for r in t.readers:
                deps.append((r, "war"))
        waits = []
        for d, kind in deps:
            if d[0] == "e":
                _, e2, idx = d
                if e2 == eng:
                    if eng == "pe" or eng == "sp":
                        continue
                    if kind != "raw":
                        continue
                if self.seen[eng][e2] >= idx:
                    continue
                self.seen[eng][e2] = idx
                self.ops[e2][idx]["signal"] = True
                waits.append(("e", e2, idx))
            else:
                _, t, cnt = d
                if self.seen_d[eng].get(t, 0) >= cnt:
                    continue
                self.seen_d[eng][t] = cnt
                waits.append(("d", t, cnt))
        return waits

    def op(self, eng, fn, reads=(), writes=()):
        ex = [t for t in reads if t.excl]
        if ex:
            reads = [t for t in reads if not t.excl]
            writes = list(writes) + [t for t in ex if t not in writes]
        waits = self._collect(eng, reads, writes)
        idx = len(self.ops[eng])
        self.ops[eng].append(dict(fn=fn, waits=waits, signal=False, dma=None))
        me = ("e", eng, idx)
        for t in writes:
            t.last_w = me
            t.readers = []
        for t in reads:
            if t not in writes:
                t.readers.append(me)
        return idx

    def dma(self, eng, fn, reads=(), writes=(), sem_tile=None):
        waits = self._collect(eng, reads, writes)
        st = sem_tile or (writes[0] if writes else reads[0])
        if st.dsem is None:
            st.dsem = True
            self.dma_tiles.append(st)
        st.dcount += 16
        me = ("d", st, st.dcount)
        self.ops[eng].append(dict(fn=fn, waits=waits, signal=False, dma=st))
        for t in writes:
            t.last_w = me
            t.readers = []
        for t in reads:
            if t not in writes:
                t.readers.append(me)

    def emit(self, final_tiles=()):
        nc = self.nc
        with ExitStack() as es:
            esem = {e: es.enter_context(nc.semaphore("s_" + e)) for e in ENGS}
            for i, t in enumerate(self.dma_tiles):
                t.dsem = es.enter_context(nc.semaphore("d%d" % i))
            signum = {}
            for e in ENGS:
                c = 0
                for i, o in enumerate(self.ops[e]):
                    if o["signal"]:
                        c += 1
                        signum[(e, i)] = c
            block = es.enter_context(nc.Block())

            def run(e, engobj):
                for i, o in enumerate(self.ops[e]):
                    for w in o["waits"]:
                        if w[0] == "e":
                            engobj.wait_ge(esem[w[1]], signum[(w[1], w[2])])
                        else:
                            engobj.wait_ge(w[1].dsem, w[2])
                    ins = o["fn"](engobj)
                    if o["dma"] is not None:
                        ins.then_inc(o["dma"].dsem, 16)
                    if o["signal"]:
                        ins.then_inc(esem[e], 1)
                if e == "sp":
                    for t in final_tiles:
                        engobj.wait_ge(t.dsem, t.dcount)

            @block.tensor
            def _(eng):
                run("pe", eng)

            @block.scalar
            def _(eng):
                run("act", eng)

            @block.vector
            def _(eng):
                run("dve", eng)

            @block.gpsimd
            def _(eng):
                run("pool", eng)

            @block.sync
            def _(eng):
                run("sp", eng)


class Rot:
    def __init__(self, items):
        self.items = list(items)
        self.i = 0

    def next(self):
        v = self.items[self.i % len(self.items)]
        self.i += 1
        return v


def mask_specs():
    return [("swa", 128, 1), ("d1", 64, 1), ("d4", 256, 4), ("d16", 1024, 16)]


def mask_layout():
    off = 0
    lay = {}
    for name, W, dil in mask_specs():
        Wt = -(-W // 128) * 128
        OFF = 384 + Wt
        width = OFF + Wt + 512
        lay[name] = (off, OFF, Wt, width, W, dil)
        off += width
    return lay, off


def build_consts(S):
    inv = 1.0 / (10000.0 ** (np.arange(0, 64, 2, dtype=np.float32) / 64.0))
    ang = np.arange(S, dtype=np.float32)[:, None] * inv[None, :].astype(np.float32)
    cos = np.cos(ang).astype(np.float32).T
    sin = np.sin(ang).astype(np.float32).T
    p = np.arange(128)
    rope = np.zeros((128, 2, S), np.float32)
    rope[:, 0, :] = cos[p % 32]
    sgn = np.where((p % 64) < 32, -1.0, 1.0).astype(np.float32)
    rope[:, 1, :] = sin[p % 32] * sgn[:, None]
    cm = np.zeros((128, 6, 128), np.float32)
    cm[:, 0, :] = 1.0
    cm[:, 1, :] = (p[:, None] // 64 == p[None, :] // 64)
    cm[:, 2, :] = (p[:, None] < 64)
    cm[:, 3, :] = (p[:, None] >= 64)
    cm[:, 4, :] = (p[:, None] == p[None, :])
    partner = np.where((p % 64) < 32, p + 32, p - 32)
    cm[:, 5, :] = (p[:, None] == partner[None, :])
    lay, tot = mask_layout()
    mt = np.full((128, tot), NEG, np.float32)
    for name, (off, OFF, Wt, width, W, dil) in lay.items():
        c = np.arange(width)
        delta = p[:, None] - c[None, :] + OFF
        valid = (np.abs(delta) <= W) & (delta % dil == 0)
        mt[:, off:off + width] = np.where(valid, 0.0, NEG)
    return rope, cm, mt


def gain_layout():
    lay = {}
    c = 0
    for l in range(4):
        lay[("attn", l)] = c; c += 8
        lay[("ffn", l)] = c; c += 8
    for j in range(2):
        lay[("qa", j)] = c; c += 3
        lay[("kva", j)] = c; c += 2
        lay[("qn_nope", j)] = c; c += 1
        lay[("qn_rope", j)] = c; c += 1
        lay[("kn_nope", j)] = c; c += 1
        lay[("kn_rope", j)] = c; c += 1
    lay["swa_q"] = c; c += 1
    lay["swa_k"] = c; c += 1
    lay["dil_q"] = c; c += 1
    lay["dil_k"] = c; c += 1
    lay["sink"] = c; c += 16
    return lay, c


def build_gains(inp):
    lay, n = gain_layout()
    g = np.zeros((128, n), np.float32)

    def put(col, vec):
        v = np.asarray(vec, np.float32)
        k = v.shape[0] // 128
        g[:, col:col + k] = v.reshape(k, 128).T

    for l in range(4):
        put(lay[("attn", l)], inp["attn_norm"][l])
        put(lay[("ffn", l)], inp["ffn_norm"][l])
    for j in range(2):
        put(lay[("qa", j)], inp["mla_q_a_norm"][j])
        put(lay[("kva", j)], inp["mla_kv_a_norm"][j])
        qn = np.asarray(inp["mla_q_norm"][j]); kn = np.asarray(inp["mla_k_norm"][j])
        put(lay[("qn_nope", j)], qn[:128])
        put(lay[("qn_rope", j)], np.concatenate([qn[128:], qn[128:]]))
        put(lay[("kn_nope", j)], kn[:128])
        put(lay[("kn_rope", j)], np.concatenate([kn[128:], kn[128:]]))
    put(lay["swa_q"], np.tile(np.asarray(inp["swa_q_norm"][0]), 2))
    put(lay["swa_k"], np.tile(np.asarray(inp["swa_k_norm"][0]), 2))
    put(lay["dil_q"], np.tile(np.asarray(inp["dil_q_norm"][0]), 2))
    put(lay["dil_k"], np.tile(np.asarray(inp["dil_k_norm"][0]), 2))
    g[:, lay["sink"]:lay["sink"] + 16] = np.broadcast_to(np.asarray(inp["swa_sink"][0], np.float32)[None, :], (128, 16))
    return g


def prep_weights(inp):
    f = lambda a: np.ascontiguousarray(np.asarray(a, np.float32))
    w = {}
    w["w_gate"] = f(inp["w_gate"]); w["w_up"] = f(inp["w_up"]); w["w_down"] = f(inp["w_down"])
    w["mla_wq_a"] = f(inp["mla_wq_a"])
    kva = np.asarray(inp["mla_wkv_a"], np.float32)
    w["mla_wkv_a"] = f(np.concatenate([kva, kva[:, :, 256:320]], axis=2))
    qb = np.asarray(inp["mla_wq_b"], np.float32).reshape(2, 384, 8, 192)
    cols = []
    for hp in range(4):
        cols += [qb[:, :, 2 * hp, :128], qb[:, :, 2 * hp + 1, :128], qb[:, :, 2 * hp, 128:], qb[:, :, 2 * hp + 1, 128:]]
    w["mla_wq_b"] = f(np.concatenate(cols, axis=2))
    kvb = np.asarray(inp["mla_wkv_b"], np.float32).reshape(2, 256, 8, 256)
    w["mla_wkb"] = f(kvb[:, :, :, :128].reshape(2, 256, 1024))
    w["mla_wvb"] = f(kvb[:, :, :, 128:].reshape(2, 256, 1024))
    w["mla_wo"] = f(inp["mla_wo"])
    sw = np.asarray(inp["swa_wqkv"], np.float32)[0]
    w["swa_wq"] = f(sw[:, :1024])
    k = sw[:, 1024:1280].reshape(1024, 4, 64)
    w["swa_wk"] = f(np.concatenate([k, k], axis=2).reshape(1024, 512))
    w["swa_wv"] = f(sw[:, 1280:1536])
    w["swa_wo"] = f(np.asarray(inp["swa_wo"], np.float32)[0])
    w["dil_wqkv"] = f(np.asarray(inp["dil_wqkv"], np.float32)[0])
    w["dil_wo"] = f(np.asarray(inp["dil_wo"], np.float32)[0])
    return w


WSHAPES = {
    "w_gate": [4, D, DFF], "w_up": [4, D, DFF], "w_down": [4, DFF, D],
    "mla_wq_a": [2, D, 384], "mla_wkv_a": [2, D, 384], "mla_wq_b": [2, 384, 1536],
    "mla_wkb": [2, 256, 1024], "mla_wvb": [2, 256, 1024], "mla_wo": [2, D, D],
    "swa_wq": [D, 1024], "swa_wk": [D, 512], "swa_wv": [D, 256], "swa_wo": [D, D],
    "dil_wqkv": [D, 4608], "dil_wo": [512, D],
}


class Builder:
    def __init__(self, S, NSEQ, layers=(0, 1, 2, 3), do_ffn=True):
        self.S, self.NSEQ, self.layers, self.do_ffn = S, NSEQ, tuple(layers), do_ffn
        self.NC = S // 512
        self.NT = S // 128
        self.glay, self.NG = gain_layout()
        self.mlay, self.MW = mask_layout()

    def mm(self, out, lhsT, rhs, start, stop, reads, w):
        self.P.op("pe", lambda e: e.matmul(out, lhsT, rhs, start=start, stop=stop), reads=reads, writes=[w])

    def act(self, out, in_, func, reads, writes, scale=1.0, bias=None):
        if bias is None:
            self.P.op("act", lambda e: e.activation(out=out, in_=in_, func=func, scale=scale), reads=reads, writes=writes)
        else:
            self.P.op("act", lambda e: e.activation(out=out, in_=in_, func=func, scale=scale, bias=bias), reads=reads, writes=writes)

    def tt(self, out, in0, in1, op, reads, writes, eng="dve"):
        self.P.op(eng, lambda e: e.tensor_tensor(out=out, in0=in0, in1=in1, op=op), reads=reads, writes=writes)

    def stt(self, out, in0, scalar, in1, op0, op1, reads, writes):
        self.P.op("dve", lambda e: e.scalar_tensor_tensor(out=out, in0=in0, scalar=scalar, in1=in1, op0=op0, op1=op1),
                  reads=reads, writes=writes)

    def tsmul(self, out, in0, scalar, reads, writes):
        self.P.op("dve", lambda e: e.tensor_scalar_mul(out=out, in0=in0, scalar1=scalar), reads=reads, writes=writes)

    def tsadd(self, out, in0, scalar, reads, writes):
        self.P.op("dve", lambda e: e.tensor_scalar_add(out=out, in0=in0, scalar1=scalar), reads=reads, writes=writes)

    def recip(self, out, in_, reads, writes):
        self.P.op("dve", lambda e: e.reciprocal(out=out, in_=in_), reads=reads, writes=writes)

    def copy(self, eng, out, in_, reads, writes):
        if eng == "act":
            self.P.op("act", lambda e: e.activation(out=out, in_=in_, func=AF.Copy), reads=reads, writes=writes)
        else:
            self.P.op(eng, lambda e: e.tensor_copy(out=out, in_=in_), reads=reads, writes=writes)

    def load_w(self, wap2d, kc, c0, n, slot=None):
        if slot is None:
            slot = self.wrot.next()
        sap, st = slot
        dst = sap[:, 0:kc * n].rearrange("p (c n) -> p c n", c=kc)
        src = wap2d.rearrange("(c p) n -> p c n", p=128)[:, :, c0:c0 + n]
        self.P.dma("pool", lambda e: e.dma_start(out=dst, in_=src), writes=[st])
        return dst, st

    def dbg(self, ap2d, tile, n=512):
        if not getattr(self, "debug", False) or self.dbg_i >= self.NDBG:
            return
        i = self.dbg_i
        self.dbg_i += 1
        stg = self.dbg_stage[i]
        stT = T("dbgst%d" % i)
        self.P.op("dve", lambda e: e.tensor_copy(out=stg[:, 0:n], in_=ap2d), reads=[tile], writes=[stT])
        self.P.dma("sp", lambda e: e.dma_start(out=self.dbg_out[i][:, 0:n], in_=stg[:, 0:n]), reads=[stT], sem_tile=self.dbg_sem)

    def arena_reset(self):
        self.aoff = 0
        self.fence = self.P.fence()

    def abf(self, n, name):
        assert self.aoff + n <= ARENA, (name, self.aoff, n)
        ap = self.arena[:, self.aoff:self.aoff + n]
        self.aoff += n
        return ap

    def af32(self, n, name):
        assert self.aoff % 2 == 0
        assert self.aoff + 2 * n <= ARENA, (name, self.aoff, n)
        ap = self.arena[:, self.aoff:self.aoff + 2 * n].bitcast(F32)
        self.aoff += 2 * n
        return ap

    def nT(self, name):
        return T(name, fence=self.fence)

    def gcol(self, key, k=0):
        c = self.glay[key] + k
        return self.gains[:, c:c + 1]

    def norm_x(self, t, gkey, hn, hnT, sqrot, r, rT):
        S = self.S
        ps, pst = self.bank[7]
        for c in range(8):
            sq, sqt = sqrot.next()
            xa = self.x3[:, c, t * 512:(t + 1) * 512]
            self.act(sq, xa, AF.Square, [self.xT[c][t]], [sqt])
            self.mm(ps, self.cm[:, 0, :], sq, c == 0, c == 7, [sqt, self.cmT], pst)
        self.act(r, ps, AF.Ln, [pst, self.cT], [rT], scale=1.0 / D, bias=self.epsb)
        self.act(r, r, AF.Exp, [rT], [rT], scale=-0.5)
        for c in range(8):
            xa = self.x3[:, c, t * 512:(t + 1) * 512]
            self.stt(hn[:, c, :], xa, self.gcol(gkey, c), r, ALU.mult, ALU.mult, [self.xT[c][t], rT, self.cT], [hnT])

    def rstd(self, r, rT, ps, pst, n):
        self.act(r, ps, AF.Ln, [pst, self.cT], [rT], scale=1.0 / n, bias=self.epsb)
        self.act(r, r, AF.Exp, [rT], [rT], scale=-0.5)

    def rope_chunk(self, p, pT, gcol, t, kg, kgT, t1, t1T, t2, t2T):
        swb, swT = self.bank[getattr(self, "swap_bank", 6)]
        tok = slice(t * 512, (t + 1) * 512)
        self.tsmul(kg, p, gcol, [pT, self.cT], [kgT])
        self.mm(swb, self.cm[:, 5, :], kg, True, True, [kgT, self.cmT], swT)
        self.stt(t1, p, gcol, self.rope[:, 0, tok], ALU.mult, ALU.mult, [pT, self.cT], [t1T])
        self.tt(t2, swb, self.rope[:, 1, tok], ALU.mult, [swT, self.cT], [t2T])
        self.tt(t1, t1, t2, ALU.add, [t1T, t2T], [t1T])

    def ffn(self, l):
        S, P = self.S, self.P
        HT = min(S, 1024)
        NCH = HT // 512
        for half in range(S // HT):
            self.arena_reset()
            hn = self.abf(8 * HT, "hn").rearrange("p (c s) -> p c s", c=8)
            hnT = [self.nT("hn%d" % i) for i in range(NCH)]
            ffh = self.abf(NFB * HT, "ffh").rearrange("p (f s) -> p f s", f=NFB)
            ffhT = [[self.nT("ffh") for _ in range(NCH)] for _ in range(NFB)]
            sqrot = Rot([(self.abf(512, "sq"), self.nT("sq")) for _ in range(3)])
            sgrot = Rot([(self.abf(512, "sg"), self.nT("sg")) for _ in range(2)])
            r = self.af32(512, "r"); rT = self.nT("r")
            for i in range(NCH):
                t = half * NCH + i
                self.norm_x(t, ("ffn", l), hn[:, :, i * 512:(i + 1) * 512], hnT[i], sqrot, r, rT)
            grot = Rot([self.bank[0], self.bank[1]])
            urot = Rot([self.bank[2], self.bank[3]])
            drot = Rot([self.bank[4], self.bank[5]])
            for fb in range(NFB):
                wg, wgT = self.load_w(self.W["w_gate"][l], 8, fb * 128, 128)
                wu, wuT = self.load_w(self.W["w_up"][l], 8, fb * 128, 128)
                for i in range(NCH):
                    pg, pgT = grot.next()
                    pu, puT = urot.next()
                    hs = hn[:, :, i * 512:(i + 1) * 512]
                    for c in range(8):
                        self.mm(pg, wg[:, c, :], hs[:, c, :], c == 0, c == 7, [wgT, hnT[i]], pgT)
                    for c in range(8):
                        self.mm(pu, wu[:, c, :], hs[:, c, :], c == 0, c == 7, [wuT, hnT[i]], puT)
                    sg, sgT = sgrot.next()
                    self.act(sg, pg, AF.Silu, [pgT], [sgT])
                    self.tt(ffh[:, fb, i * 512:(i + 1) * 512], pu, sg, ALU.mult, [puT, sgT], [ffhT[fb][i]])
            for dc in range(8):
                slot = self.wdrot.next()
                sap, st = slot
                dst = sap[:, :].rearrange("p (f n) -> p f n", f=NFB)
                src = self.W["w_down"][l].rearrange("(f p) n -> p f n", p=128)[:, :, dc * 128:(dc + 1) * 128]
                P.dma("pool", lambda e, dst=dst, src=src: e.dma_start(out=dst, in_=src), writes=[st])
                for i in range(NCH):
                    t = half * NCH + i
                    pd, pdT = drot.next()
                    for fb in range(NFB):
                        self.mm(pd, dst[:, fb, :], ffh[:, fb, i * 512:(i + 1) * 512], fb == 0, fb == NFB - 1,
                                [st, ffhT[fb][i]], pdT)
                    xa = self.x3[:, dc, t * 512:(t + 1) * 512]
                    self.tt(xa, pd, xa, ALU.add, [pdT], [self.xT[dc][t]])

    def mla(self, l, j):
        S, P, NC, NT = self.S, self.P, self.NC, self.NT
        W = self.W
        self.swap_bank = 6
        self.arena_reset()
        cqn = self.abf(3 * S, "cqn").rearrange("p (c s) -> p c s", c=3)
        ckvn = self.abf(2 * S, "ckvn").rearrange("p (c s) -> p c s", c=2)
        U = self.abf(S, "U")
        sqkr = self.abf(S, "sqkr")
        cqnT = [self.nT("cqn") for _ in range(NC)]
        ckvnT = [self.nT("ckvn") for _ in range(NC)]
        UT = [self.nT("U") for _ in range(NC)]
        sqkrT = [self.nT("sqkr") for _ in range(NC)]
        persist = self.aoff
        hnrot = Rot([(self.abf(8 * 512, "hn").rearrange("p (c s) -> p c s", c=8), self.nT("hn")) for _ in range(2)])
        sqrot = Rot([(self.abf(512, "sq"), self.nT("sq")) for _ in range(3)])
        rrot = Rot([(self.af32(512, "r"), self.nT("r")) for _ in range(3)])
        t1 = self.af32(512, "t1"); t1T = self.nT("t1")
        t2 = self.af32(512, "t2"); t2T = self.nT("t2")
        kg = self.abf(512, "kg"); kgT = self.nT("kg")
        prot = Rot([self.bank[i] for i in range(6)])
        hns = {}

        def donorm(t):
            if t < NC and t not in hns:
                hn_, hnT_ = hnrot.next()
                r_, rT_ = rrot.next()
                self.norm_x(t, ("attn", l), hn_, hnT_, sqrot, r_, rT_)
                hns[t] = (hn_, hnT_)
        donorm(0)
        for t in range(NC):
            tok = slice(t * 512, (t + 1) * 512)
            hn, hnT = hns[t]
            pq = []
            for jj in range(3):
                w_, wT = self.load_w(W["mla_wq_a"][j], 8, jj * 128, 128)
                p, pT = prot.next()
                for c in range(8):
                    self.mm(p, w_[:, c, :], hn[:, c, :], c == 0, c == 7, [wT, hnT], pT)
                pq.append((p, pT))
            pk = []
            for jj in range(3):
                w_, wT = self.load_w(W["mla_wkv_a"][j], 8, jj * 128, 128)
                p, pT = prot.next()
                for c in range(8):
                    self.mm(p, w_[:, c, :], hn[:, c, :], c == 0, c == 7, [wT, hnT], pT)
                pk.append((p, pT))
            donorm(t + 1)
            ss, ssT = self.bank[7]
            for jj in range(3):
                sq, sqt = sqrot.next()
                self.act(sq, pq[jj][0], AF.Square, [pq[jj][1]], [sqt])
                self.mm(ss, self.cm[:, 0, :], sq, jj == 0, jj == 2, [sqt, self.cmT], ssT)
            r, rT = rrot.next()
            self.rstd(r, rT, ss, ssT, 384)
            for jj in range(3):
                self.stt(cqn[:, jj, tok], pq[jj][0], self.gcol(("qa", j), jj), r, ALU.mult, ALU.mult,
                         [pq[jj][1], rT, self.cT], [cqnT[t]])
            for jj in range(2):
                sq, sqt = sqrot.next()
                self.act(sq, pk[jj][0], AF.Square, [pk[jj][1]], [sqt])
                self.mm(ss, self.cm[:, 0, :], sq, jj == 0, jj == 1, [sqt, self.cmT], ssT)
            r, rT = rrot.next()
            self.rstd(r, rT, ss, ssT, 256)
            for jj in range(2):
                self.stt(ckvn[:, jj, tok], pk[jj][0], self.gcol(("kva", j), jj), r, ALU.mult, ALU.mult,
                         [pk[jj][1], rT, self.cT], [ckvnT[t]])
            self.act(sqkr[:, tok], pk[2][0], AF.Square, [pk[2][1]], [sqkrT[t]])
            self.rope_chunk(pk[2][0], pk[2][1], self.gcol(("kn_rope", j)), t, kg, kgT, t1, t1T, t2, t2T)
            self.copy("dve", U[:, tok], t1, [t1T], [UT[t]])
        self.swap_bank = 7
        self.aoff = persist
        self.fence = P.fence()
        kT = self.abf(3 * S, "kT").rearrange("p (c s) -> p c s", c=3)
        kTT = [self.nT("kT") for _ in range(NC)]
        V = self.abf(NT * 256, "V").rearrange("p (t n) -> p t n", n=256)
        VT = [self.nT("V") for _ in range(NC)]
        qrot = Rot([(self.abf(4 * 512, "qT").rearrange("p (c s) -> p c s", c=4), self.nT("qT")) for _ in range(2)])
        for (qb, qbT) in qrot.items:
            P.op("pool", lambda e, a=qb[64:128, 2, :]: e.memset(a, 0.0), writes=[qbT])
            P.op("pool", lambda e, a=qb[0:64, 3, :]: e.memset(a, 0.0), writes=[qbT])
        orot = Rot([(self.abf(2 * 512, "oT").rearrange("p (c s) -> p c s", c=2), self.nT("oT")) for _ in range(2)])
        ptrot = Rot([(self.abf(512, "pt"), self.nT("pt")) for _ in range(4)])
        sqrot = Rot([(self.abf(512, "sq"), self.nT("sq")) for _ in range(3)])
        rrot = Rot([(self.af32(512, "r"), self.nT("r")) for _ in range(4)])
        t1 = self.af32(512, "t1"); t1T = self.nT("t1")
        t2 = self.af32(512, "t2"); t2T = self.nT("t2")
        kg = self.abf(512, "kg"); kgT = self.nT("kg")
        arot = Rot([self.bank[0], self.bank[1], self.bank[2]])
        accrot = Rot([(self.bank[3], self.bank[4]), (self.bank[5], self.bank[6])])
        scale = 192.0 ** -0.5
        for hp in range(4):
            wk = [self.load_w(W["mla_wkb"][j], 2, (2 * hp + h) * 128, 128) for h in range(2)]
            wv, wvT = self.load_w(W["mla_wvb"][j], 2, 2 * hp * 128, 256)
            items = []
            for t in range(NC):
                tok = slice(t * 512, (t + 1) * 512)
                rk = []
                for h in range(2):
                    box = {}

                    def kproj(t=t, tok=tok, h=h, box=box):
                        p, pT = arot.next()
                        for c in range(2):
                            self.mm(p, wk[h][0][:, c, :], ckvn[:, c, tok], c == 0, c == 1, [wk[h][1], ckvnT[t]], pT)
                        box["p"] = (p, pT)

                    def kfin(t=t, tok=tok, h=h, box=box, rk=rk):
                        p, pT = box["p"]
                        sq, sqt = sqrot.next()
                        self.act(sq, p, AF.Square, [pT], [sqt])
                        ss, ssT = self.bank[7]
                        self.mm(ss, self.cm[:, 0, :], sq, True, False, [sqt, self.cmT], ssT)
                        self.mm(ss, self.cm[:, 2, :], sqkr[:, tok], False, True, [sqkrT[t], self.cmT], ssT)
                        r, rT = rrot.next()
                        self.rstd(r, rT, ss, ssT, 192)
                        self.stt(kT[:, h, tok], p, self.gcol(("kn_nope", j)), r, ALU.mult, ALU.mult,
                                 [pT, rT, self.cT], [kTT[t]])
                        rk.append((r, rT))
                        if h == 1:
                            for hh in range(2):
                                rows = slice(64 * hh, 64 * hh + 64)
                                self.tt(kT[rows, 2, tok], U[rows, tok], rk[hh][0][rows, :], ALU.mult, [UT[t], rk[hh][1]], [kTT[t]])
                    items.append((kproj, kfin))
                for i2 in range(2):
                    box = {}

                    def vproj(t=t, i2=i2, box=box):
                        p, pT = arot.next()
                        for ii in range(2):
                            tile_ = t * 4 + i2 * 2 + ii
                            for c in range(2):
                                self.mm(p[:, ii * 256:(ii + 1) * 256], ckvn[:, c, tile_ * 128:(tile_ + 1) * 128], wv[:, c, :],
                                        c == 0, c == 1, [wvT, ckvnT[t]], pT)
                        box["p"] = (p, pT)

                    def vfin(t=t, i2=i2, box=box):
                        p, pT = box["p"]
                        t0_ = t * 4 + i2 * 2
                        self.copy("act", V[:, t0_:t0_ + 2, :], p.rearrange("p (t n) -> p t n", n=256), [pT], [VT[t]])
                    items.append((vproj, vfin))
            self.run_pipe(items)
            qbuf = {}

            def emitQ(qc):
                if qc >= NC or qc in qbuf:
                    return
                tok = slice(qc * 512, (qc + 1) * 512)
                qT, qTT = qrot.next()
                pq = []
                for jj in range(3):
                    w_, wT = self.load_w(W["mla_wq_b"][j], 3, hp * 384 + jj * 128, 128)
                    p, pT = arot.next()
                    for c in range(3):
                        self.mm(p, w_[:, c, :], cqn[:, c, tok], c == 0, c == 2, [wT, cqnT[qc]], pT)
                    pq.append((p, pT))
                sqs = []
                for jj in range(3):
                    sq, sqt = sqrot.next()
                    self.act(sq, pq[jj][0], AF.Square, [pq[jj][1]], [sqt])
                    sqs.append((sq, sqt))
                rq = []
                for h in range(2):
                    ss, ssT = self.bank[7]
                    self.mm(ss, self.cm[:, 0, :], sqs[h][0], True, False, [sqs[h][1], self.cmT], ssT)
                    self.mm(ss, self.cm[:, 2 + h, :], sqs[2][0], False, True, [sqs[2][1], self.cmT], ssT)
                    r, rT = rrot.next()
                    self.rstd(r, rT, ss, ssT, 192)
                    rq.append((r, rT))
                    self.stt(qT[:, h, :], pq[h][0], self.gcol(("qn_nope", j)), r, ALU.mult, ALU.mult,
                             [pq[h][1], rT, self.cT], [qTT])
                self.rope_chunk(pq[2][0], pq[2][1], self.gcol(("qn_rope", j)), qc, kg, kgT, t1, t1T, t2, t2T)
                for h in range(2):
                    rows = slice(64 * h, 64 * h + 64)
                    self.tt(qT[rows, 2 + h, :], t1[rows, :], rq[h][0][rows, :], ALU.mult, [t1T, rq[h][1]], [qTT])
                qbuf[qc] = (qT, qTT)

            emitQ(0)
            for qc in range(NC):
                tok = slice(qc * 512, (qc + 1) * 512)
                emitQ(qc)
                qT, qTT = qbuf[qc]
                oT, oTT = orot.next()
                for h in range(2):
                    rows = slice(64 * h, 64 * h + 64)
                    (po, poT), (pdn, pdnT) = accrot.next()
                    pend = None
                    for kt in range(NT + 1):
                        if kt < NT:
                            ktok = slice(kt * 128, (kt + 1) * 128)
                            ps, psT = arot.next()
                            self.mm(ps, kT[:, h, ktok], qT[:, h, :], True, False, [kTT[kt // 4], qTT], psT)
                            self.mm(ps, kT[:, 2, ktok], qT[:, 2 + h, :], False, True, [kTT[kt // 4], qTT], psT)
                            pt, ptT = ptrot.next()
                            self.act(pt, ps, AF.Exp, [psT], [ptT], scale=scale)
                            nxt = (kt, pt, ptT)
                        else:
                            nxt = None
                        if pend is not None:
                            k0, pt0, ptT0 = pend
                            self.mm(po, V[:, k0, h * 128:(h + 1) * 128], pt0, k0 == 0, k0 == NT - 1, [VT[k0 // 4], ptT0], poT)
                            self.mm(pdn, self.cm[:, 0, :], pt0, k0 == 0, k0 == NT - 1, [self.cmT, ptT0], pdnT)
                        pend = nxt
                    r, rT = rrot.next()
                    self.act(r, pdn, AF.Ln, [pdnT], [rT])
                    self.act(r, r, AF.Exp, [rT], [rT], scale=-1.0)
                    self.tt(oT[:, h, :], po, r, ALU.mult, [poT, rT], [oTT])
                    if h == 0 and PREFETCH_Q:
                        emitQ(qc + 1)
                for hf in range(2):
                    slot = self.wrot.next()
                    sap, st = slot
                    dst = sap[:, 0:1024].rearrange("p (c n) -> p c n", c=2)
                    src = W["mla_wo"][j][2 * hp * 128:(2 * hp + 2) * 128, :].rearrange("(c p) n -> p c n", p=128)[:, :, hf * 512:(hf + 1) * 512]
                    P.dma("pool", lambda e, dst=dst, src=src: e.dma_start(out=dst, in_=src), writes=[st])
                    for d4 in range(4):
                        dc = hf * 4 + d4
                        pw, pwT = arot.next()
                        for h in range(2):
                            self.mm(pw, dst[:, h, d4 * 128:(d4 + 1) * 128], oT[:, h, :], h == 0, h == 1, [st, oTT], pwT)
                        xa = self.x3[:, dc, tok]
                        self.tt(xa, pw, xa, ALU.add, [pwT], [self.xT[dc][qc]])

    def qk_proj(self, w2d, col0, hn, hnT, prot):
        w_, wT = self.load_w(w2d, 8, col0, 128)
        p, pT = prot.next()
        for c in range(8):
            self.mm(p, w_[:, c, :], hn[:, c, :], c == 0, c == 7, [wT, hnT], pT)
        return p, pT

    def qk_fin(self, p, pT, gcol, t, dst, dstT, st):
        sqrot, rrot, t1, t1T, t2, t2T, kg, kgT, prot = st
        sq, sqt = sqrot.next()
        self.act(sq, p, AF.Square, [pT], [sqt])
        ss, ssT = self.bank[7]
        self.mm(ss, self.cm[:, 1, :], sq, True, True, [sqt, self.cmT], ssT)
        r, rT = rrot.next()
        self.rstd(r, rT, ss, ssT, 64)
        self.rope_chunk(p, pT, gcol, t, kg, kgT, t1, t1T, t2, t2T)
        self.tt(dst, t1, r, ALU.mult, [t1T, rT], [dstT])

    def qk_item(self, w2d, col0, hn, hnT, gcol, t, dst, dstT, st, pre=None):
        box = {}

        def proj():
            if pre is not None:
                pre()
            box["p"] = self.qk_proj(w2d, col0, hn, hnT, st[-1])

        def fin():
            self.qk_fin(box["p"][0], box["p"][1], gcol, t, dst, dstT, st)
        return (proj, fin)

    @staticmethod
    def run_pipe(items):
        n = len(items)
        if n:
            items[0][0]()
        for i in range(n):
            if i + 1 < n:
                items[i + 1][0]()
            items[i][1]()

    def alloc_zq(self):
        self.zq = []
        for par in range(2):
            z = self.abf(512, "zq"); zT = self.nT("zq")
            zr = slice(64, 128) if par == 0 else slice(0, 64)
            self.P.op("pool", lambda e, a=z[zr, :]: e.memset(a, 0.0), writes=[zT])
            self.zq.append((z, zT))

    def band_attn(self, mname, qT, qTT, kfn, vfn, VT, qc, nheads, hinfo, po_epilogue, ptrot, arot, scale):
        NT = self.NT
        off, OFF, Wt, width, W, dil = self.mlay[mname]
        dt_max = -(-W // 128)
        kts = list(range(max(0, 4 * qc - Wt // 128), min(NT - 1, 4 * qc + 3 + Wt // 128) + 1))
        for hh in range(nheads):
            qch, rows, kch = hinfo(hh)
            zq, zqT = self.zq[hh % 2]
            self.copy("pool", zq[rows, :], qT[rows, qch, :], [qTT], [zqT])
            po, poT = self.accr.next()
            contrib = {jq: [kt for kt in kts if abs(kt - (4 * qc + jq)) <= dt_max] for jq in range(4)}
            pv_list = [(kt, jq) for kt in kts for jq in range(4) if kt in contrib[jq]]
            pv_first, pv_last = pv_list[0], pv_list[-1]
            pend = None
            for kt in kts + [None]:
                if kt is not None:
                    ktok = slice(kt * 128, (kt + 1) * 128)
                    ps, psT = arot.next()
                    u0 = off + 512 * qc - 128 * kt + OFF
                    jqs = [jq for jq in range(4) if kt in contrib[jq]]
                    c0, c1 = jqs[0] * 128, (jqs[-1] + 1) * 128
                    self.mm(ps[:, c0:c1], self.cm[:, 4, :], self.mtab[:, u0 + c0:u0 + c1], True, False, [self.cmT], psT)
                    kap, kTt = kfn(kch, slice(0, 128), ktok, kt)
                    self.mm(ps[:, c0:c1], kap, zq[:, c0:c1], False, True, [kTt, zqT], psT)
                    pt, ptT = ptrot.next()
                    self.act(pt[:, c0:c1], ps[:, c0:c1], AF.Exp, [psT], [ptT], scale=scale)
                    nxt = (kt, pt, ptT)
                else:
                    nxt = None
                if pend is not None:
                    k0, pt0, ptT0 = pend
                    vap = vfn(hh, k0)
                    for jq in range(4):
                        cl = contrib[jq]
                        if k0 in cl:
                            self.mm(po[:, jq * 128:jq * 128 + 65], pt0[:, jq * 128:(jq + 1) * 128], vap,
                                    (k0, jq) == pv_first, (k0, jq) == pv_last, [VT[k0 // 4], ptT0], poT)
                pend = nxt
            po_epilogue(hh, po, poT)

    def swa(self, l):
        self.swap_bank = 6
        S, P, NC, NT = self.S, self.P, self.NC, self.NT
        W = self.W
        scale = 64.0 ** -0.5
        self.arena_reset()
        hnF = self.abf(8 * S, "hnF").rearrange("p (c s) -> p c s", c=8)
        hnFT = [self.nT("hnF") for _ in range(NC)]
        sq0 = Rot([(self.abf(512, "sq"), self.nT("sq")) for _ in range(3)])
        r0 = self.af32(512, "r"); r0T = self.nT("r")
        for t in range(NC):
            self.norm_x(t, ("attn", l), hnF[:, :, t * 512:(t + 1) * 512], hnFT[t], sq0, r0, r0T)
        mark = self.aoff
        for g in range(4):
            self.aoff = mark
            self.fence = P.fence()
            qT = self.abf(2 * S, "qT").rearrange("p (c s) -> p c s", c=2)
            qTT = [self.nT("qT") for _ in range(NC)]
            kT = self.abf(S, "kT")
            kTT = [self.nT("kT") for _ in range(NC)]
            Va = self.abf(NT * 80, "Va").rearrange("p (t n) -> p t n", n=80)
            VT = [self.nT("Va") for _ in range(NC)]
            otm_rot = Rot([(self.abf(4 * 256, "otm").rearrange("p (t n) -> p t n", n=256), self.nT("otm")) for _ in range(2)])
            oTrot = Rot([(self.abf(2 * 512, "oT").rearrange("p (c s) -> p c s", c=2), self.nT("oT")) for _ in range(2)])
            ptrot = Rot([(self.abf(512, "pt"), self.nT("pt")) for _ in range(4)])
            sqrot = Rot([(self.abf(512, "sq"), self.nT("sq")) for _ in range(3)])
            rrot = Rot([(self.af32(512, "r"), self.nT("r")) for _ in range(3)])
            t1 = self.af32(512, "t1"); t1T = self.nT("t1")
            t2 = self.af32(512, "t2"); t2T = self.nT("t2")
            kg = self.abf(512, "kg"); kgT = self.nT("kg")
            den = self.af32(8, "den"); denT = self.nT("den")
            self.alloc_zq()
            prot = Rot([self.bank[0], self.bank[1], self.bank[2]])
            st = (sqrot, rrot, t1, t1T, t2, t2T, kg, kgT, prot)
            for t in range(NC):
                P.op("pool", lambda e, a=Va[:, t * 4:(t + 1) * 4, 64:65]: e.memset(a, 1.0), writes=[VT[t]])
            items = []
            for t in range(NC):
                tok = slice(t * 512, (t + 1) * 512)
                hn, hnT = hnF[:, :, tok], hnFT[t]
                for jj in range(2):
                    items.append(self.qk_item(W["swa_wq"], (2 * g + jj) * 128, hn, hnT, self.gcol("swa_q"), t, qT[:, jj, tok], qTT[t], st))
                items.append(self.qk_item(W["swa_wk"], g * 128, hn, hnT, self.gcol("swa_k"), t, kT[:, tok], kTT[t], st))
                box = {}

                def vproj(t=t, hn=hn, hnT=hnT, box=box):
                    wv, wvT = self.load_w(W["swa_wv"], 8, g * 64, 64)
                    p, pT = prot.next()
                    for ii in range(4):
                        for c in range(8):
                            self.mm(p[:, ii * 64:(ii + 1) * 64], hn[:, c, ii * 128:(ii + 1) * 128], wv[:, c, :], c == 0, c == 7,
                                    [wvT, hnT], pT)
                    box["p"] = (p, pT)

                def vfin(t=t, box=box):
                    p, pT = box["p"]
                    self.copy("act", Va[:, t * 4:(t + 1) * 4, 0:64], p[:, 0:256].rearrange("p (t n) -> p t n", n=64), [pT], [VT[t]])
                items.append((vproj, vfin))
            self.run_pipe(items)
            if g == 1:
                self.dbg(qT[:, 0, 0:512], qTT[0])
                self.dbg(qT[:, 1, 0:512], qTT[0])
                self.dbg(kT[:, 0:512], kTT[0])
                self.dbg(Va[:, 0:4, :].rearrange("p t n -> p (t n)"), VT[0], n=320)
            arot = Rot([self.bank[0], self.bank[1], self.bank[2]])
            self.accr = Rot([self.bank[3], self.bank[4], self.bank[5]])
            sinkc = self.glay["sink"]
            for qc in range(NC):
                tok = slice(qc * 512, (qc + 1) * 512)
                otm, otmT = otm_rot.next()

                def epi(hh, po, poT, otm=otm, otmT=otmT, g=g):
                    po3 = po.rearrange("p (t n) -> p t n", n=128)
                    hq = 4 * g + hh
                    self.tsadd(den[:, 0:4], po3[:, :, 64], self.esink[:, hq:hq + 1], [poT, self.cT], [denT])
                    self.recip(den[:, 0:4], den[:, 0:4], [denT], [denT])
                    self.tt(otm[:, :, hh * 64:(hh + 1) * 64], po3[:, :, 0:64],
                            den[:, 0:4].unsqueeze(2).broadcast_to([128, 4, 64]), ALU.mult, [poT, denT], [otmT])

                self.band_attn("swa", qT[:, :, tok], qTT[qc],
                               lambda kch, rows, ktok, kt: (kT[rows, ktok], kTT[kt // 4]),
                               lambda hh, k0: Va[:, k0, 0:65], VT, qc, 4,
                               lambda hh: (hh // 2, slice(64 * (hh % 2), 64 * (hh % 2) + 64), 0),
                               epi, ptrot, arot, scale)
                if qc == 0:
                    self.dbg(otm[:, 0:2, :].rearrange("p t n -> p (t n)"), otmT)
                    self.dbg(otm[:, 2:4, :].rearrange("p t n -> p (t n)"), otmT)
                self.otm_to_x(otm, otmT, 2, oTrot, W["swa_wo"], g * 256, qc, arot)

    def otm_to_x(self, otm, otmT, nfc, oTrot, wo2d, row0, qc, arot):
        P = self.P
        tok = slice(qc * 512, (qc + 1) * 512)
        oT, oTT = oTrot.next()
        pb, pbT = self.bank[7]
        pbb = pb.bitcast(BF16)
        for fc in range(nfc):
            for jq in range(4):
                P.op("pe", lambda e, o=pbb[:, fc * 512 + jq * 128: fc * 512 + (jq + 1) * 128],
                     i=otm[:, jq, fc * 128:(fc + 1) * 128]: e.transpose(o, i, self.cm[:, 4, :]),
                     reads=[otmT, self.cmT], writes=[pbT])
        self.copy("act", oT, pbb[:, 0:nfc * 512].rearrange("p (c s) -> p c s", c=nfc), [pbT], [oTT])
        if getattr(self, "debug", False) and self.dbg_i in (96, 97):
            self.dbg(oT[:, 0, :], oTT)
            self.dbg(oT[:, 1, :], oTT)
        for hf in range(2):
            slot = self.wrot.next()
            sap, st = slot
            n = 1024 // nfc
            assert n == 512
            dst = sap[:, 0:1024].rearrange("p (c n) -> p c n", c=nfc)
            src = wo2d[row0:row0 + nfc * 128, :].rearrange("(c p) n -> p c n", p=128)[:, :, hf * 512:(hf + 1) * 512]
            P.dma("pool", lambda e, dst=dst, src=src: e.dma_start(out=dst, in_=src), writes=[st])
            for d4 in range(4):
                dc = hf * 4 + d4
                pw, pwT = arot.next()
                for fc in range(nfc):
                    self.mm(pw, dst[:, fc, d4 * 128:(d4 + 1) * 128], oT[:, fc, :], fc == 0, fc == nfc - 1, [st, oTT], pwT)
                xa = self.x3[:, dc, tok]
                self.tt(xa, pw, xa, ALU.add, [pwT], [self.xT[dc][qc]])

    def dil(self, l):
        self.swap_bank = 6
        S, P, NC, NT = self.S, self.P, self.NC, self.NT
        W = self.W
        scale = 64.0 ** -0.5
        names = ["d1", "d4", "d16"]
        self.arena_reset()
        otmF = self.abf(NT * 512, "otmF").rearrange("p (t n) -> p t n", n=512)
        otmFT = [self.nT("otmF") for _ in range(NC)]
        mark = self.aoff
        for hf in range(2):
            self.aoff = mark
            self.fence = P.fence()
            hnrot = Rot([(self.abf(8 * 512, "hn").rearrange("p (c s) -> p c s", c=8), self.nT("hn")) for _ in range(2)])
            qT = self.abf(2 * S, "qT").rearrange("p (c s) -> p c s", c=2)
            kT = self.abf(2 * S, "kT").rearrange("p (c s) -> p c s", c=2)
            Va = self.abf(NT * 264, "Va").rearrange("p (t h n) -> p t h n", h=4, n=66)
            nacc = self.af32(NT * 264, "nacc").rearrange("p (t h n) -> p t h n", h=4, n=66)
            naccT = [[self.nT("nacc") for _ in range(4)] for _ in range(NC)]
            ptrot = Rot([(self.abf(512, "pt"), self.nT("pt")) for _ in range(3)])
            sqrot = Rot([(self.abf(512, "sq"), self.nT("sq")) for _ in range(3)])
            rrot = Rot([(self.af32(512, "r"), self.nT("r")) for _ in range(2)])
            t1 = self.af32(512, "t1"); t1T = self.nT("t1")
            t2 = self.af32(512, "t2"); t2T = self.nT("t2")
            kg = self.abf(512, "kg"); kgT = self.nT("kg")
            den = self.af32(8, "den"); denT = self.nT("den")
            self.alloc_zq()
            prot = Rot([self.bank[0], self.bank[1], self.bank[2]])
            st = (sqrot, rrot, t1, t1T, t2, t2T, kg, kgT, prot)
            for gi in range(3):
                qTT = [self.nT("qT") for _ in range(NC)]
                kTT = [self.nT("kT") for _ in range(NC)]
                VT = [self.nT("Va") for _ in range(NC)]
                if gi > 0:
                    f = P.fence()
                    for lst in (qTT, kTT, VT):
                        for tt_ in lst:
                            tt_.readers = list(f)
                for t in range(NC):
                    P.op("pool", lambda e, a=Va[:, t * 4:(t + 1) * 4, :, 64:65]: e.memset(a, 1.0), writes=[VT[t]])
                hns = {}

                def donorm(t):
                    if t < NC and t not in hns:
                        hn_, hnT_ = hnrot.next()
                        r, rT = rrot.next()
                        self.norm_x(t, ("attn", l), hn_, hnT_, sqrot, r, rT)
                        hns[t] = (hn_, hnT_)
                donorm(0)
                items = []
                for t in range(NC):
                    tok = slice(t * 512, (t + 1) * 512)
                    first = True
                    for jj in range(2):
                        col = gi * 512 + hf * 256 + jj * 128
                        for (cc, gk, dst_, dT_) in ((col, "dil_q", qT[:, jj, tok], qTT[t]), (1536 + col, "dil_k", kT[:, jj, tok], kTT[t])):
                            box = {}

                            def proj(t=t, cc=cc, box=box, first=first):
                                if first:
                                    donorm(t + 1)
                                box["p"] = self.qk_proj(W["dil_wqkv"], cc, hns[t][0], hns[t][1], prot)

                            def fin(t=t, gk=gk, dst_=dst_, dT_=dT_, box=box):
                                self.qk_fin(box["p"][0], box["p"][1], self.gcol(gk), t, dst_, dT_, st)
                            items.append((proj, fin))
                            first = False
                    for i2 in range(2):
                        box = {}

                        def vproj(t=t, i2=i2, box=box):
                            hn_, hnT_ = hns[t]
                            wv, wvT = self.load_w(W["dil_wqkv"], 8, 3072 + gi * 512 + hf * 256, 128)
                            wv2, wv2T = self.load_w(W["dil_wqkv"], 8, 3072 + gi * 512 + hf * 256 + 128, 128)
                            p, pT = prot.next()
                            for ii in range(2):
                                tl = i2 * 2 + ii
                                for (wv_, wvT_, cc) in ((wv, wvT, 0), (wv2, wv2T, 128)):
                                    for c in range(8):
                                        self.mm(p[:, ii * 256 + cc: ii * 256 + cc + 128], hn_[:, c, tl * 128:(tl + 1) * 128], wv_[:, c, :],
                                                c == 0, c == 7, [wvT_, hnT_], pT)
                            box["p"] = (p, pT)

                        def vfin(t=t, i2=i2, box=box):
                            p, pT = box["p"]
                            t0_ = t * 4 + i2 * 2
                            self.copy("act", Va[:, t0_:t0_ + 2, :, 0:64], p.rearrange("p (t h n) -> p t h n", h=4, n=64), [pT], [VT[t]])
                        items.append((vproj, vfin))
                self.run_pipe(items)
                arot = Rot([self.bank[0], self.bank[1], self.bank[2]])
                self.accr = Rot([self.bank[3], self.bank[4], self.bank[5]])
                for qc in range(NC):
                    tok = slice(qc * 512, (qc + 1) * 512)

                    def epi(hh, po, poT, qc=qc, gi=gi):
                        po3 = po.rearrange("p (t n) -> p t n", n=128)[:, :, 0:65]
                        dst = nacc[:, qc * 4:(qc + 1) * 4, hh, 0:65]
                        if gi == 0:
                            self.copy("dve", dst, po3, [poT], [naccT[qc][hh]])
                        else:
                            self.tt(dst, po3, dst, ALU.add, [poT], [naccT[qc][hh]])

                    self.band_attn(names[gi], qT[:, :, tok], qTT[qc],
                                   lambda kch, rows, ktok, kt: (kT[rows, kch, ktok], kTT[kt // 4]),
                                   lambda hh, k0: Va[:, k0, hh, 0:65], VT, qc, 4,
                                   lambda hh: (hh // 2, slice(64 * (hh % 2), 64 * (hh % 2) + 64), hh // 2),
                                   epi, ptrot, arot, scale)
            for qc in range(NC):
                for hh in range(4):
                    nq = nacc[:, qc * 4:(qc + 1) * 4, hh, :]
                    c0 = hf * 256 + hh * 64
                    self.recip(den[:, 0:4], nq[:, :, 64], [naccT[qc][hh]], [denT])
                    self.tt(otmF[:, qc * 4:(qc + 1) * 4, c0:c0 + 64], nq[:, :, 0:64],
                            den[:, 0:4].unsqueeze(2).broadcast_to([128, 4, 64]), ALU.mult, [naccT[qc][hh], denT], [otmFT[qc]])
        self.aoff = mark
        self.fence = P.fence()
        oTrot = Rot([(self.abf(2 * 512, "oT").rearrange("p (c s) -> p c s", c=2), self.nT("oT")) for _ in range(2)])
        arot = Rot([self.bank[0], self.bank[1], self.bank[2]])
        for qc in range(NC):
            for hf in range(2):
                self.otm_to_x(otmF[:, qc * 4:(qc + 1) * 4, hf * 256:(hf + 1) * 256], otmFT[qc], 2, oTrot, W["dil_wo"], hf * 256, qc, arot)

    def build(self):
        S, NSEQ = self.S, self.NSEQ
        nc = bass.Bass("TRN2", target_bir_lowering=False)
        self.nc = nc
        xin = nc.dram_tensor("xT", [NSEQ, 8, 128, S], F32, kind="ExternalInput").ap()
        yout = nc.dram_tensor("yT", [NSEQ, 8, 128, S], F32, kind="ExternalOutput").ap()
        rope_h = nc.dram_tensor("rope", [128, 2, S], F32, kind="ExternalInput").ap()
        cm_h = nc.dram_tensor("cmat", [128, 6, 128], F32, kind="ExternalInput").ap()
        mt_h = nc.dram_tensor("mtab", [128, self.MW], F32, kind="ExternalInput").ap()
        g_h = nc.dram_tensor("gains", [128, self.NG], F32, kind="ExternalInput").ap()
        self.W = {k: nc.dram_tensor(k, shp, F32, kind="ExternalInput").ap() for k, shp in WSHAPES.items()}
        self.NDBG = 12
        self.dbg_i = 0
        if getattr(self, "debug", False):
            self.dbg_out = nc.dram_tensor("dbg", [self.NDBG, 128, 512], F32, kind="ExternalOutput").ap()
            self.dbg_sem = T("dbgsem")
        with ExitStack() as es:
            sb = lambda name, shape, dt: es.enter_context(nc.sbuf_tensor(name, shape, dt))
            xs = sb("xs", [128, 8 * S], F32)
            self.x3 = xs[:, :].rearrange("p (c s) -> p c s", c=8)
            self.rope = sb("rope_sb", [128, 2 * S], BF16)[:, :].rearrange("p (c s) -> p c s", c=2)
            self.cm = sb("cm_sb", [128, 6 * 128], BF16)[:, :].rearrange("p (c s) -> p c s", c=6)
            self.mtab = sb("mtab_sb", [128, self.MW], BF16)[:, :]
            self.gains = sb("gains_sb", [128, self.NG], F32)[:, :]
            self.esink = sb("esink", [128, 16], F32)[:, :]
            self.epsb = sb("epsb", [128, 1], F32)[:, :]
            self.arena = sb("arena", [128, ARENA], BF16)[:, :]
            wslots = [sb("ws%d" % i, [128, 1024], BF16)[:, :] for i in range(8)]
            wdslots = [sb("wd%d" % i, [128, NFB * 128], BF16)[:, :] for i in range(2)]
            banks = [es.enter_context(nc.psum_tensor("pb%d" % i, [128, 512], F32))[:, :] for i in range(8)]
            if getattr(self, "debug", False):
                self.dbg_stage = [sb("dbgst%d" % i, [128, 512], F32)[:, :] for i in range(self.NDBG)]
            P = Prog(nc)
            self.P = P
            self.bank = [(banks[i], T("bank%d" % i, excl=True)) for i in range(8)]
            self.wrot = Rot([(wslots[i], T("ws%d" % i)) for i in range(8)])
            self.wdrot = Rot([(wdslots[i], T("wd%d" % i)) for i in range(2)])
            self.cT = T("consts")
            self.cmT = T("cm")
            self.xT = [[T("x%d_%d" % (c, t)) for t in range(self.NC)] for c in range(8)]
            xsem = [T("xsem%d" % c) for c in range(8)]
            self.fence = []
            P.dma("pool", lambda e: e.dma_start(out=self.rope, in_=rope_h), writes=[self.cT])
            P.dma("sp", lambda e: e.dma_start(out=self.gains, in_=g_h), writes=[self.cT])
            P.dma("pool", lambda e: e.dma_start(out=self.cm, in_=cm_h), writes=[self.cmT])
            P.dma("pool", lambda e: e.dma_start(out=self.mtab, in_=mt_h), writes=[self.cmT])
            P.op("dve", lambda e: e.memset(self.epsb, EPS), writes=[self.cT])
            sc = self.glay["sink"]
            self.act(self.esink, self.gains[:, sc:sc + 16], AF.Exp, [self.cT], [self.cT])
            def load_x(s, c):
                P.dma("sp", lambda e, c=c, s=s: e.dma_start(out=self.x3[:, c, :], in_=xin[s, c]),
                      writes=self.xT[c], sem_tile=xsem[c])

            for c in range(8):
                load_x(0, c)
            for s in range(NSEQ):
                for l in self.layers:
                    kind = l % 3
                    if kind == 0:
                        self.mla(l, l // 3)
                    elif kind == 1:
                        self.swa(l)
                    else:
                        self.dil(l)
                    if self.do_ffn:
                        self.ffn(l)
                for c in range(8):
                    P.dma("sp", lambda e, c=c, s=s: e.dma_start(out=yout[s, c], in_=self.x3[:, c, :]),
                          reads=self.xT[c], sem_tile=xsem[c])
                    if s + 1 < NSEQ:
                        load_x(s + 1, c)
            fin = list(xsem)
            if getattr(self, "debug", False) and self.dbg_i > 0:
                fin.append(self.dbg_sem)
            P.emit(final_tiles=fin)
        return nc


_CACHE = {}


def run_device(xT_cores, inp, S, NSEQ, layers=(0, 1, 2, 3), do_ffn=True):
    key = (S, NSEQ, tuple(layers), do_ffn)
    if key not in _CACHE:
        _CACHE[key] = Builder(S, NSEQ, layers, do_ffn).build()
    nc = _CACHE[key]
    rope, cm, mt = build_consts(S)
    gains = build_gains(inp)
    w = prep_weights(inp)
    in_maps = []
    for xc in xT_cores:
        m = {"xT": xc, "rope": rope, "cmat": cm, "mtab": mt, "gains": gains}
        m.update(w)
        in_maps.append(m)
    res = run_bass_kernel_spmd(nc, in_maps, core_ids=list(range(len(xT_cores))))
    return [r["yT"] for r in res.results]


def kernel(**inputs):
    xp = np.asarray(inputs["x_prompt"], np.float32)
    xs_ = np.asarray(inputs["x_sample"], np.float32)
    B, S, _ = xp.shape
    Bd = xs_.shape[0]
    allx = np.concatenate([xp, xs_], axis=0)
    n = allx.shape[0]
    NSEQ = n // 8
    order = np.arange(n).reshape(8, NSEQ)
    cores = []
    for c in range(8):
        xc = allx[order[c]]
        xT = np.ascontiguousarray(xc.transpose(0, 2, 1)).reshape(NSEQ, 8, 128, S)
        cores.append(xT)
    outs = run_device(cores, inputs, S, NSEQ)
    y = np.empty_like(allx)
    for c in range(8):
        yT = np.asarray(outs[c], np.float32).reshape(NSEQ, D, S)
        y[order[c]] = yT.transpose(0, 2, 1)
    return (np.ascontiguousarray(y[:B]), np.ascontiguousarray(y[B:]))
```

```python
import numpy as np
import concourse.bass as bass
import concourse.mybir as mybir
from concourse.bass_utils import run_bass_kernel_spmd
from contextlib import ExitStack

F32 = mybir.dt.float32
BF16 = mybir.dt.bfloat16
ALU = mybir.AluOpType
AF = mybir.ActivationFunctionType

D = 1024
DFF = 2816
NFB = 22
EPS = 1e-6
NEG = -30000.0
ENGS = ("pe", "act", "dve", "pool", "sp")
ARENA = 46080
PREFETCH_Q = False


class T:
    __slots__ = ("name", "last_w", "readers", "dsem", "dcount", "excl")

    def __init__(self, name, excl=False, fence=()):
        self.name = name
        self.excl = excl
        self.last_w = None
        self.readers = list(fence)
        self.dsem = None
        self.dcount = 0


class Prog:
    def __init__(self, nc):
        self.nc = nc
        self.ops = {e: [] for e in ENGS}
        self.seen = {e: {e2: -1 for e2 in ENGS} for e in ENGS}
        self.seen_d = {e: {} for e in ENGS}
        self.dma_tiles = []

    def fence(self):
        f = []
        for e in ("pe", "act", "dve", "pool"):
            for i in range(len(self.ops[e]) - 1, -1, -1):
                if self.ops[e][i]["dma"] is None:
                    f.append(("e", e, i))
                    break
        return f

    def _collect(self, eng, reads, writes):
        deps = []
        for t in reads:
            if t.last_w is not None:
                deps.append((t.last_w, "raw"))
        for t in writes:
            if t.last_w is not None:
                deps.append((t.last_w, "waw"))
            for r in t.readers:
                deps.append((r, "war"))
        waits = []
        for d, kind in deps:
            if d[0] == "e":
                _, e2, idx = d
                if e2 == eng:
                    if eng == "pe" or eng == "sp":
                        continue
                    if kind != "raw":
                        continue
                if self.seen[eng][e2] >= idx:
                    continue
                self.seen[eng][e2] = idx
                self.ops[e2][idx]["signal"] = True
                waits.append(("e", e2, idx))
            else:
                _, t, cnt = d
                if self.seen_d[eng].get(t, 0) >= cnt:
                    continue
                self.seen_d[eng][t] = cnt
                waits.append(("d", t, cnt))
        return waits

    def op(self, eng, fn, reads=(), writes=()):
        ex = [t for t in reads if t.excl]
        if ex:
            reads = [t for t in reads if not t.excl]
            writes = list(writes) + [t for t in ex if t not in writes]
        waits = self._collect(eng, reads, writes)
        idx = len(self.ops[eng])
        self.ops[eng].append(dict(fn=fn, waits=waits, signal=False, dma=None))
        me = ("e", eng, idx)
        for t in writes:
            t.last_w = me
            t.readers = []
        for t in reads:
            if t not in writes:
                t.readers.append(me)
        return idx

    def dma(self, eng, fn, reads=(), writes=(), sem_tile=None):
        waits = self._collect(eng, reads, writes)
        st = sem_tile or (writes[0] if writes else reads[0])
        if st.dsem is None:
            st.dsem = True
            self.dma_tiles.append(st)
        st.dcount += 16
        me = ("d", st, st.dcount)
        self.ops[eng].append(dict(fn=fn, waits=waits, signal=False, dma=st))
        for t in writes:
            t.last_w = me
            t.readers = []
        for t in reads:
            if t not in writes:
                t.readers.append(me)

    def emit(self, final_tiles=()):
        nc = self.nc
        with ExitStack() as es:
            esem = {e: es.enter_context(nc.semaphore("s_" + e)) for e in ENGS}
            for i, t in enumerate(self.dma_tiles):
                t.dsem = es.enter_context(nc.semaphore("d%d" % i))
            signum = {}
            for e in ENGS:
                c = 0
                for i, o in enumerate(self.ops[e]):
                    if o["signal"]:
                        c += 1
                        signum[(e, i)] = c
            block = es.enter_context(nc.Block())

            def run(e, engobj):
                for i, o in enumerate(self.ops[e]):
                    for w in o["waits"]:
                        if w[0] == "e":
                            engobj.wait_ge(esem[w[1]], signum[(w[1], w[2])])
                        else:
                            engobj.wait_ge(w[1].dsem, w[2])
                    ins = o["fn"](engobj)
                    if o["dma"] is not None:
                        ins.then_inc(o["dma"].dsem, 16)
                    if o["signal"]:
                        ins.then_inc(esem[e], 1)
                if e == "sp":
                    for t in final_tiles:
                        engobj.wait_ge(t.dsem, t.dcount)

            @block.tensor
            def _(eng):
                run("pe", eng)

            @block.scalar
            def _(eng):
                run("act", eng)

            @block.vector
            def _(eng):
                run("dve", eng)

            @block.gpsimd
            def _(eng):
                run("pool", eng)

            @block.sync
            def _(eng):
                run("sp", eng)


class Rot:
    def __init__(self, items):
        self.items = list(items)
        self.i = 0

    def next(self):
        v = self.items[self.i % len(self.items)]
        self.i += 1
        return v


def mask_specs():
    return [("swa", 128, 1), ("d1", 64, 1), ("d4", 256, 4), ("d16", 1024, 16)]


def mask_layout():
    off = 0
    lay = {}
    for name, W, dil in mask_specs():
        Wt = -(-W // 128) * 128
        OFF = 384 + Wt
        width = OFF + Wt + 512
        lay[name] = (off, OFF, Wt, width, W, dil)
        off += width
    return lay, off


def build_consts(S):
    inv = 1.0 / (10000.0 ** (np.arange(0, 64, 2, dtype=np.float32) / 64.0))
    ang = np.arange(S, dtype=np.float32)[:, None] * inv[None, :].astype(np.float32)
    cos = np.cos(ang).astype(np.float32).T
    sin = np.sin(ang).astype(np.float32).T
    p = np.arange(128)
    rope = np.zeros((128, 2, S), np.float32)
    rope[:, 0, :] = cos[p % 32]
    sgn = np.where((p % 64) < 32, -1.0, 1.0).astype(np.float32)
    rope[:, 1, :] = sin[p % 32] * sgn[:, None]
    cm = np.zeros((128, 6, 128), np.float32)
    cm[:, 0, :] = 1.0
    cm[:, 1, :] = (p[:, None] // 64 == p[None, :] // 64)
    cm[:, 2, :] = (p[:, None] < 64)
    cm[:, 3, :] = (p[:, None] >= 64)
    cm[:, 4, :] = (p[:, None] == p[None, :])
    partner = np.where((p % 64) < 32, p + 32, p - 32)
    cm[:, 5, :] = (p[:, None] == partner[None, :])
    lay, tot = mask_layout()
    mt = np.full((128, tot), NEG, np.float32)
    for name, (off, OFF, Wt, width, W, dil) in lay.items():
        c = np.arange(width)
        delta = p[:, None] - c[None, :] + OFF
        valid = (np.abs(delta) <= W) & (delta % dil == 0)
        mt[:, off:off + width] = np.where(valid, 0.0, NEG)
    return rope, cm, mt


def gain_layout():
    lay = {}
    c = 0
    for l in range(4):
        lay[("attn", l)] = c; c += 8
        lay[("ffn", l)] = c; c += 8
    for j in range(2):
        lay[("qa", j)] = c; c += 3
        lay[("kva", j)] = c; c += 2
        lay[("qn_nope", j)] = c; c += 1
        lay[("qn_rope", j)] = c; c += 1
        lay[("kn_nope", j)] = c; c += 1
        lay[("kn_rope", j)] = c; c += 1
    lay["swa_q"] = c; c += 1
    lay["swa_k"] = c; c += 1
    lay["dil_q"] = c; c += 1
    lay["dil_k"] = c; c += 1
    lay["sink"] = c; c += 16
    return lay, c


def build_gains(inp):
    lay, n = gain_layout()
    g = np.zeros((128, n), np.float32)

    def put(col, vec):
        v = np.asarray(vec, np.float32)
        k = v.shape[0] // 128
        g[:, col:col + k] = v.reshape(k, 128).T

    for l in range(4):
        put(lay[("attn", l)], inp["attn_norm"][l])
        put(lay[("ffn", l)], inp["ffn_norm"][l])
    for j in range(2):
        put(lay[("qa", j)], inp["mla_q_a_norm"][j])
        put(lay[("kva", j)], inp["mla_kv_a_norm"][j])
        qn = np.asarray(inp["mla_q_norm"][j]); kn = np.asarray(inp["mla_k_norm"][j])
        put(lay[("qn_nope", j)], qn[:128])
        put(lay[("qn_rope", j)], np.concatenate([qn[128:], qn[128:]]))
        put(lay[("kn_nope", j)], kn[:128])
        put(lay[("kn_rope", j)], np.concatenate([kn[128:], kn[128:]]))
    put(lay["swa_q"], np.tile(np.asarray(inp["swa_q_norm"][0]), 2))
    put(lay["swa_k"], np.tile(np.asarray(inp["swa_k_norm"][0]), 2))
    put(lay["dil_q"], np.tile(np.asarray(inp["dil_q_norm"][0]), 2))
    put(lay["dil_k"], np.tile(np.asarray(inp["dil_k_norm"][0]), 2))
    g[:, lay["sink"]:lay["sink"] + 16] = np.broadcast_to(np.asarray(inp["swa_sink"][0], np.float32)[None, :], (128, 16))
    return g


def prep_weights(inp):
    f = lambda a: np.ascontiguousarray(np.asarray(a, np.float32))
    w = {}
    w["w_gate"] = f(inp["w_gate"]); w["w_up"] = f(inp["w_up"]); w["w_down"] = f(inp["w_down"])
    w["mla_wq_a"] = f(inp["mla_wq_a"])
    kva = np.asarray(inp["mla_wkv_a"], np.float32)
    w["mla_wkv_a"] = f(np.concatenate([kva, kva[:, :, 256:320]], axis=2))
    qb = np.asarray(inp["mla_wq_b"], np.float32).reshape(2, 384, 8, 192)
    cols = []
    for hp in range(4):
        cols += [qb[:, :, 2 * hp, :128], qb[:, :, 2 * hp + 1, :128], qb[:, :, 2 * hp, 128:], qb[:, :, 2 * hp + 1, 128:]]
    w["mla_wq_b"] = f(np.concatenate(cols, axis=2))
    kvb = np.asarray(inp["mla_wkv_b"], np.float32).reshape(2, 256, 8, 256)
    w["mla_wkb"] = f(kvb[:, :, :, :128].reshape(2, 256, 1024))
    w["mla_wvb"] = f(kvb[:, :, :, 128:].reshape(2, 256, 1024))
    w["mla_wo"] = f(inp["mla_wo"])
    sw = np.asarray(inp["swa_wqkv"], np.float32)[0]
    w["swa_wq"] = f(sw[:, :1024])
    k = sw[:, 1024:1280].reshape(1024, 4, 64)
    w["swa_wk"] = f(np.concatenate([k, k], axis=2).reshape(1024, 512))
    w["swa_wv"] = f(sw[:, 1280:1536])
    w["swa_wo"] = f(np.asarray(inp["swa_wo"], np.float32)[0])
    w["dil_wqkv"] = f(np.asarray(inp["dil_wqkv"], np.float32)[0])
    w["dil_wo"] = f(np.asarray(inp["dil_wo"], np.float32)[0])
    return w


WSHAPES = {
    "w_gate": [4, D, DFF], "w_up": [4, D, DFF], "w_down": [4, DFF, D],
    "mla_wq_a": [2, D, 384], "mla_wkv_a": [2, D, 384], "mla_wq_b": [2, 384, 1536],
    "mla_wkb": [2, 256, 1024], "mla_wvb": [2, 256, 1024], "mla_wo": [2, D, D],
    "swa_wq": [D, 1024], "swa_wk": [D, 512], "swa_wv": [D, 256], "swa_wo": [D, D],
    "dil_wqkv": [D, 4608], "dil_wo": [512, D],
}


class Builder:
    def __init__(self, S, NSEQ, layers=(0, 1, 2, 3), do_ffn=True):
        self.S, self.NSEQ, self.layers, self.do_ffn = S, NSEQ, tuple(layers), do_ffn
        self.NC = S // 512
        self.NT = S // 128
        self.glay, self.NG = gain_layout()
        self.mlay, self.MW = mask_layout()

    def mm(self, out, lhsT, rhs, start, stop, reads, w):
        self.P.op("pe", lambda e: e.matmul(out, lhsT, rhs, start=start, stop=stop), reads=reads, writes=[w])

    def act(self, out, in_, func, reads, writes, scale=1.0, bias=None):
        if bias is None:
            self.P.op("act", lambda e: e.activation(out=out, in_=in_, func=func, scale=scale), reads=reads, writes=writes)
        else:
            self.P.op("act", lambda e: e.activation(out=out, in_=in_, func=func, scale=scale, bias=bias), reads=reads, writes=writes)

    def tt(self, out, in0, in1, op, reads, writes, eng="dve"):
        self.P.op(eng, lambda e: e.tensor_tensor(out=out, in0=in0, in1=in1, op=op), reads=reads, writes=writes)

    def stt(self, out, in0, scalar, in1, op0, op1, reads, writes):
        self.P.op("dve", lambda e: e.scalar_tensor_tensor(out=out, in0=in0, scalar=scalar, in1=in1, op0=op0, op1=op1),
                  reads=reads, writes=writes)

    def tsmul(self, out, in0, scalar, reads, writes):
        self.P.op("dve", lambda e: e.tensor_scalar_mul(out=out, in0=in0, scalar1=scalar), reads=reads, writes=writes)

    def tsadd(self, out, in0, scalar, reads, writes):
        self.P.op("dve", lambda e: e.tensor_scalar_add(out=out, in0=in0, scalar1=scalar), reads=reads, writes=writes)

    def recip(self, out, in_, reads, writes):
        self.P.op("dve", lambda e: e.reciprocal(out=out, in_=in_), reads=reads, writes=writes)

    def copy(self, eng, out, in_, reads, writes):
        if eng == "act":
            self.P.op("act", lambda e: e.activation(out=out, in_=in_, func=AF.Copy), reads=reads, writes=writes)
        else:
            self.P.op(eng, lambda e: e.tensor_copy(out=out, in_=in_), reads=reads, writes=writes)

    def load_w(self, wap2d, kc, c0, n, slot=None):
        if slot is None:
            slot = self.wrot.next()
        sap, st = slot
        dst = sap[:, 0:kc * n].rearrange("p (c n) -> p c n", c=kc)
        src = wap2d.rearrange("(c p) n -> p c n", p=128)[:, :, c0:c0 + n]
        self.P.dma("pool", lambda e: e.dma_start(out=dst, in_=src), writes=[st])
        return dst, st

    def dbg(self, ap2d, tile, n=512):
        if not getattr(self, "debug", False) or self.dbg_i >= self.NDBG:
            return
        i = self.dbg_i
        self.dbg_i += 1
        stg = self.dbg_stage[i]
        stT = T("dbgst%d" % i)
        self.P.op("dve", lambda e: e.tensor_copy(out=stg[:, 0:n], in_=ap2d), reads=[tile], writes=[stT])
        self.P.dma("sp", lambda e: e.dma_start(out=self.dbg_out[i][:, 0:n], in_=stg[:, 0:n]), reads=[stT], sem_tile=self.dbg_sem)

    def arena_reset(self):
        self.aoff = 0
        self.fence = self.P.fence()

    def abf(self, n, name):
        assert self.aoff + n <= ARENA, (name, self.aoff, n)
        ap = self.arena[:, self.aoff:self.aoff + n]
        self.aoff += n
        return ap

    def af32(self, n, name):
        assert self.aoff % 2 == 0
        assert self.aoff + 2 * n <= ARENA, (name, self.aoff, n)
        ap = self.arena[:, self.aoff:self.aoff + 2 * n].bitcast(F32)
        self.aoff += 2 * n
        return ap

    def nT(self, name):
        return T(name, fence=self.fence)

    def gcol(self, key, k=0):
        c = self.glay[key] + k
        return self.gains[:, c:c + 1]

    def norm_x(self, t, gkey, hn, hnT, sqrot, r, rT):
        S = self.S
        ps, pst = self.bank[7]
        for c in range(8):
            sq, sqt = sqrot.next()
            xa = self.x3[:, c, t * 512:(t + 1) * 512]
            self.act(sq, xa, AF.Square, [self.xT[c][t]], [sqt])
            self.mm(ps, self.cm[:, 0, :], sq, c == 0, c == 7, [sqt, self.cmT], pst)
        self.act(r, ps, AF.Ln, [pst, self.cT], [rT], scale=1.0 / D, bias=self.epsb)
        self.act(r, r, AF.Exp, [rT], [rT], scale=-0.5)
        for c in range(8):
            xa = self.x3[:, c, t * 512:(t + 1) * 512]
            self.stt(hn[:, c, :], xa, self.gcol(gkey, c), r, ALU.mult, ALU.mult, [self.xT[c][t], rT, self.cT], [hnT])

    def rstd(self, r, rT, ps, pst, n):
        self.act(r, ps, AF.Ln, [pst, self.cT], [rT], scale=1.0 / n, bias=self.epsb)
        self.act(r, r, AF.Exp, [rT], [rT], scale=-0.5)

    def rope_chunk(self, p, pT, gcol, t, kg, kgT, t1, t1T, t2, t2T):
        swb, swT = self.bank[getattr(self, "swap_bank", 6)]
        tok = slice(t * 512, (t + 1) * 512)
        self.tsmul(kg, p, gcol, [pT, self.cT], [kgT])
        self.mm(swb, self.cm[:, 5, :], kg, True, True, [kgT, self.cmT], swT)
        self.stt(t1, p, gcol, self.rope[:, 0, tok], ALU.mult, ALU.mult, [pT, self.cT], [t1T])
        self.tt(t2, swb, self.rope[:, 1, tok], ALU.mult, [swT, self.cT], [t2T])
        self.tt(t1, t1, t2, ALU.add, [t1T, t2T], [t1T])

    def ffn(self, l):
        S, P = self.S, self.P
        HT = min(S, 1024)
        NCH = HT // 512
        for half in range(S // HT):
            self.arena_reset()
            hn = self.abf(8 * HT, "hn").rearrange("p (c s) -> p c s", c=8)
            hnT = [self.nT("hn%d" % i) for i in range(NCH)]
            ffh = self.abf(NFB * HT, "ffh").rearrange("p (f s) -> p f s", f=NFB)
            ffhT = [[self.nT("ffh") for _ in range(NCH)] for _ in range(NFB)]
            sqrot = Rot([(self.abf(512, "sq"), self.nT("sq")) for _ in range(3)])
            sgrot = Rot([(self.abf(512, "sg"), self.nT("sg")) for _ in range(2)])
            r = self.af32(512, "r"); rT = self.nT("r")
            for i in range(NCH):
                t = half * NCH + i
                self.norm_x(t, ("ffn", l), hn[:, :, i * 512:(i + 1) * 512], hnT[i], sqrot, r, rT)
            grot = Rot([self.bank[0], self.bank[1]])
            urot = Rot([self.bank[2], self.bank[3]])
            drot = Rot([self.bank[4], self.bank[5]])
            for fb in range(NFB):
                wg, wgT = self.load_w(self.W["w_gate"][l], 8, fb * 128, 128)
                wu, wuT = self.load_w(self.W["w_up"][l], 8, fb * 128, 128)
                for i in range(NCH):
                    pg, pgT = grot.next()
                    pu, puT = urot.next()
                    hs = hn[:, :, i * 512:(i + 1) * 512]
                    for c in range(8):
                        self.mm(pg, wg[:, c, :], hs[:, c, :], c == 0, c == 7, [wgT, hnT[i]], pgT)
                    for c in range(8):
                        self.mm(pu, wu[:, c, :], hs[:, c, :], c == 0, c == 7, [wuT, hnT[i]], puT)
                    sg, sgT = sgrot.next()
                    self.act(sg, pg, AF.Silu, [pgT], [sgT])
                    self.tt(ffh[:, fb, i * 512:(i + 1) * 512], pu, sg, ALU.mult, [puT, sgT], [ffhT[fb][i]])
            for dc in range(8):
                slot = self.wdrot.next()
                sap, st = slot
                dst = sap[:, :].rearrange("p (f n) -> p f n", f=NFB)
                src = self.W["w_down"][l].rearrange("(f p) n -> p f n", p=128)[:, :, dc * 128:(dc + 1) * 128]
                P.dma("pool", lambda e, dst=dst, src=src: e.dma_start(out=dst, in_=src), writes=[st])
                for i in range(NCH):
                    t = half * NCH + i
                    pd, pdT = drot.next()
                    for fb in range(NFB):
                        self.mm(pd, dst[:, fb, :], ffh[:, fb, i * 512:(i + 1) * 512], fb == 0, fb == NFB - 1,
                                [st, ffhT[fb][i]], pdT)
                    xa = self.x3[:, dc, t * 512:(t + 1) * 512]
                    self.tt(xa, pd, xa, ALU.add, [pdT], [self.xT[dc][t]])

    def mla(self, l, j):
        S, P, NC, NT = self.S, self.P, self.NC, self.NT
        W = self.W
        self.swap_bank = 6
        self.arena_reset()
        cqn = self.abf(3 * S, "cqn").rearrange("p (c s) -> p c s", c=3)
        ckvn = self.abf(2 * S, "ckvn").rearrange("p (c s) -> p c s", c=2)
        U = self.abf(S, "U")
        sqkr = self.abf(S, "sqkr")
        cqnT = [self.nT("cqn") for _ in range(NC)]
        ckvnT = [self.nT("ckvn") for _ in range(NC)]
        UT = [self.nT("U") for _ in range(NC)]
        sqkrT = [self.nT("sqkr") for _ in range(NC)]
        persist = self.aoff
        hnrot = Rot([(self.abf(8 * 512, "hn").rearrange("p (c s) -> p c s", c=8), self.nT("hn")) for _ in range(2)])
        sqrot = Rot([(self.abf(512, "sq"), self.nT("sq")) for _ in range(3)])
        rrot = Rot([(self.af32(512, "r"), self.nT("r")) for _ in range(3)])
        t1 = self.af32(512, "t1"); t1T = self.nT("t1")
        t2 = self.af32(512, "t2"); t2T = self.nT("t2")
        kg = self.abf(512, "kg"); kgT = self.nT("kg")
        prot = Rot([self.bank[i] for i in range(6)])
        hns = {}

        def donorm(t):
            if t < NC and t not in hns:
                hn_, hnT_ = hnrot.next()
                r_, rT_ = rrot.next()
                self.norm_x(t, ("attn", l), hn_, hnT_, sqrot, r_, rT_)
                hns[t] = (hn_, hnT_)
        donorm(0)
        for t in range(NC):
            tok = slice(t * 512, (t + 1) * 512)
            hn, hnT = hns[t]
            pq = []
            for jj in range(3):
                w_, wT = self.load_w(W["mla_wq_a"][j], 8, jj * 128, 128)
                p, pT = prot.next()
                for c in range(8):
                    self.mm(p, w_[:, c, :], hn[:, c, :], c == 0, c == 7, [wT, hnT], pT)
                pq.append((p, pT))
            pk = []
            for jj in range(3):
                w_, wT = self.load_w(W["mla_wkv_a"][j], 8, jj * 128, 128)
                p, pT = prot.next()
                for c in range(8):
                    self.mm(p, w_[:, c, :], hn[:, c, :], c == 0, c == 7, [wT, hnT], pT)
                pk.append((p, pT))
            donorm(t + 1)
            ss, ssT = self.bank[7]
            for jj in range(3):
                sq, sqt = sqrot.next()
                self.act(sq, pq[jj][0], AF.Square, [pq[jj][1]], [sqt])
                self.mm(ss, self.cm[:, 0, :], sq, jj == 0, jj == 2, [sqt, self.cmT], ssT)
            r, rT = rrot.next()
            self.rstd(r, rT, ss, ssT, 384)
            for jj in range(3):
                self.stt(cqn[:, jj, tok], pq[jj][0], self.gcol(("qa", j), jj), r, ALU.mult, ALU.mult,
                         [pq[jj][1], rT, self.cT], [cqnT[t]])
            for jj in range(2):
                sq, sqt = sqrot.next()
                self.act(sq, pk[jj][0], AF.Square, [pk[jj][1]], [sqt])
                self.mm(ss, self.cm[:, 0, :], sq, jj == 0, jj == 1, [sqt, self.cmT], ssT)
            r, rT = rrot.next()
            self.rstd(r, rT, ss, ssT, 256)
            for jj in range(2):
                self.stt(ckvn[:, jj, tok], pk[jj][0], self.gcol(("kva", j), jj), r, ALU.mult, ALU.mult,
                         [pk[jj][1], rT, self.cT], [ckvnT[t]])
            self.act(sqkr[:, tok], pk[2][0], AF.Square, [pk[2][1]], [sqkrT[t]])
            self.rope_chunk(pk[2][0], pk[2][1], self.gcol(("kn_rope", j)), t, kg, kgT, t1, t1T, t2, t2T)
            self.copy("dve", U[:, tok], t1, [t1T], [UT[t]])
        self.swap_bank = 7
        self.aoff = persist
        self.fence = P.fence()
        kT = self.abf(3 * S, "kT").rearrange("p (c s) -> p c s", c=3)
        kTT = [self.nT("kT") for _ in range(NC)]
        V = self.abf(NT * 256, "V").rearrange("p (t n) -> p t n", n=256)
        VT = [self.nT("V") for _ in range(NC)]
        qrot = Rot([(self.abf(4 * 512, "qT").rearrange("p (c s) -> p c s", c=4), self.nT("qT")) for _ in range(2)])
        for (qb, qbT) in qrot.items:
            P.op("pool", lambda e, a=qb[64:128, 2, :]: e.memset(a, 0.0), writes=[qbT])
            P.op("pool", lambda e, a=qb[0:64, 3, :]: e.memset(a, 0.0), writes=[qbT])
        orot = Rot([(self.abf(2 * 512, "oT").rearrange("p (c s) -> p c s", c=2), self.nT("oT")) for _ in range(2)])
        ptrot = Rot([(self.abf(512, "pt"), self.nT("pt")) for _ in range(4)])
        sqrot = Rot([(self.abf(512, "sq"), self.nT("sq")) for _ in range(3)])
        rrot = Rot([(self.af32(512, "r"), self.nT("r")) for _ in range(4)])
        t1 = self.af32(512, "t1"); t1T = self.nT("t1")
        t2 = self.af32(512, "t2"); t2T = self.nT("t2")
        kg = self.abf(512, "kg"); kgT = self.nT("kg")
        daccrot = Rot([(self.af32(512, "dacc"), self.nT("dacc")) for _ in range(2)])
        dhi = self.abf(512, "dhi"); dhiT = self.nT("dhi")
        dlo = self.abf(512, "dlo"); dloT = self.nT("dlo")
        arot = Rot([self.bank[0], self.bank[1], self.bank[2]])
        accrot = Rot([(self.bank[3], self.bank[4]), (self.bank[5], self.bank[6])])
        scale = 192.0 ** -0.5
        for hp in range(4):
            wk = [self.load_w(W["mla_wkb"][j], 2, (2 * hp + h) * 128, 128) for h in range(2)]
            wv, wvT = self.load_w(W["mla_wvb"][j], 2, 2 * hp * 128, 256)
            items = []
            for t in range(NC):
                tok = slice(t * 512, (t + 1) * 512)
                rk = []
                for h in range(2):
                    box = {}

                    def kproj(t=t, tok=tok, h=h, box=box):
                        p, pT = arot.next()
                        for c in range(2):
                            self.mm(p, wk[h][0][:, c, :], ckvn[:, c, tok], c == 0, c == 1, [wk[h][1], ckvnT[t]], pT)
                        box["p"] = (p, pT)

                    def kfin(t=t, tok=tok, h=h, box=box, rk=rk):
                        p, pT = box["p"]
                        sq, sqt = sqrot.next()
                        self.act(sq, p, AF.Square, [pT], [sqt])
                        ss, ssT = self.bank[7]
                        self.mm(ss, self.cm[:, 0, :], sq, True, False, [sqt, self.cmT], ssT)
                        self.mm(ss, self.cm[:, 2, :], sqkr[:, tok], False, True, [sqkrT[t], self.cmT], ssT)
                        r, rT = rrot.next()
                        self.rstd(r, rT, ss, ssT, 192)
                        self.stt(kT[:, h, tok], p, self.gcol(("kn_nope", j)), r, ALU.mult, ALU.mult,
                                 [pT, rT, self.cT], [kTT[t]])
                        rk.append((r, rT))
                        if h == 1:
                            for hh in range(2):
                                rows = slice(64 * hh, 64 * hh + 64)
                                self.tt(kT[rows, 2, tok], U[rows, tok], rk[hh][0][rows, :], ALU.mult, [UT[t], rk[hh][1]], [kTT[t]])
                    items.append((kproj, kfin))
                for i2 in range(2):
                    box = {}

                    def vproj(t=t, i2=i2, box=box):
                        p, pT = arot.next()
                        for ii in range(2):
                            tile_ = t * 4 + i2 * 2 + ii
                            for c in range(2):
                                self.mm(p[:, ii * 256:(ii + 1) * 256], ckvn[:, c, tile_ * 128:(tile_ + 1) * 128], wv[:, c, :],
                                        c == 0, c == 1, [wvT, ckvnT[t]], pT)
                        box["p"] = (p, pT)

                    def vfin(t=t, i2=i2, box=box):
                        p, pT = box["p"]
                        t0_ = t * 4 + i2 * 2
                        self.copy("act", V[:, t0_:t0_ + 2, :], p.rearrange("p (t n) -> p t n", n=256), [pT], [VT[t]])
                    items.append((vproj, vfin))
            self.run_pipe(items)
            qbuf = {}

            def emitQ(qc):
                if qc >= NC or qc in qbuf:
                    return
                tok = slice(qc * 512, (qc + 1) * 512)
                qT, qTT = qrot.next()
                pq = []
                for jj in range(3):
                    w_, wT = self.load_w(W["mla_wq_b"][j], 3, hp * 384 + jj * 128, 128)
                    p, pT = arot.next()
                    for c in range(3):
                        self.mm(p, w_[:, c, :], cqn[:, c, tok], c == 0, c == 2, [wT, cqnT[qc]], pT)
                    pq.append((p, pT))
                sqs = []
                for jj in range(3):
                    sq, sqt = sqrot.next()
                    self.act(sq, pq[jj][0], AF.Square, [pq[jj][1]], [sqt])
                    sqs.append((sq, sqt))
                rq = []
                for h in range(2):
                    ss, ssT = self.bank[7]
                    self.mm(ss, self.cm[:, 0, :], sqs[h][0], True, False, [sqs[h][1], self.cmT], ssT)
                    self.mm(ss, self.cm[:, 2 + h, :], sqs[2][0], False, True, [sqs[2][1], self.cmT], ssT)
                    r, rT = rrot.next()
                    self.rstd(r, rT, ss, ssT, 192)
                    rq.append((r, rT))
                    self.stt(qT[:, h, :], pq[h][0], self.gcol(("qn_nope", j)), r, ALU.mult, ALU.mult,
                             [pq[h][1], rT, self.cT], [qTT])
                self.rope_chunk(pq[2][0], pq[2][1], self.gcol(("qn_rope", j)), qc, kg, kgT, t1, t1T, t2, t2T)
                for h in range(2):
                    rows = slice(64 * h, 64 * h + 64)
                    self.tt(qT[rows, 2 + h, :], t1[rows, :], rq[h][0][rows, :], ALU.mult, [t1T, rq[h][1]], [qTT])
                qbuf[qc] = (qT, qTT)

            emitQ(0)
            for qc in range(NC):
                tok = slice(qc * 512, (qc + 1) * 512)
                emitQ(qc)
                qT, qTT = qbuf[qc]
                oT, oTT = orot.next()
                for h in range(2):
                    rows = slice(64 * h, 64 * h + 64)
                    (po, poT), (pdn, pdnT) = accrot.next()
                    pend = None
                    for kt in range(NT + 1):
                        if kt < NT:
                            ktok = slice(kt * 128, (kt + 1) * 128)
                            ps, psT = arot.next()
                            self.mm(ps, kT[:, h, ktok], qT[:, h, :], True, False, [kTT[kt // 4], qTT], psT)
                            self.mm(ps, kT[:, 2, ktok], qT[:, 2 + h, :], False, True, [kTT[kt // 4], qTT], psT)
                            pt, ptT = ptrot.next()
                            self.act(pt, ps, AF.Exp, [psT], [ptT], scale=scale)
                            nxt = (kt, pt, ptT)
                        else:
                            nxt = None
                        if pend is not None:
                            k0, pt0, ptT0 = pend
                            self.mm(po, V[:, k0, h * 128:(h + 1) * 128], pt0, k0 == 0, k0 == NT - 1, [VT[k0 // 4], ptT0], poT)
                            if k0 == 0:
                                dacc, daccT = daccrot.next()
                                self.copy("dve", dacc, pt0, [ptT0], [daccT])
                            else:
                                self.tt(dacc, dacc, pt0, ALU.add, [ptT0], [daccT])
                        pend = nxt
                    self.copy("dve", dhi, dacc, [daccT], [dhiT])
                    self.tt(dlo, dacc, dhi, ALU.subtract, [daccT, dhiT], [dloT])
                    self.mm(pdn, self.cm[:, 0, :], dhi, True, False, [self.cmT, dhiT], pdnT)
                    self.mm(pdn, self.cm[:, 0, :], dlo, False, True, [self.cmT, dloT], pdnT)
                    r, rT = rrot.next()
                    self.act(r, pdn, AF.Ln, [pdnT], [rT])
                    self.act(r, r, AF.Exp, [rT], [rT], scale=-1.0)
                    self.tt(oT[:, h, :], po, r, ALU.mult, [poT, rT], [oTT])
                    if h == 0 and PREFETCH_Q:
                        emitQ(qc + 1)
                for hf in range(2):
                    slot = self.wrot.next()
                    sap, st = slot
                    dst = sap[:, 0:1024].rearrange("p (c n) -> p c n", c=2)
                    src = W["mla_wo"][j][2 * hp * 128:(2 * hp + 2) * 128, :].rearrange("(c p) n -> p c n", p=128)[:, :, hf * 512:(hf + 1) * 512]
                    P.dma("pool", lambda e, dst=dst, src=src: e.dma_start(out=dst, in_=src), writes=[st])
                    for d4 in range(4):
                        dc = hf * 4 + d4
                        pw, pwT = arot.next()
                        for h in range(2):
                            self.mm(pw, dst[:, h, d4 * 128:(d4 + 1) * 128], oT[:, h, :], h == 0, h == 1, [st, oTT], pwT)
                        xa = self.x3[:, dc, tok]
                        self.tt(xa, pw, xa, ALU.add, [pwT], [self.xT[dc][qc]])

    def qk_proj(self, w2d, col0, hn, hnT, prot):
        w_, wT = self.load_w(w2d, 8, col0, 128)
        p, pT = prot.next()
        for c in range(8):
            self.mm(p, w_[:, c, :], hn[:, c, :], c == 0, c == 7, [wT, hnT], pT)
        return p, pT

    def qk_fin(self, p, pT, gcol, t, dst, dstT, st):
        sqrot, rrot, t1, t1T, t2, t2T, kg, kgT, prot = st
        sq, sqt = sqrot.next()
        self.act(sq, p, AF.Square, [pT], [sqt])
        ss, ssT = self.bank[7]
        self.mm(ss, self.cm[:, 1, :], sq, True, True, [sqt, self.cmT], ssT)
        r, rT = rrot.next()
        self.rstd(r, rT, ss, ssT, 64)
        self.rope_chunk(p, pT, gcol, t, kg, kgT, t1, t1T, t2, t2T)
        self.tt(dst, t1, r, ALU.mult, [t1T, rT], [dstT])

    def qk_item(self, w2d, col0, hn, hnT, gcol, t, dst, dstT, st, pre=None):
        box = {}

        def proj():
            if pre is not None:
                pre()
            box["p"] = self.qk_proj(w2d, col0, hn, hnT, st[-1])

        def fin():
            self.qk_fin(box["p"][0], box["p"][1], gcol, t, dst, dstT, st)
        return (proj, fin)

    @staticmethod
    def run_pipe(items):
        n = len(items)
        if n:
            items[0][0]()
        for i in range(n):
            if i + 1 < n:
                items[i + 1][0]()
            items[i][1]()

    def alloc_zq(self):
        self.zq = []
        for par in range(2):
            z = self.abf(512, "zq"); zT = self.nT("zq")
            zr = slice(64, 128) if par == 0 else slice(0, 64)
            self.P.op("pool", lambda e, a=z[zr, :]: e.memset(a, 0.0), writes=[zT])
            self.zq.append((z, zT))

    def band_attn(self, mname, qT, qTT, kfn, vfn, VT, qc, nheads, hinfo, po_epilogue, ptrot, arot, scale):
        NT = self.NT
        off, OFF, Wt, width, W, dil = self.mlay[mname]
        dt_max = -(-W // 128)
        kts = list(range(max(0, 4 * qc - Wt // 128), min(NT - 1, 4 * qc + 3 + Wt // 128) + 1))
        for hh in range(nheads):
            qch, rows, kch = hinfo(hh)
            zq, zqT = self.zq[hh % 2]
            self.copy("pool", zq[rows, :], qT[rows, qch, :], [qTT], [zqT])
            po, poT = self.accr.next()
            contrib = {jq: [kt for kt in kts if abs(kt - (4 * qc + jq)) <= dt_max] for jq in range(4)}
            pv_list = [(kt, jq) for kt in kts for jq in range(4) if kt in contrib[jq]]
            pv_first, pv_last = pv_list[0], pv_list[-1]
            pend = None
            for kt in kts + [None]:
                if kt is not None:
                    ktok = slice(kt * 128, (kt + 1) * 128)
                    ps, psT = arot.next()
                    u0 = off + 512 * qc - 128 * kt + OFF
                    jqs = [jq for jq in range(4) if kt in contrib[jq]]
                    c0, c1 = jqs[0] * 128, (jqs[-1] + 1) * 128
                    self.mm(ps[:, c0:c1], self.cm[:, 4, :], self.mtab[:, u0 + c0:u0 + c1], True, False, [self.cmT], psT)
                    kap, kTt = kfn(kch, slice(0, 128), ktok, kt)
                    self.mm(ps[:, c0:c1], kap, zq[:, c0:c1], False, True, [kTt, zqT], psT)
                    pt, ptT = ptrot.next()
                    self.act(pt[:, c0:c1], ps[:, c0:c1], AF.Exp, [psT], [ptT], scale=scale)
                    nxt = (kt, pt, ptT)
                else:
                    nxt = None
                if pend is not None:
                    k0, pt0, ptT0 = pend
                    vap = vfn(hh, k0)
                    for jq in range(4):
                        cl = contrib[jq]
                        if k0 in cl:
                            self.mm(po[:, jq * 128:jq * 128 + 65], pt0[:, jq * 128:(jq + 1) * 128], vap,
                                    (k0, jq) == pv_first, (k0, jq) == pv_last, [VT[k0 // 4], ptT0], poT)
                pend = nxt
            po_epilogue(hh, po, poT)

    def swa(self, l):
        self.swap_bank = 6
        S, P, NC, NT = self.S, self.P, self.NC, self.NT
        W = self.W
        scale = 64.0 ** -0.5
        self.arena_reset()
        hnF = self.abf(8 * S, "hnF").rearrange("p (c s) -> p c s", c=8)
        hnFT = [self.nT("hnF") for _ in range(NC)]
        sq0 = Rot([(self.abf(512, "sq"), self.nT("sq")) for _ in range(3)])
        r0 = self.af32(512, "r"); r0T = self.nT("r")
        for t in range(NC):
            self.norm_x(t, ("attn", l), hnF[:, :, t * 512:(t + 1) * 512], hnFT[t], sq0, r0, r0T)
        mark = self.aoff
        for g in range(4):
            self.aoff = mark
            self.fence = P.fence()
            qT = self.abf(2 * S, "qT").rearrange("p (c s) -> p c s", c=2)
            qTT = [self.nT("qT") for _ in range(NC)]
            kT = self.abf(S, "kT")
            kTT = [self.nT("kT") for _ in range(NC)]
            Va = self.abf(NT * 80, "Va").rearrange("p (t n) -> p t n", n=80)
            VT = [self.nT("Va") for _ in range(NC)]
            otm_rot = Rot([(self.abf(4 * 256, "otm").rearrange("p (t n) -> p t n", n=256), self.nT("otm")) for _ in range(2)])
            oTrot = Rot([(self.abf(2 * 512, "oT").rearrange("p (c s) -> p c s", c=2), self.nT("oT")) for _ in range(2)])
            ptrot = Rot([(self.abf(512, "pt"), self.nT("pt")) for _ in range(4)])
            sqrot = Rot([(self.abf(512, "sq"), self.nT("sq")) for _ in range(3)])
            rrot = Rot([(self.af32(512, "r"), self.nT("r")) for _ in range(3)])
            t1 = self.af32(512, "t1"); t1T = self.nT("t1")
            t2 = self.af32(512, "t2"); t2T = self.nT("t2")
            kg = self.abf(512, "kg"); kgT = self.nT("kg")
            den = self.af32(8, "den"); denT = self.nT("den")
            self.alloc_zq()
            prot = Rot([self.bank[0], self.bank[1], self.bank[2]])
            st = (sqrot, rrot, t1, t1T, t2, t2T, kg, kgT, prot)
            for t in range(NC):
                P.op("pool", lambda e, a=Va[:, t * 4:(t + 1) * 4, 64:65]: e.memset(a, 1.0), writes=[VT[t]])
            items = []
            for t in range(NC):
                tok = slice(t * 512, (t + 1) * 512)
                hn, hnT = hnF[:, :, tok], hnFT[t]
                for jj in range(2):
                    items.append(self.qk_item(W["swa_wq"], (2 * g + jj) * 128, hn, hnT, self.gcol("swa_q"), t, qT[:, jj, tok], qTT[t], st))
                items.append(self.qk_item(W["swa_wk"], g * 128, hn, hnT, self.gcol("swa_k"), t, kT[:, tok], kTT[t], st))
                box = {}

                def vproj(t=t, hn=hn, hnT=hnT, box=box):
                    wv, wvT = self.load_w(W["swa_wv"], 8, g * 64, 64)
                    p, pT = prot.next()
                    for ii in range(4):
                        for c in range(8):
                            self.mm(p[:, ii * 64:(ii + 1) * 64], hn[:, c, ii * 128:(ii + 1) * 128], wv[:, c, :], c == 0, c == 7,
                                    [wvT, hnT], pT)
                    box["p"] = (p, pT)

                def vfin(t=t, box=box):
                    p, pT = box["p"]
                    self.copy("act", Va[:, t * 4:(t + 1) * 4, 0:64], p[:, 0:256].rearrange("p (t n) -> p t n", n=64), [pT], [VT[t]])
                items.append((vproj, vfin))
            self.run_pipe(items)
            if g == 1:
                self.dbg(qT[:, 0, 0:512], qTT[0])
                self.dbg(qT[:, 1, 0:512], qTT[0])
                self.dbg(kT[:, 0:512], kTT[0])
                self.dbg(Va[:, 0:4, :].rearrange("p t n -> p (t n)"), VT[0], n=320)
            arot = Rot([self.bank[0], self.bank[1], self.bank[2]])
            self.accr = Rot([self.bank[3], self.bank[4], self.bank[5]])
            sinkc = self.glay["sink"]
            for qc in range(NC):
                tok = slice(qc * 512, (qc + 1) * 512)
                otm, otmT = otm_rot.next()

                def epi(hh, po, poT, otm=otm, otmT=otmT, g=g):
                    po3 = po.rearrange("p (t n) -> p t n", n=128)
                    hq = 4 * g + hh
                    self.tsadd(den[:, 0:4], po3[:, :, 64], self.esink[:, hq:hq + 1], [poT, self.cT], [denT])
                    self.recip(den[:, 0:4], den[:, 0:4], [denT], [denT])
                    self.tt(otm[:, :, hh * 64:(hh + 1) * 64], po3[:, :, 0:64],
                            den[:, 0:4].unsqueeze(2).broadcast_to([128, 4, 64]), ALU.mult, [poT, denT], [otmT])

                self.band_attn("swa", qT[:, :, tok], qTT[qc],
                               lambda kch, rows, ktok, kt: (kT[rows, ktok], kTT[kt // 4]),
                               lambda hh, k0: Va[:, k0, 0:65], VT, qc, 4,
                               lambda hh: (hh // 2, slice(64 * (hh % 2), 64 * (hh % 2) + 64), 0),
                               epi, ptrot, arot, scale)
                if qc == 0:
                    self.dbg(otm[:, 0:2, :].rearrange("p t n -> p (t n)"), otmT)
                    self.dbg(otm[:, 2:4, :].rearrange("p t n -> p (t n)"), otmT)
                self.otm_to_x(otm, otmT, 2, oTrot, W["swa_wo"], g * 256, qc, arot)

    def otm_to_x(self, otm, otmT, nfc, oTrot, wo2d, row0, qc, arot):
        P = self.P
        tok = slice(qc * 512, (qc + 1) * 512)
        oT, oTT = oTrot.next()
        pb, pbT = self.bank[7]
        pbb = pb.bitcast(BF16)
        for fc in range(nfc):
            for jq in range(4):
                P.op("pe", lambda e, o=pbb[:, fc * 512 + jq * 128: fc * 512 + (jq + 1) * 128],
                     i=otm[:, jq, fc * 128:(fc + 1) * 128]: e.transpose(o, i, self.cm[:, 4, :]),
                     reads=[otmT, self.cmT], writes=[pbT])
        self.copy("act", oT, pbb[:, 0:nfc * 512].rearrange("p (c s) -> p c s", c=nfc), [pbT], [oTT])
        if getattr(self, "debug", False) and self.dbg_i in (96, 97):
            self.dbg(oT[:, 0, :], oTT)
            self.dbg(oT[:, 1, :], oTT)
        for hf in range(2):
            slot = self.wrot.next()
            sap, st = slot
            n = 1024 // nfc
            assert n == 512
            dst = sap[:, 0:1024].rearrange("p (c n) -> p c n", c=nfc)
            src = wo2d[row0:row0 + nfc * 128, :].rearrange("(c p) n -> p c n", p=128)[:, :, hf * 512:(hf + 1) * 512]
            P.dma("pool", lambda e, dst=dst, src=src: e.dma_start(out=dst, in_=src), writes=[st])
            for d4 in range(4):
                dc = hf * 4 + d4
                pw, pwT = arot.next()
                for fc in range(nfc):
                    self.mm(pw, dst[:, fc, d4 * 128:(d4 + 1) * 128], oT[:, fc, :], fc == 0, fc == nfc - 1, [st, oTT], pwT)
                xa = self.x3[:, dc, tok]
                self.tt(xa, pw, xa, ALU.add, [pwT], [self.xT[dc][qc]])

    def dil(self, l):
        self.swap_bank = 6
        S, P, NC, NT = self.S, self.P, self.NC, self.NT
        W = self.W
        scale = 64.0 ** -0.5
        names = ["d1", "d4", "d16"]
        self.arena_reset()
        otmF = self.abf(NT * 512, "otmF").rearrange("p (t n) -> p t n", n=512)
        otmFT = [self.nT("otmF") for _ in range(NC)]
        mark = self.aoff
        for hf in range(2):
            self.aoff = mark
            self.fence = P.fence()
            hnrot = Rot([(self.abf(8 * 512, "hn").rearrange("p (c s) -> p c s", c=8), self.nT("hn")) for _ in range(2)])
            qT = self.abf(2 * S, "qT").rearrange("p (c s) -> p c s", c=2)
            kT = self.abf(2 * S, "kT").rearrange("p (c s) -> p c s", c=2)
            Va = self.abf(NT * 264, "Va").rearrange("p (t h n) -> p t h n", h=4, n=66)
            nacc = self.af32(NT * 264, "nacc").rearrange("p (t h n) -> p t h n", h=4, n=66)
            naccT = [[self.nT("nacc") for _ in range(4)] for _ in range(NC)]
            ptrot = Rot([(self.abf(512, "pt"), self.nT("pt")) for _ in range(3)])
            sqrot = Rot([(self.abf(512, "sq"), self.nT("sq")) for _ in range(3)])
            rrot = Rot([(self.af32(512, "r"), self.nT("r")) for _ in range(2)])
            t1 = self.af32(512, "t1"); t1T = self.nT("t1")
            t2 = self.af32(512, "t2"); t2T = self.nT("t2")
            kg = self.abf(512, "kg"); kgT = self.nT("kg")
            den = self.af32(8, "den"); denT = self.nT("den")
            self.alloc_zq()
            prot = Rot([self.bank[0], self.bank[1], self.bank[2]])
            st = (sqrot, rrot, t1, t1T, t2, t2T, kg, kgT, prot)
            for gi in range(3):
                qTT = [self.nT("qT") for _ in range(NC)]
                kTT = [self.nT("kT") for _ in range(NC)]
                VT = [self.nT("Va") for _ in range(NC)]
                if gi > 0:
                    f = P.fence()
                    for lst in (qTT, kTT, VT):
                        for tt_ in lst:
                            tt_.readers = list(f)
                for t in range(NC):
                    P.op("pool", lambda e, a=Va[:, t * 4:(t + 1) * 4, :, 64:65]: e.memset(a, 1.0), writes=[VT[t]])
                hns = {}

                def donorm(t):
                    if t < NC and t not in hns:
                        hn_, hnT_ = hnrot.next()
                        r, rT = rrot.next()
                        self.norm_x(t, ("attn", l), hn_, hnT_, sqrot, r, rT)
                        hns[t] = (hn_, hnT_)
                donorm(0)
                items = []
                for t in range(NC):
                    tok = slice(t * 512, (t + 1) * 512)
                    first = True
                    for jj in range(2):
                        col = gi * 512 + hf * 256 + jj * 128
                        for (cc, gk, dst_, dT_) in ((col, "dil_q", qT[:, jj, tok], qTT[t]), (1536 + col, "dil_k", kT[:, jj, tok], kTT[t])):
                            box = {}

                            def proj(t=t, cc=cc, box=box, first=first):
                                if first:
                                    donorm(t + 1)
                                box["p"] = self.qk_proj(W["dil_wqkv"], cc, hns[t][0], hns[t][1], prot)

                            def fin(t=t, gk=gk, dst_=dst_, dT_=dT_, box=box):
                                self.qk_fin(box["p"][0], box["p"][1], self.gcol(gk), t, dst_, dT_, st)
                            items.append((proj, fin))
                            first = False
                    for i2 in range(2):
                        box = {}

                        def vproj(t=t, i2=i2, box=box):
                            hn_, hnT_ = hns[t]
                            wv, wvT = self.load_w(W["dil_wqkv"], 8, 3072 + gi * 512 + hf * 256, 128)
                            wv2, wv2T = self.load_w(W["dil_wqkv"], 8, 3072 + gi * 512 + hf * 256 + 128, 128)
                            p, pT = prot.next()
                            for ii in range(2):
                                tl = i2 * 2 + ii
                                for (wv_, wvT_, cc) in ((wv, wvT, 0), (wv2, wv2T, 128)):
                                    for c in range(8):
                                        self.mm(p[:, ii * 256 + cc: ii * 256 + cc + 128], hn_[:, c, tl * 128:(tl + 1) * 128], wv_[:, c, :],
                                                c == 0, c == 7, [wvT_, hnT_], pT)
                            box["p"] = (p, pT)

                        def vfin(t=t, i2=i2, box=box):
                            p, pT = box["p"]
                            t0_ = t * 4 + i2 * 2
                            self.copy("act", Va[:, t0_:t0_ + 2, :, 0:64], p.rearrange("p (t h n) -> p t h n", h=4, n=64), [pT], [VT[t]])
                        items.append((vproj, vfin))
                self.run_pipe(items)
                arot = Rot([self.bank[0], self.bank[1], self.bank[2]])
                self.accr = Rot([self.bank[3], self.bank[4], self.bank[5]])
                for qc in range(NC):
                    tok = slice(qc * 512, (qc + 1) * 512)

                    def epi(hh, po, poT, qc=qc, gi=gi):
                        po3 = po.rearrange("p (t n) -> p t n", n=128)[:, :, 0:65]
                        dst = nacc[:, qc * 4:(qc + 1) * 4, hh, 0:65]
                        if gi == 0:
                            self.copy("dve", dst, po3, [poT], [naccT[qc][hh]])
                        else:
                            self.tt(dst, po3, dst, ALU.add, [poT], [naccT[qc][hh]])

                    self.band_attn(names[gi], qT[:, :, tok], qTT[qc],
                                   lambda kch, rows, ktok, kt: (kT[rows, kch, ktok], kTT[kt // 4]),
                                   lambda hh, k0: Va[:, k0, hh, 0:65], VT, qc, 4,
                                   lambda hh: (hh // 2, slice(64 * (hh % 2), 64 * (hh % 2) + 64), hh // 2),
                                   epi, ptrot, arot, scale)
            for qc in range(NC):
                for hh in range(4):
                    nq = nacc[:, qc * 4:(qc + 1) * 4, hh, :]
                    c0 = hf * 256 + hh * 64
                    self.recip(den[:, 0:4], nq[:, :, 64], [naccT[qc][hh]], [denT])
                    self.tt(otmF[:, qc * 4:(qc + 1) * 4, c0:c0 + 64], nq[:, :, 0:64],
                            den[:, 0:4].unsqueeze(2).broadcast_to([128, 4, 64]), ALU.mult, [naccT[qc][hh], denT], [otmFT[qc]])
        self.aoff = mark
        self.fence = P.fence()
        oTrot = Rot([(self.abf(2 * 512, "oT").rearrange("p (c s) -> p c s", c=2), self.nT("oT")) for _ in range(2)])
        arot = Rot([self.bank[0], self.bank[1], self.bank[2]])
        for qc in range(NC):
            for hf in range(2):
                self.otm_to_x(otmF[:, qc * 4:(qc + 1) * 4, hf * 256:(hf + 1) * 256], otmFT[qc], 2, oTrot, W["dil_wo"], hf * 256, qc, arot)

    def build(self):
        S, NSEQ = self.S, self.NSEQ
        nc = bass.Bass("TRN2", target_bir_lowering=False)
        self.nc = nc
        xin = nc.dram_tensor("xT", [NSEQ, 8, 128, S], F32, kind="ExternalInput").ap()
        yout = nc.dram_tensor("yT", [NSEQ, 8, 128, S], F32, kind="ExternalOutput").ap()
        rope_h = nc.dram_tensor("rope", [128, 2, S], F32, kind="ExternalInput").ap()
        cm_h = nc.dram_tensor("cmat", [128, 6, 128], F32, kind="ExternalInput").ap()
        mt_h = nc.dram_tensor("mtab", [128, self.MW], F32, kind="ExternalInput").ap()
        g_h = nc.dram_tensor("gains", [128, self.NG], F32, kind="ExternalInput").ap()
        self.W = {k: nc.dram_tensor(k, shp, F32, kind="ExternalInput").ap() for k, shp in WSHAPES.items()}
        self.NDBG = 12
        self.dbg_i = 0
        if getattr(self, "debug", False):
            self.dbg_out = nc.dram_tensor("dbg", [self.NDBG, 128, 512], F32, kind="ExternalOutput").ap()
            self.dbg_sem = T("dbgsem")
        with ExitStack() as es:
            sb = lambda name, shape, dt: es.enter_context(nc.sbuf_tensor(name, shape, dt))
            xs = sb("xs", [128, 8 * S], F32)
            self.x3 = xs[:, :].rearrange("p (c s) -> p c s", c=8)
            self.rope = sb("rope_sb", [128, 2 * S], BF16)[:, :].rearrange("p (c s) -> p c s", c=2)
            self.cm = sb("cm_sb", [128, 6 * 128], BF16)[:, :].rearrange("p (c s) -> p c s", c=6)
            self.mtab = sb("mtab_sb", [128, self.MW], BF16)[:, :]
            self.gains = sb("gains_sb", [128, self.NG], F32)[:, :]
            self.esink = sb("esink", [128, 16], F32)[:, :]
            self.epsb = sb("epsb", [128, 1], F32)[:, :]
            self.arena = sb("arena", [128, ARENA], BF16)[:, :]
            wslots = [sb("ws%d" % i, [128, 1024], BF16)[:, :] for i in range(8)]
            wdslots = [sb("wd%d" % i, [128, NFB * 128], BF16)[:, :] for i in range(2)]
            banks = [es.enter_context(nc.psum_tensor("pb%d" % i, [128, 512], F32))[:, :] for i in range(8)]
            if getattr(self, "debug", False):
                self.dbg_stage = [sb("dbgst%d" % i, [128, 512], F32)[:, :] for i in range(self.NDBG)]
            P = Prog(nc)
            self.P = P
            self.bank = [(banks[i], T("bank%d" % i, excl=True)) for i in range(8)]
            self.wrot = Rot([(wslots[i], T("ws%d" % i)) for i in range(8)])
            self.wdrot = Rot([(wdslots[i], T("wd%d" % i)) for i in range(2)])
            self.cT = T("consts")
            self.cmT = T("cm")
            self.xT = [[T("x%d_%d" % (c, t)) for t in range(self.NC)] for c in range(8)]
            xsem = [T("xsem%d" % c) for c in range(8)]
            self.fence = []
            P.dma("pool", lambda e: e.dma_start(out=self.rope, in_=rope_h), writes=[self.cT])
            P.dma("sp", lambda e: e.dma_start(out=self.gains, in_=g_h), writes=[self.cT])
            P.dma("pool", lambda e: e.dma_start(out=self.cm, in_=cm_h), writes=[self.cmT])
            P.dma("pool", lambda e: e.dma_start(out=self.mtab, in_=mt_h), writes=[self.cmT])
            P.op("dve", lambda e: e.memset(self.epsb, EPS), writes=[self.cT])
            sc = self.glay["sink"]
            self.act(self.esink, self.gains[:, sc:sc + 16], AF.Exp, [self.cT], [self.cT])
            def load_x(s, c):
                P.dma("sp", lambda e, c=c, s=s: e.dma_start(out=self.x3[:, c, :], in_=xin[s, c]),
                      writes=self.xT[c], sem_tile=xsem[c])

            for c in range(8):
                load_x(0, c)
            for s in range(NSEQ):
                for l in self.layers:
                    kind = l % 3
                    if kind == 0:
                        self.mla(l, l // 3)
                    elif kind == 1:
                        self.swa(l)
                    else:
                        self.dil(l)
                    if self.do_ffn:
                        self.ffn(l)
                for c in range(8):
                    P.dma("sp", lambda e, c=c, s=s: e.dma_start(out=yout[s, c], in_=self.x3[:, c, :]),
                          reads=self.xT[c], sem_tile=xsem[c])
                    if s + 1 < NSEQ:
                        load_x(s + 1, c)
            fin = list(xsem)
            if getattr(self, "debug", False) and self.dbg_i > 0:
                fin.append(self.dbg_sem)
            P.emit(final_tiles=fin)
        return nc


_CACHE = {}


def run_device(xT_cores, inp, S, NSEQ, layers=(0, 1, 2, 3), do_ffn=True):
    key = (S, NSEQ, tuple(layers), do_ffn)
    if key not in _CACHE:
        _CACHE[key] = Builder(S, NSEQ, layers, do_ffn).build()
    nc = _CACHE[key]
    rope, cm, mt = build_consts(S)
    gains = build_gains(inp)
    w = prep_weights(inp)
    in_maps = []
    for xc in xT_cores:
        m = {"xT": xc, "rope": rope, "cmat": cm, "mtab": mt, "gains": gains}
        m.update(w)
        in_maps.append(m)
    res = run_bass_kernel_spmd(nc, in_maps, core_ids=list(range(len(xT_cores))))
    return [r["yT"] for r in res.results]


def kernel(**inputs):
    xp = np.asarray(inputs["x_prompt"], np.float32)
    xs_ = np.asarray(inputs["x_sample"], np.float32)
    B, S, _ = xp.shape
    Bd = xs_.shape[0]
    allx = np.concatenate([xp, xs_], axis=0)
    n = allx.shape[0]
    NSEQ = n // 8
    order = np.arange(n).reshape(8, NSEQ)
    cores = []
    for c in range(8):
        xc = allx[order[c]]
        xT = np.ascontiguousarray(xc.transpose(0, 2, 1)).reshape(NSEQ, 8, 128, S)
        cores.append(xT)
    outs = run_device(cores, inputs, S, NSEQ)
    y = np.empty_like(allx)
    for c in range(8):
        yT = np.asarray(outs[c], np.float32).reshape(NSEQ, D, S)
        y[order[c]] = yT.transpose(0, 2, 1)
    return (np.ascontiguousarray(y[:B]), np.ascontiguousarray(y[B:]))
```

```python
import numpy as np
import concourse.bass as bass
import concourse.mybir as mybir
from concourse.bass_utils import run_bass_kernel_spmd
from contextlib import ExitStack

F32 = mybir.dt.float32
BF16 = mybir.dt.bfloat16
ALU = mybir.AluOpType
AF = mybir.ActivationFunctionType

D = 1024
DFF = 2816
NFB = 22
EPS = 1e-6
NEG = -30000.0
ENGS = ("pe", "act", "dve", "pool", "sp")
ARENA = 46080


class T:
    __slots__ = ("name", "last_w", "readers", "dsem", "dcount", "excl")

    def __init__(self, name, excl=False, fence=()):
        self.name = name
        self.excl = excl
        self.last_w = None
        self.readers = list(fence)
        self.dsem = None
        self.dcount = 0


class Prog:
    def __init__(self, nc):
        self.nc = nc
        self.ops = {e: [] for e in ENGS}
        self.seen = {e: {e2: -1 for e2 in ENGS} for e in ENGS}
        self.seen_d = {e: {} for e in ENGS}
        self.dma_tiles = []

    def fence(self):
        f = []
        for e in ("pe", "act", "dve", "pool"):
            for i in range(len(self.ops[e]) - 1, -1, -1):
                if self.ops[e][i]["dma"] is None:
                    f.append(("e", e, i))
                    break
        return f

    def _collect(self, eng, reads, writes):
        deps = []
        for t in reads:
            if t.last_w is not None:
                deps.append((t.last_w, "raw"))
        for t in writes:
            if t.last_w is not None:
                deps.append((t.last_w, "waw"))
            for r in t.readers:
                deps.append((r, "war"))
        waits = []
        for d, kind in deps:
            if d[0] == "e":
                _, e2, idx = d
                if e2 == eng:
                    if eng == "pe" or eng == "sp":
                        continue
                    if kind != "raw":
                        continue
                if self.seen[eng][e2] >= idx:
                    continue
                self.seen[eng][e2] = idx
                self.ops[e2][idx]["signal"] = True
                waits.append(("e", e2, idx))
            else:
                _, t, cnt = d
                if self.seen_d[eng].get(t, 0) >= cnt:
                    continue
                self.seen_d[eng][t] = cnt
                waits.append(("d", t, cnt))
        return waits

    def op(self, eng, fn, reads=(), writes=()):
        ex = [t for t in reads if t.excl]
        if ex:
            reads = [t for t in reads if not t.excl]
            writes = list(writes) + [t for t in ex if t not in writes]
        waits = self._collect(eng, reads, writes)
        idx = len(self.ops[eng])
        self.ops[eng].append(dict(fn=fn, waits=waits, signal=False, dma=None))
        me = ("e", eng, idx)
        for t in writes:
            t.last_w = me
            t.readers = []
        for t in reads:
            if t not in writes:
                t.readers.append(me)
        return idx

    def dma(self, eng, fn, reads=(), writes=(), sem_tile=None):
        waits = self._collect(eng, reads, writes)
        st = sem_tile or (writes[0] if writes else reads[0])
        if st.dsem is None:
            st.dsem = True
            self.dma_tiles.append(st)
        st.dcount += 16
        me = ("d", st, st.dcount)
        self.ops[eng].append(dict(fn=fn, waits=waits, signal=False, dma=st))
        for t in writes:
            t.last_w = me
            t.readers = []
        for t in reads:
            if t not in writes:
                t.readers.append(me)

    def emit(self, final_tiles=()):
        nc = self.nc
        with ExitStack() as es:
            esem = {e: es.enter_context(nc.semaphore("s_" + e)) for e in ENGS}
            for i, t in enumerate(self.dma_tiles):
                t.dsem = es.enter_context(nc.semaphore("d%d" % i))
            signum = {}
            for e in ENGS:
                c = 0
                for i, o in enumerate(self.ops[e]):
                    if o["signal"]:
                        c += 1
                        signum[(e, i)] = c
            block = es.enter_context(nc.Block())

            def run(e, engobj):
                for i, o in enumerate(self.ops[e]):
                    for w in o["waits"]:
                        if w[0] == "e":
                            engobj.wait_ge(esem[w[1]], signum[(w[1], w[2])])
                        else:
                            engobj.wait_ge(w[1].dsem, w[2])
                    ins = o["fn"](engobj)
                    if o["dma"] is not None:
                        ins.then_inc(o["dma"].dsem, 16)
                    if o["signal"]:
                        ins.then_inc(esem[e], 1)
                if e == "sp":
                    for t in final_tiles:
                        engobj.wait_ge(t.dsem, t.dcount)

            @block.tensor
            def _(eng):
                run("pe", eng)

            @block.scalar
            def _(eng):
                run("act", eng)

            @block.vector
            def _(eng):
                run("dve", eng)

            @block.gpsimd
            def _(eng):
                run("pool", eng)

            @block.sync
            def _(eng):
                run("sp", eng)


class Rot:
    def __init__(self, items):
        self.items = list(items)
        self.i = 0

    def next(self):
        v = self.items[self.i % len(self.items)]
        self.i += 1
        return v


def mask_specs():
    return [("swa", 128, 1), ("d1", 64, 1), ("d4", 256, 4), ("d16", 1024, 16)]


def mask_layout():
    off = 0
    lay = {}
    for name, W, dil in mask_specs():
        Wt = -(-W // 128) * 128
        OFF = 384 + Wt
        width = OFF + Wt + 512
        lay[name] = (off, OFF, Wt, width, W, dil)
        off += width
    return lay, off


def build_consts(S):
    inv = 1.0 / (10000.0 ** (np.arange(0, 64, 2, dtype=np.float32) / 64.0))
    ang = np.arange(S, dtype=np.float32)[:, None] * inv[None, :].astype(np.float32)
    cos = np.cos(ang).astype(np.float32).T
    sin = np.sin(ang).astype(np.float32).T
    p = np.arange(128)
    rope = np.zeros((128, 2, S), np.float32)
    rope[:, 0, :] = cos[p % 32]
    sgn = np.where((p % 64) < 32, -1.0, 1.0).astype(np.float32)
    rope[:, 1, :] = sin[p % 32] * sgn[:, None]
    cm = np.zeros((128, 6, 128), np.float32)
    cm[:, 0, :] = 1.0
    cm[:, 1, :] = (p[:, None] // 64 == p[None, :] // 64)
    cm[:, 2, :] = (p[:, None] < 64)
    cm[:, 3, :] = (p[:, None] >= 64)
    cm[:, 4, :] = (p[:, None] == p[None, :])
    partner = np.where((p % 64) < 32, p + 32, p - 32)
    cm[:, 5, :] = (p[:, None] == partner[None, :])
    lay, tot = mask_layout()
    mt = np.full((128, tot), NEG, np.float32)
    for name, (off, OFF, Wt, width, W, dil) in lay.items():
        c = np.arange(width)
        delta = p[:, None] - c[None, :] + OFF
        valid = (np.abs(delta) <= W) & (delta % dil == 0)
        mt[:, off:off + width] = np.where(valid, 0.0, NEG)
    return rope, cm, mt


def gain_layout():
    lay = {}
    c = 0
    for l in range(4):
        lay[("attn", l)] = c; c += 8
        lay[("ffn", l)] = c; c += 8
    for j in range(2):
        lay[("qa", j)] = c; c += 3
        lay[("kva", j)] = c; c += 2
        lay[("qn_nope", j)] = c; c += 1
        lay[("qn_rope", j)] = c; c += 1
        lay[("kn_nope", j)] = c; c += 1
        lay[("kn_rope", j)] = c; c += 1
    lay["swa_q"] = c; c += 1
    lay["swa_k"] = c; c += 1
    lay["dil_q"] = c; c += 1
    lay["dil_k"] = c; c += 1
    lay["sink"] = c; c += 16
    return lay, c


def build_gains(inp):
    lay, n = gain_layout()
    g = np.zeros((128, n), np.float32)

    def put(col, vec):
        v = np.asarray(vec, np.float32)
        k = v.shape[0] // 128
        g[:, col:col + k] = v.reshape(k, 128).T

    for l in range(4):
        put(lay[("attn", l)], inp["attn_norm"][l])
        put(lay[("ffn", l)], inp["ffn_norm"][l])
    for j in range(2):
        put(lay[("qa", j)], inp["mla_q_a_norm"][j])
        put(lay[("kva", j)], inp["mla_kv_a_norm"][j])
        qn = np.asarray(inp["mla_q_norm"][j]); kn = np.asarray(inp["mla_k_norm"][j])
        put(lay[("qn_nope", j)], qn[:128])
        put(lay[("qn_rope", j)], np.concatenate([qn[128:], qn[128:]]))
        put(lay[("kn_nope", j)], kn[:128])
        put(lay[("kn_rope", j)], np.concatenate([kn[128:], kn[128:]]))
    put(lay["swa_q"], np.tile(np.asarray(inp["swa_q_norm"][0]), 2))
    put(lay["swa_k"], np.tile(np.asarray(inp["swa_k_norm"][0]), 2))
    put(lay["dil_q"], np.tile(np.asarray(inp["dil_q_norm"][0]), 2))
    put(lay["dil_k"], np.tile(np.asarray(inp["dil_k_norm"][0]), 2))
    g[:, lay["sink"]:lay["sink"] + 16] = np.broadcast_to(np.asarray(inp["swa_sink"][0], np.float32)[None, :], (128, 16))
    return g


def prep_weights(inp):
    f = lambda a: np.ascontiguousarray(np.asarray(a, np.float32))
    w = {}
    w["w_gate"] = f(inp["w_gate"]); w["w_up"] = f(inp["w_up"]); w["w_down"] = f(inp["w_down"])
    w["mla_wq_a"] = f(inp["mla_wq_a"])
    kva = np.asarray(inp["mla_wkv_a"], np.float32)
    w["mla_wkv_a"] = f(np.concatenate([kva, kva[:, :, 256:320]], axis=2))
    qb = np.asarray(inp["mla_wq_b"], np.float32).reshape(2, 384, 8, 192)
    cols = []
    for hp in range(4):
        cols += [qb[:, :, 2 * hp, :128], qb[:, :, 2 * hp + 1, :128], qb[:, :, 2 * hp, 128:], qb[:, :, 2 * hp + 1, 128:]]
    w["mla_wq_b"] = f(np.concatenate(cols, axis=2))
    kvb = np.asarray(inp["mla_wkv_b"], np.float32).reshape(2, 256, 8, 256)
    w["mla_wkb"] = f(kvb[:, :, :, :128].reshape(2, 256, 1024))
    w["mla_wvb"] = f(kvb[:, :, :, 128:].reshape(2, 256, 1024))
    w["mla_wo"] = f(inp["mla_wo"])
    sw = np.asarray(inp["swa_wqkv"], np.float32)[0]
    w["swa_wq"] = f(sw[:, :1024])
    k = sw[:, 1024:1280].reshape(1024, 4, 64)
    w["swa_wk"] = f(np.concatenate([k, k], axis=2).reshape(1024, 512))
    w["swa_wv"] = f(sw[:, 1280:1536])
    w["swa_wo"] = f(np.asarray(inp["swa_wo"], np.float32)[0])
    w["dil_wqkv"] = f(np.asarray(inp["dil_wqkv"], np.float32)[0])
    w["dil_wo"] = f(np.asarray(inp["dil_wo"], np.float32)[0])
    return w


WSHAPES = {
    "w_gate": [4, D, DFF], "w_up": [4, D, DFF], "w_down": [4, DFF, D],
    "mla_wq_a": [2, D, 384], "mla_wkv_a": [2, D, 384], "mla_wq_b": [2, 384, 1536],
    "mla_wkb": [2, 256, 1024], "mla_wvb": [2, 256, 1024], "mla_wo": [2, D, D],
    "swa_wq": [D, 1024], "swa_wk": [D, 512], "swa_wv": [D, 256], "swa_wo": [D, D],
    "dil_wqkv": [D, 4608], "dil_wo": [512, D],
}


class Builder:
    def __init__(self, S, NSEQ, layers=(0, 1, 2, 3), do_ffn=True):
        self.S, self.NSEQ, self.layers, self.do_ffn = S, NSEQ, tuple(layers), do_ffn
        self.NC = S // 512
        self.NT = S // 128
        self.glay, self.NG = gain_layout()
        self.mlay, self.MW = mask_layout()

    def mm(self, out, lhsT, rhs, start, stop, reads, w):
        self.P.op("pe", lambda e: e.matmul(out, lhsT, rhs, start=start, stop=stop), reads=reads, writes=[w])

    def act(self, out, in_, func, reads, writes, scale=1.0, bias=None):
        if bias is None:
            self.P.op("act", lambda e: e.activation(out=out, in_=in_, func=func, scale=scale), reads=reads, writes=writes)
        else:
            self.P.op("act", lambda e: e.activation(out=out, in_=in_, func=func, scale=scale, bias=bias), reads=reads, writes=writes)

    def tt(self, out, in0, in1, op, reads, writes, eng="dve"):
        self.P.op(eng, lambda e: e.tensor_tensor(out=out, in0=in0, in1=in1, op=op), reads=reads, writes=writes)

    def stt(self, out, in0, scalar, in1, op0, op1, reads, writes):
        self.P.op("dve", lambda e: e.scalar_tensor_tensor(out=out, in0=in0, scalar=scalar, in1=in1, op0=op0, op1=op1),
                  reads=reads, writes=writes)

    def tsmul(self, out, in0, scalar, reads, writes):
        self.P.op("dve", lambda e: e.tensor_scalar_mul(out=out, in0=in0, scalar1=scalar), reads=reads, writes=writes)

    def tsadd(self, out, in0, scalar, reads, writes):
        self.P.op("dve", lambda e: e.tensor_scalar_add(out=out, in0=in0, scalar1=scalar), reads=reads, writes=writes)

    def recip(self, out, in_, reads, writes):
        self.P.op("dve", lambda e: e.reciprocal(out=out, in_=in_), reads=reads, writes=writes)

    def copy(self, eng, out, in_, reads, writes):
        if eng == "act":
            self.P.op("act", lambda e: e.activation(out=out, in_=in_, func=AF.Copy), reads=reads, writes=writes)
        else:
            self.P.op(eng, lambda e: e.tensor_copy(out=out, in_=in_), reads=reads, writes=writes)

    def load_w(self, wap2d, kc, c0, n, slot=None):
        if slot is None:
            slot = self.wrot.next()
        sap, st = slot
        dst = sap[:, 0:kc * n].rearrange("p (c n) -> p c n", c=kc)
        src = wap2d.rearrange("(c p) n -> p c n", p=128)[:, :, c0:c0 + n]
        self.P.dma("pool", lambda e: e.dma_start(out=dst, in_=src), writes=[st])
        return dst, st

    def dbg(self, ap2d, tile, n=512):
        if not getattr(self, "debug", False) or self.dbg_i >= self.NDBG:
            return
        i = self.dbg_i
        self.dbg_i += 1
        stg = self.dbg_stage[i]
        stT = T("dbgst%d" % i)
        self.P.op("dve", lambda e: e.tensor_copy(out=stg[:, 0:n], in_=ap2d), reads=[tile], writes=[stT])
        self.P.dma("sp", lambda e: e.dma_start(out=self.dbg_out[i][:, 0:n], in_=stg[:, 0:n]), reads=[stT], sem_tile=self.dbg_sem)

    def arena_reset(self):
        self.aoff = 0
        self.fence = self.P.fence()

    def abf(self, n, name):
        assert self.aoff + n <= ARENA, (name, self.aoff, n)
        ap = self.arena[:, self.aoff:self.aoff + n]
        self.aoff += n
        return ap

    def af32(self, n, name):
        assert self.aoff % 2 == 0
        assert self.aoff + 2 * n <= ARENA, (name, self.aoff, n)
        ap = self.arena[:, self.aoff:self.aoff + 2 * n].bitcast(F32)
        self.aoff += 2 * n
        return ap

    def nT(self, name):
        return T(name, fence=self.fence)

    def gcol(self, key, k=0):
        c = self.glay[key] + k
        return self.gains[:, c:c + 1]

    def norm_x(self, t, gkey, hn, hnT, sqrot, r, rT):
        S = self.S
        ps, pst = self.bank[7]
        for c in range(8):
            sq, sqt = sqrot.next()
            xa = self.x3[:, c, t * 512:(t + 1) * 512]
            self.act(sq, xa, AF.Square, [self.xT[c][t]], [sqt])
            self.mm(ps, self.cm[:, 0, :], sq, c == 0, c == 7, [sqt, self.cmT], pst)
        self.act(r, ps, AF.Ln, [pst, self.cT], [rT], scale=1.0 / D, bias=self.epsb)
        self.act(r, r, AF.Exp, [rT], [rT], scale=-0.5)
        for c in range(8):
            xa = self.x3[:, c, t * 512:(t + 1) * 512]
            self.stt(hn[:, c, :], xa, self.gcol(gkey, c), r, ALU.mult, ALU.mult, [self.xT[c][t], rT, self.cT], [hnT])

    def rstd(self, r, rT, ps, pst, n):
        self.act(r, ps, AF.Ln, [pst, self.cT], [rT], scale=1.0 / n, bias=self.epsb)
        self.act(r, r, AF.Exp, [rT], [rT], scale=-0.5)

    def rope_chunk(self, p, pT, gcol, t, kg, kgT, t1, t1T, t2, t2T):
        swb, swT = self.bank[6]
        tok = slice(t * 512, (t + 1) * 512)
        self.tsmul(kg, p, gcol, [pT, self.cT], [kgT])
        self.mm(swb, self.cm[:, 5, :], kg, True, True, [kgT, self.cmT], swT)
        self.stt(t1, p, gcol, self.rope[:, 0, tok], ALU.mult, ALU.mult, [pT, self.cT, self.ropeT], [t1T])
        self.tt(t2, swb, self.rope[:, 1, tok], ALU.mult, [swT, self.ropeT], [t2T])
        self.tt(t1, t1, t2, ALU.add, [t1T, t2T], [t1T])

    def ffn(self, l):
        S, P = self.S, self.P
        HT = min(S, 1024)
        NCH = HT // 512
        for half in range(S // HT):
            self.arena_reset()
            hn = self.abf(8 * HT, "hn").rearrange("p (c s) -> p c s", c=8)
            hnT = [self.nT("hn%d" % i) for i in range(NCH)]
            ffh = self.abf(NFB * HT, "ffh").rearrange("p (f s) -> p f s", f=NFB)
            ffhT = [[self.nT("ffh") for _ in range(NCH)] for _ in range(NFB)]
            sqrot = Rot([(self.abf(512, "sq"), self.nT("sq")) for _ in range(3)])
            sgrot = Rot([(self.abf(512, "sg"), self.nT("sg")) for _ in range(2)])
            r = self.af32(512, "r"); rT = self.nT("r")
            for i in range(NCH):
                t = half * NCH + i
                self.norm_x(t, ("ffn", l), hn[:, :, i * 512:(i + 1) * 512], hnT[i], sqrot, r, rT)
            grot = Rot([self.bank[0], self.bank[1]])
            urot = Rot([self.bank[2], self.bank[3]])
            drot = Rot([self.bank[4], self.bank[5]])
            for fb in range(NFB):
                wg, wgT = self.load_w(self.W["w_gate"][l], 8, fb * 128, 128)
                wu, wuT = self.load_w(self.W["w_up"][l], 8, fb * 128, 128)
                for i in range(NCH):
                    pg, pgT = grot.next()
                    pu, puT = urot.next()
                    hs = hn[:, :, i * 512:(i + 1) * 512]
                    for c in range(8):
                        self.mm(pg, wg[:, c, :], hs[:, c, :], c == 0, c == 7, [wgT, hnT[i]], pgT)
                    for c in range(8):
                        self.mm(pu, wu[:, c, :], hs[:, c, :], c == 0, c == 7, [wuT, hnT[i]], puT)
                    sg, sgT = sgrot.next()
                    self.act(sg, pg, AF.Silu, [pgT], [sgT])
                    self.tt(ffh[:, fb, i * 512:(i + 1) * 512], pu, sg, ALU.mult, [puT, sgT], [ffhT[fb][i]])
            for dc in range(8):
                slot = self.wdrot.next()
                sap, st = slot
                dst = sap[:, :].rearrange("p (f n) -> p f n", f=NFB)
                src = self.W["w_down"][l].rearrange("(f p) n -> p f n", p=128)[:, :, dc * 128:(dc + 1) * 128]
                P.dma("pool", lambda e, dst=dst, src=src: e.dma_start(out=dst, in_=src), writes=[st])
                for i in range(NCH):
                    t = half * NCH + i
                    pd, pdT = drot.next()
                    for fb in range(NFB):
                        self.mm(pd, dst[:, fb, :], ffh[:, fb, i * 512:(i + 1) * 512], fb == 0, fb == NFB - 1,
                                [st, ffhT[fb][i]], pdT)
                    xa = self.x3[:, dc, t * 512:(t + 1) * 512]
                    self.tt(xa, pd, xa, ALU.add, [pdT], [self.xT[dc][t]])

    def mla(self, l, j):
        S, P, NC, NT = self.S, self.P, self.NC, self.NT
        W = self.W
        self.arena_reset()
        cqn = self.abf(3 * S, "cqn").rearrange("p (c s) -> p c s", c=3)
        ckvn = self.abf(2 * S, "ckvn").rearrange("p (c s) -> p c s", c=2)
        U = self.abf(S, "U")
        sqkr = self.abf(S, "sqkr")
        cqnT = [self.nT("cqn") for _ in range(NC)]
        ckvnT = [self.nT("ckvn") for _ in range(NC)]
        UT = [self.nT("U") for _ in range(NC)]
        sqkrT = [self.nT("sqkr") for _ in range(NC)]
        persist = self.aoff
        hnrot = Rot([(self.abf(8 * 512, "hn").rearrange("p (c s) -> p c s", c=8), self.nT("hn")) for _ in range(2)])
        sqrot = Rot([(self.abf(512, "sq"), self.nT("sq")) for _ in range(3)])
        rrot = Rot([(self.af32(512, "r"), self.nT("r")) for _ in range(2)])
        t1 = self.af32(512, "t1"); t1T = self.nT("t1")
        t2 = self.af32(512, "t2"); t2T = self.nT("t2")
        kg = self.abf(512, "kg"); kgT = self.nT("kg")
        prot = Rot([self.bank[i] for i in range(6)])
        for t in range(NC):
            tok = slice(t * 512, (t + 1) * 512)
            hn, hnT = hnrot.next()
            r, rT = rrot.next()
            self.norm_x(t, ("attn", l), hn, hnT, sqrot, r, rT)
            pq = []
            for jj in range(3):
                w_, wT = self.load_w(W["mla_wq_a"][j], 8, jj * 128, 128)
                p, pT = prot.next()
                for c in range(8):
                    self.mm(p, w_[:, c, :], hn[:, c, :], c == 0, c == 7, [wT, hnT], pT)
                pq.append((p, pT))
            ss, ssT = self.bank[7]
            for jj in range(3):
                sq, sqt = sqrot.next()
                self.act(sq, pq[jj][0], AF.Square, [pq[jj][1]], [sqt])
                self.mm(ss, self.cm[:, 0, :], sq, jj == 0, jj == 2, [sqt, self.cmT], ssT)
            r, rT = rrot.next()
            self.rstd(r, rT, ss, ssT, 384)
            for jj in range(3):
                self.stt(cqn[:, jj, tok], pq[jj][0], self.gcol(("qa", j), jj), r, ALU.mult, ALU.mult,
                         [pq[jj][1], rT, self.cT], [cqnT[t]])
            pk = []
            for jj in range(3):
                w_, wT = self.load_w(W["mla_wkv_a"][j], 8, jj * 128, 128)
                p, pT = prot.next()
                for c in range(8):
                    self.mm(p, w_[:, c, :], hn[:, c, :], c == 0, c == 7, [wT, hnT], pT)
                pk.append((p, pT))
            for jj in range(2):
                sq, sqt = sqrot.next()
                self.act(sq, pk[jj][0], AF.Square, [pk[jj][1]], [sqt])
                self.mm(ss, self.cm[:, 0, :], sq, jj == 0, jj == 1, [sqt, self.cmT], ssT)
            r, rT = rrot.next()
            self.rstd(r, rT, ss, ssT, 256)
            for jj in range(2):
                self.stt(ckvn[:, jj, tok], pk[jj][0], self.gcol(("kva", j), jj), r, ALU.mult, ALU.mult,
                         [pk[jj][1], rT, self.cT], [ckvnT[t]])
            self.act(sqkr[:, tok], pk[2][0], AF.Square, [pk[2][1]], [sqkrT[t]])
            self.rope_chunk(pk[2][0], pk[2][1], self.gcol(("kn_rope", j)), t, kg, kgT, t1, t1T, t2, t2T)
            self.copy("dve", U[:, tok], t1, [t1T], [UT[t]])
        self.aoff = persist
        self.fence = P.fence()
        kT = self.abf(3 * S, "kT").rearrange("p (c s) -> p c s", c=3)
        kTT = [self.nT("kT") for _ in range(NC)]
        V = self.abf(NT * 256, "V").rearrange("p (t n) -> p t n", n=256)
        VT = [self.nT("V") for _ in range(NC)]
        qrot = Rot([(self.abf(4 * 512, "qT").rearrange("p (c s) -> p c s", c=4), self.nT("qT")) for _ in range(2)])
        for (qb, qbT) in qrot.items:
            P.op("pool", lambda e, a=qb[64:128, 2, :]: e.memset(a, 0.0), writes=[qbT])
            P.op("pool", lambda e, a=qb[0:64, 3, :]: e.memset(a, 0.0), writes=[qbT])
        orot = Rot([(self.abf(2 * 512, "oT").rearrange("p (c s) -> p c s", c=2), self.nT("oT")) for _ in range(2)])
        ptrot = Rot([(self.abf(512, "pt"), self.nT("pt")) for _ in range(4)])
        sqrot = Rot([(self.abf(512, "sq"), self.nT("sq")) for _ in range(3)])
        rrot = Rot([(self.af32(512, "r"), self.nT("r")) for _ in range(4)])
        t1 = self.af32(512, "t1"); t1T = self.nT("t1")
        t2 = self.af32(512, "t2"); t2T = self.nT("t2")
        kg = self.abf(512, "kg"); kgT = self.nT("kg")
        arot = Rot([self.bank[0], self.bank[1], self.bank[2]])
        accrot = Rot([(self.bank[3], self.bank[4]), (self.bank[5], self.bank[6])])
        scale = 192.0 ** -0.5
        for hp in range(4):
            wk = [self.load_w(W["mla_wkb"][j], 2, (2 * hp + h) * 128, 128) for h in range(2)]
            wv, wvT = self.load_w(W["mla_wvb"][j], 2, 2 * hp * 128, 256)
            for t in range(NC):
                tok = slice(t * 512, (t + 1) * 512)
                rk = []
                for h in range(2):
                    p, pT = arot.next()
                    for c in range(2):
                        self.mm(p, wk[h][0][:, c, :], ckvn[:, c, tok], c == 0, c == 1, [wk[h][1], ckvnT[t]], pT)
                    sq, sqt = sqrot.next()
                    self.act(sq, p, AF.Square, [pT], [sqt])
                    ss, ssT = self.bank[7]
                    self.mm(ss, self.cm[:, 0, :], sq, True, False, [sqt, self.cmT], ssT)
                    self.mm(ss, self.cm[:, 2, :], sqkr[:, tok], False, True, [sqkrT[t], self.cmT], ssT)
                    r, rT = rrot.next()
                    self.rstd(r, rT, ss, ssT, 192)
                    self.stt(kT[:, h, tok], p, self.gcol(("kn_nope", j)), r, ALU.mult, ALU.mult,
                             [pT, rT, self.cT], [kTT[t]])
                    rk.append((r, rT))
                for h in range(2):
                    rows = slice(64 * h, 64 * h + 64)
                    self.tt(kT[rows, 2, tok], U[rows, tok], rk[h][0][rows, :], ALU.mult, [UT[t], rk[h][1]], [kTT[t]])
                for i2 in range(2):
                    p, pT = arot.next()
                    for ii in range(2):
                        tile_ = t * 4 + i2 * 2 + ii
                        for c in range(2):
                            self.mm(p[:, ii * 256:(ii + 1) * 256], ckvn[:, c, tile_ * 128:(tile_ + 1) * 128], wv[:, c, :],
                                    c == 0, c == 1, [wvT, ckvnT[t]], pT)
                    t0_ = t * 4 + i2 * 2
                    self.copy("act", V[:, t0_:t0_ + 2, :], p.rearrange("p (t n) -> p t n", n=256), [pT], [VT[t]])
            for qc in range(NC):
                tok = slice(qc * 512, (qc + 1) * 512)
                qT, qTT = qrot.next()
                pq = []
                for jj in range(3):
                    w_, wT = self.load_w(W["mla_wq_b"][j], 3, hp * 384 + jj * 128, 128)
                    p, pT = arot.next()
                    for c in range(3):
                        self.mm(p, w_[:, c, :], cqn[:, c, tok], c == 0, c == 2, [wT, cqnT[qc]], pT)
                    pq.append((p, pT))
                sqs = []
                for jj in range(3):
                    sq, sqt = sqrot.next()
                    self.act(sq, pq[jj][0], AF.Square, [pq[jj][1]], [sqt])
                    sqs.append((sq, sqt))
                rq = []
                for h in range(2):
                    ss, ssT = self.bank[7]
                    self.mm(ss, self.cm[:, 0, :], sqs[h][0], True, False, [sqs[h][1], self.cmT], ssT)
                    self.mm(ss, self.cm[:, 2 + h, :], sqs[2][0], False, True, [sqs[2][1], self.cmT], ssT)
                    r, rT = rrot.next()
                    self.rstd(r, rT, ss, ssT, 192)
                    rq.append((r, rT))
                    self.stt(qT[:, h, :], pq[h][0], self.gcol(("qn_nope", j)), r, ALU.mult, ALU.mult,
                             [pq[h][1], rT, self.cT], [qTT])
                self.rope_chunk(pq[2][0], pq[2][1], self.gcol(("qn_rope", j)), qc, kg, kgT, t1, t1T, t2, t2T)
                for h in range(2):
                    rows = slice(64 * h, 64 * h + 64)
                    self.tt(qT[rows, 2 + h, :], t1[rows, :], rq[h][0][rows, :], ALU.mult, [t1T, rq[h][1]], [qTT])
                oT, oTT = orot.next()
                for h in range(2):
                    rows = slice(64 * h, 64 * h + 64)
                    (po, poT), (pdn, pdnT) = accrot.next()
                    pend = None
                    for kt in range(NT + 1):
                        if kt < NT:
                            ktok = slice(kt * 128, (kt + 1) * 128)
                            ps, psT = arot.next()
                            self.mm(ps, kT[:, h, ktok], qT[:, h, :], True, False, [kTT[kt // 4], qTT], psT)
                            self.mm(ps, kT[:, 2, ktok], qT[:, 2 + h, :], False, True, [kTT[kt // 4], qTT], psT)
                            pt, ptT = ptrot.next()
                            self.act(pt, ps, AF.Exp, [psT], [ptT], scale=scale)
                            nxt = (kt, pt, ptT)
                        else:
                            nxt = None
                        if pend is not None:
                            k0, pt0, ptT0 = pend
                            self.mm(po, V[:, k0, h * 128:(h + 1) * 128], pt0, k0 == 0, k0 == NT - 1, [VT[k0 // 4], ptT0], poT)
                            self.mm(pdn, self.cm[:, 0, :], pt0, k0 == 0, k0 == NT - 1, [self.cmT, ptT0], pdnT)
                        pend = nxt
                    r, rT = rrot.next()
                    self.act(r, pdn, AF.Ln, [pdnT], [rT])
                    self.act(r, r, AF.Exp, [rT], [rT], scale=-1.0)
                    self.tt(oT[:, h, :], po, r, ALU.mult, [poT, rT], [oTT])
                for hf in range(2):
                    slot = self.wrot.next()
                    sap, st = slot
                    dst = sap[:, 0:1024].rearrange("p (c n) -> p c n", c=2)
                    src = W["mla_wo"][j][2 * hp * 128:(2 * hp + 2) * 128, :].rearrange("(c p) n -> p c n", p=128)[:, :, hf * 512:(hf + 1) * 512]
                    P.dma("pool", lambda e, dst=dst, src=src: e.dma_start(out=dst, in_=src), writes=[st])
                    for d4 in range(4):
                        dc = hf * 4 + d4
                        pw, pwT = arot.next()
                        for h in range(2):
                            self.mm(pw, dst[:, h, d4 * 128:(d4 + 1) * 128], oT[:, h, :], h == 0, h == 1, [st, oTT], pwT)
                        xa = self.x3[:, dc, tok]
                        self.tt(xa, pw, xa, ALU.add, [pwT], [self.xT[dc][qc]])

    def qk_proj(self, w2d, col0, hn, hnT, prot):
        w_, wT = self.load_w(w2d, 8, col0, 128)
        p, pT = prot.next()
        for c in range(8):
            self.mm(p, w_[:, c, :], hn[:, c, :], c == 0, c == 7, [wT, hnT], pT)
        return p, pT

    def qk_fin(self, p, pT, gcol, t, dst, dstT, st):
        sqrot, rrot, t1, t1T, t2, t2T, kg, kgT, prot = st
        sq, sqt = sqrot.next()
        self.act(sq, p, AF.Square, [pT], [sqt])
        ss, ssT = self.bank[7]
        self.mm(ss, self.cm[:, 1, :], sq, True, True, [sqt, self.cmT], ssT)
        r, rT = rrot.next()
        self.rstd(r, rT, ss, ssT, 64)
        self.rope_chunk(p, pT, gcol, t, kg, kgT, t1, t1T, t2, t2T)
        self.tt(dst, t1, r, ALU.mult, [t1T, rT], [dstT])

    def qk_item(self, w2d, col0, hn, hnT, gcol, t, dst, dstT, st, pre=None):
        box = {}

        def proj():
            if pre is not None:
                pre()
            box["p"] = self.qk_proj(w2d, col0, hn, hnT, st[-1])

        def fin():
            self.qk_fin(box["p"][0], box["p"][1], gcol, t, dst, dstT, st)
        return (proj, fin)

    @staticmethod
    def run_pipe(items):
        n = len(items)
        if n:
            items[0][0]()
        for i in range(n):
            if i + 1 < n:
                items[i + 1][0]()
            items[i][1]()

    def alloc_zq(self):
        self.zq = []
        for par in range(2):
            z = self.abf(512, "zq"); zT = self.nT("zq")
            zr = slice(64, 128) if par == 0 else slice(0, 64)
            self.P.op("pool", lambda e, a=z[zr, :]: e.memset(a, 0.0), writes=[zT])
            self.zq.append((z, zT))

    def band_attn(self, mname, qT, qTT, kfn, vfn, VT, qc, nheads, hinfo, po_epilogue, ptrot, arot, scale):
        NT = self.NT
        off, OFF, Wt, width, W, dil = self.mlay[mname]
        dt_max = -(-W // 128)
        kts = list(range(max(0, 4 * qc - Wt // 128), min(NT - 1, 4 * qc + 3 + Wt // 128) + 1))
        for hh in range(nheads):
            qch, rows, kch = hinfo(hh)
            zq, zqT = self.zq[hh % 2]
            self.copy("pool", zq[rows, :], qT[rows, qch, :], [qTT], [zqT])
            po, poT = self.accr.next()
            contrib = {jq: [kt for kt in kts if abs(kt - (4 * qc + jq)) <= dt_max] for jq in range(4)}
            pv_list = [(kt, jq) for kt in kts for jq in range(4) if kt in contrib[jq]]
            pv_first, pv_last = pv_list[0], pv_list[-1]
            pend = None
            for kt in kts + [None]:
                if kt is not None:
                    ktok = slice(kt * 128, (kt + 1) * 128)
                    ps, psT = arot.next()
                    u0 = off + 512 * qc - 128 * kt + OFF
                    jqs = [jq for jq in range(4) if kt in contrib[jq]]
                    c0, c1 = jqs[0] * 128, (jqs[-1] + 1) * 128
                    self.mm(ps[:, c0:c1], self.cm[:, 4, :], self.mtab[:, u0 + c0:u0 + c1], True, False, [self.cmT], psT)
                    kap, kTt = kfn(kch, slice(0, 128), ktok, kt)
                    self.mm(ps[:, c0:c1], kap, zq[:, c0:c1], False, True, [kTt, zqT], psT)
                    pt, ptT = ptrot.next()
                    self.act(pt[:, c0:c1], ps[:, c0:c1], AF.Exp, [psT], [ptT], scale=scale)
                    nxt = (kt, pt, ptT)
                else:
                    nxt = None
                if pend is not None:
                    k0, pt0, ptT0 = pend
                    vap = vfn(hh, k0)
                    for jq in range(4):
                        cl = contrib[jq]
                        if k0 in cl:
                            self.mm(po[:, jq * 128:jq * 128 + 65], pt0[:, jq * 128:(jq + 1) * 128], vap,
                                    (k0, jq) == pv_first, (k0, jq) == pv_last, [VT[k0 // 4], ptT0], poT)
                pend = nxt
            po_epilogue(hh, po, poT)

    def swa(self, l):
        S, P, NC, NT = self.S, self.P, self.NC, self.NT
        W = self.W
        scale = 64.0 ** -0.5
        self.arena_reset()
        hnF = self.abf(8 * S, "hnF").rearrange("p (c s) -> p c s", c=8)
        hnFT = [self.nT("hnF") for _ in range(NC)]
        sq0 = Rot([(self.abf(512, "sq"), self.nT("sq")) for _ in range(3)])
        r0 = self.af32(512, "r"); r0T = self.nT("r")
        for t in range(NC):
            self.norm_x(t, ("attn", l), hnF[:, :, t * 512:(t + 1) * 512], hnFT[t], sq0, r0, r0T)
        mark = self.aoff
        for g in range(4):
            self.aoff = mark
            self.fence = P.fence()
            qT = self.abf(2 * S, "qT").rearrange("p (c s) -> p c s", c=2)
            qTT = [self.nT("qT") for _ in range(NC)]
            kT = self.abf(S, "kT")
            kTT = [self.nT("kT") for _ in range(NC)]
            Va = self.abf(NT * 80, "Va").rearrange("p (t n) -> p t n", n=80)
            VT = [self.nT("Va") for _ in range(NC)]
            otm_rot = Rot([(self.abf(4 * 256, "otm").rearrange("p (t n) -> p t n", n=256), self.nT("otm")) for _ in range(2)])
            oTrot = Rot([(self.abf(2 * 512, "oT").rearrange("p (c s) -> p c s", c=2), self.nT("oT")) for _ in range(2)])
            ptrot = Rot([(self.abf(512, "pt"), self.nT("pt")) for _ in range(4)])
            sqrot = Rot([(self.abf(512, "sq"), self.nT("sq")) for _ in range(3)])
            rrot = Rot([(self.af32(512, "r"), self.nT("r")) for _ in range(3)])
            t1 = self.af32(512, "t1"); t1T = self.nT("t1")
            t2 = self.af32(512, "t2"); t2T = self.nT("t2")
            kg = self.abf(512, "kg"); kgT = self.nT("kg")
            den = self.af32(8, "den"); denT = self.nT("den")
            self.alloc_zq()
            prot = Rot([self.bank[0], self.bank[1], self.bank[2]])
            st = (sqrot, rrot, t1, t1T, t2, t2T, kg, kgT, prot)
            for t in range(NC):
                P.op("pool", lambda e, a=Va[:, t * 4:(t + 1) * 4, 64:65]: e.memset(a, 1.0), writes=[VT[t]])
            items = []
            for t in range(NC):
                tok = slice(t * 512, (t + 1) * 512)
                hn, hnT = hnF[:, :, tok], hnFT[t]
                for jj in range(2):
                    items.append(self.qk_item(W["swa_wq"], (2 * g + jj) * 128, hn, hnT, self.gcol("swa_q"), t, qT[:, jj, tok], qTT[t], st))
                items.append(self.qk_item(W["swa_wk"], g * 128, hn, hnT, self.gcol("swa_k"), t, kT[:, tok], kTT[t], st))
                box = {}

                def vproj(t=t, hn=hn, hnT=hnT, box=box):
                    wv, wvT = self.load_w(W["swa_wv"], 8, g * 64, 64)
                    p, pT = prot.next()
                    for ii in range(4):
                        for c in range(8):
                            self.mm(p[:, ii * 64:(ii + 1) * 64], hn[:, c, ii * 128:(ii + 1) * 128], wv[:, c, :], c == 0, c == 7,
                                    [wvT, hnT], pT)
                    box["p"] = (p, pT)

                def vfin(t=t, box=box):
                    p, pT = box["p"]
                    self.copy("act", Va[:, t * 4:(t + 1) * 4, 0:64], p[:, 0:256].rearrange("p (t n) -> p t n", n=64), [pT], [VT[t]])
                items.append((vproj, vfin))
            self.run_pipe(items)
            if g == 1:
                self.dbg(qT[:, 0, 0:512], qTT[0])
                self.dbg(qT[:, 1, 0:512], qTT[0])
                self.dbg(kT[:, 0:512], kTT[0])
                self.dbg(Va[:, 0:4, :].rearrange("p t n -> p (t n)"), VT[0], n=320)
            arot = Rot([self.bank[0], self.bank[1], self.bank[2]])
            self.accr = Rot([self.bank[3], self.bank[4], self.bank[5]])
            sinkc = self.glay["sink"]
            for qc in range(NC):
                tok = slice(qc * 512, (qc + 1) * 512)
                otm, otmT = otm_rot.next()

                def epi(hh, po, poT, otm=otm, otmT=otmT, g=g):
                    po3 = po.rearrange("p (t n) -> p t n", n=128)
                    hq = 4 * g + hh
                    self.tsadd(den[:, 0:4], po3[:, :, 64], self.esink[:, hq:hq + 1], [poT, self.cT], [denT])
                    self.recip(den[:, 0:4], den[:, 0:4], [denT], [denT])
                    self.tt(otm[:, :, hh * 64:(hh + 1) * 64], po3[:, :, 0:64],
                            den[:, 0:4].unsqueeze(2).broadcast_to([128, 4, 64]), ALU.mult, [poT, denT], [otmT])

                self.band_attn("swa", qT[:, :, tok], qTT[qc],
                               lambda kch, rows, ktok, kt: (kT[rows, ktok], kTT[kt // 4]),
                               lambda hh, k0: Va[:, k0, 0:65], VT, qc, 4,
                               lambda hh: (hh // 2, slice(64 * (hh % 2), 64 * (hh % 2) + 64), 0),
                               epi, ptrot, arot, scale)
                if qc == 0:
                    self.dbg(otm[:, 0:2, :].rearrange("p t n -> p (t n)"), otmT)
                    self.dbg(otm[:, 2:4, :].rearrange("p t n -> p (t n)"), otmT)
                self.otm_to_x(otm, otmT, 2, oTrot, W["swa_wo"], g * 256, qc, arot)

    def otm_to_x(self, otm, otmT, nfc, oTrot, wo2d, row0, qc, arot):
        P = self.P
        tok = slice(qc * 512, (qc + 1) * 512)
        oT, oTT = oTrot.next()
        pb, pbT = self.bank[7]
        pbb = pb.bitcast(BF16)
        for fc in range(nfc):
            for jq in range(4):
                P.op("pe", lambda e, o=pbb[:, fc * 512 + jq * 128: fc * 512 + (jq + 1) * 128],
                     i=otm[:, jq, fc * 128:(fc + 1) * 128]: e.transpose(o, i, self.cm[:, 4, :]),
                     reads=[otmT, self.cmT], writes=[pbT])
        self.copy("act", oT, pbb[:, 0:nfc * 512].rearrange("p (c s) -> p c s", c=nfc), [pbT], [oTT])
        if getattr(self, "debug", False) and self.dbg_i in (96, 97):
            self.dbg(oT[:, 0, :], oTT)
            self.dbg(oT[:, 1, :], oTT)
        for hf in range(2):
            slot = self.wrot.next()
            sap, st = slot
            n = 1024 // nfc
            assert n == 512
            dst = sap[:, 0:1024].rearrange("p (c n) -> p c n", c=nfc)
            src = wo2d[row0:row0 + nfc * 128, :].rearrange("(c p) n -> p c n", p=128)[:, :, hf * 512:(hf + 1) * 512]
            P.dma("pool", lambda e, dst=dst, src=src: e.dma_start(out=dst, in_=src), writes=[st])
            for d4 in range(4):
                dc = hf * 4 + d4
                pw, pwT = arot.next()
                for fc in range(nfc):
                    self.mm(pw, dst[:, fc, d4 * 128:(d4 + 1) * 128], oT[:, fc, :], fc == 0, fc == nfc - 1, [st, oTT], pwT)
                xa = self.x3[:, dc, tok]
                self.tt(xa, pw, xa, ALU.add, [pwT], [self.xT[dc][qc]])

    def dil(self, l):
        S, P, NC, NT = self.S, self.P, self.NC, self.NT
        W = self.W
        scale = 64.0 ** -0.5
        names = ["d1", "d4", "d16"]
        self.arena_reset()
        otmF = self.abf(NT * 512, "otmF").rearrange("p (t n) -> p t n", n=512)
        otmFT = [self.nT("otmF") for _ in range(NC)]
        mark = self.aoff
        for hf in range(2):
            self.aoff = mark
            self.fence = P.fence()
            hnrot = Rot([(self.abf(8 * 512, "hn").rearrange("p (c s) -> p c s", c=8), self.nT("hn")) for _ in range(2)])
            qT = self.abf(2 * S, "qT").rearrange("p (c s) -> p c s", c=2)
            kT = self.abf(2 * S, "kT").rearrange("p (c s) -> p c s", c=2)
            Va = self.abf(NT * 264, "Va").rearrange("p (t h n) -> p t h n", h=4, n=66)
            nacc = self.af32(NT * 264, "nacc").rearrange("p (t h n) -> p t h n", h=4, n=66)
            naccT = [[self.nT("nacc") for _ in range(4)] for _ in range(NC)]
            ptrot = Rot([(self.abf(512, "pt"), self.nT("pt")) for _ in range(3)])
            sqrot = Rot([(self.abf(512, "sq"), self.nT("sq")) for _ in range(3)])
            rrot = Rot([(self.af32(512, "r"), self.nT("r")) for _ in range(2)])
            t1 = self.af32(512, "t1"); t1T = self.nT("t1")
            t2 = self.af32(512, "t2"); t2T = self.nT("t2")
            kg = self.abf(512, "kg"); kgT = self.nT("kg")
            den = self.af32(8, "den"); denT = self.nT("den")
            self.alloc_zq()
            prot = Rot([self.bank[0], self.bank[1], self.bank[2]])
            st = (sqrot, rrot, t1, t1T, t2, t2T, kg, kgT, prot)
            for gi in range(3):
                qTT = [self.nT("qT") for _ in range(NC)]
                kTT = [self.nT("kT") for _ in range(NC)]
                VT = [self.nT("Va") for _ in range(NC)]
                if gi > 0:
                    f = P.fence()
                    for lst in (qTT, kTT, VT):
                        for tt_ in lst:
                            tt_.readers = list(f)
                for t in range(NC):
                    P.op("pool", lambda e, a=Va[:, t * 4:(t + 1) * 4, :, 64:65]: e.memset(a, 1.0), writes=[VT[t]])
                hns = {}

                def donorm(t):
                    if t < NC and t not in hns:
                        hn_, hnT_ = hnrot.next()
                        r, rT = rrot.next()
                        self.norm_x(t, ("attn", l), hn_, hnT_, sqrot, r, rT)
                        hns[t] = (hn_, hnT_)
                donorm(0)
                items = []
                for t in range(NC):
                    tok = slice(t * 512, (t + 1) * 512)
                    first = True
                    for jj in range(2):
                        col = gi * 512 + hf * 256 + jj * 128
                        for (cc, gk, dst_, dT_) in ((col, "dil_q", qT[:, jj, tok], qTT[t]), (1536 + col, "dil_k", kT[:, jj, tok], kTT[t])):
                            box = {}

                            def proj(t=t, cc=cc, box=box, first=first):
                                if first:
                                    donorm(t + 1)
                                box["p"] = self.qk_proj(W["dil_wqkv"], cc, hns[t][0], hns[t][1], prot)

                            def fin(t=t, gk=gk, dst_=dst_, dT_=dT_, box=box):
                                self.qk_fin(box["p"][0], box["p"][1], self.gcol(gk), t, dst_, dT_, st)
                            items.append((proj, fin))
                            first = False
                    for i2 in range(2):
                        box = {}

                        def vproj(t=t, i2=i2, box=box):
                            hn_, hnT_ = hns[t]
                            wv, wvT = self.load_w(W["dil_wqkv"], 8, 3072 + gi * 512 + hf * 256, 128)
                            wv2, wv2T = self.load_w(W["dil_wqkv"], 8, 3072 + gi * 512 + hf * 256 + 128, 128)
                            p, pT = prot.next()
                            for ii in range(2):
                                tl = i2 * 2 + ii
                                for (wv_, wvT_, cc) in ((wv, wvT, 0), (wv2, wv2T, 128)):
                                    for c in range(8):
                                        self.mm(p[:, ii * 256 + cc: ii * 256 + cc + 128], hn_[:, c, tl * 128:(tl + 1) * 128], wv_[:, c, :],
                                                c == 0, c == 7, [wvT_, hnT_], pT)
                            box["p"] = (p, pT)

                        def vfin(t=t, i2=i2, box=box):
                            p, pT = box["p"]
                            t0_ = t * 4 + i2 * 2
                            self.copy("act", Va[:, t0_:t0_ + 2, :, 0:64], p.rearrange("p (t h n) -> p t h n", h=4, n=64), [pT], [VT[t]])
                        items.append((vproj, vfin))
                self.run_pipe(items)
                arot = Rot([self.bank[0], self.bank[1], self.bank[2]])
                self.accr = Rot([self.bank[3], self.bank[4], self.bank[5]])
                for qc in range(NC):
                    tok = slice(qc * 512, (qc + 1) * 512)

                    def epi(hh, po, poT, qc=qc, gi=gi):
                        po3 = po.rearrange("p (t n) -> p t n", n=128)[:, :, 0:65]
                        dst = nacc[:, qc * 4:(qc + 1) * 4, hh, 0:65]
                        if gi == 0:
                            self.copy("dve", dst, po3, [poT], [naccT[qc][hh]])
                        else:
                            self.tt(dst, po3, dst, ALU.add, [poT], [naccT[qc][hh]])

                    self.band_attn(names[gi], qT[:, :, tok], qTT[qc],
                                   lambda kch, rows, ktok, kt: (kT[rows, kch, ktok], kTT[kt // 4]),
                                   lambda hh, k0: Va[:, k0, hh, 0:65], VT, qc, 4,
                                   lambda hh: (hh // 2, slice(64 * (hh % 2), 64 * (hh % 2) + 64), hh // 2),
                                   epi, ptrot, arot, scale)
            for qc in range(NC):
                for hh in range(4):
                    nq = nacc[:, qc * 4:(qc + 1) * 4, hh, :]
                    c0 = hf * 256 + hh * 64
                    self.recip(den[:, 0:4], nq[:, :, 64], [naccT[qc][hh]], [denT])
                    self.tt(otmF[:, qc * 4:(qc + 1) * 4, c0:c0 + 64], nq[:, :, 0:64],
                            den[:, 0:4].unsqueeze(2).broadcast_to([128, 4, 64]), ALU.mult, [naccT[qc][hh], denT], [otmFT[qc]])
        self.aoff = mark
        self.fence = P.fence()
        oTrot = Rot([(self.abf(2 * 512, "oT").rearrange("p (c s) -> p c s", c=2), self.nT("oT")) for _ in range(2)])
        arot = Rot([self.bank[0], self.bank[1], self.bank[2]])
        for qc in range(NC):
            for hf in range(2):
                self.otm_to_x(otmF[:, qc * 4:(qc + 1) * 4, hf * 256:(hf + 1) * 256], otmFT[qc], 2, oTrot, W["dil_wo"], hf * 256, qc, arot)

    def build(self):
        S, NSEQ = self.S, self.NSEQ
        nc = bass.Bass("TRN2", target_bir_lowering=False)
        self.nc = nc
        xin = nc.dram_tensor("xT", [NSEQ, 8, 128, S], F32, kind="ExternalInput").ap()
        yout = nc.dram_tensor("yT", [NSEQ, 8, 128, S], F32, kind="ExternalOutput").ap()
        rope_h = nc.dram_tensor("rope", [128, 2, S], F32, kind="ExternalInput").ap()
        cm_h = nc.dram_tensor("cmat", [128, 6, 128], F32, kind="ExternalInput").ap()
        mt_h = nc.dram_tensor("mtab", [128, self.MW], F32, kind="ExternalInput").ap()
        g_h = nc.dram_tensor("gains", [128, self.NG], F32, kind="ExternalInput").ap()
        self.W = {k: nc.dram_tensor(k, shp, F32, kind="ExternalInput").ap() for k, shp in WSHAPES.items()}
        self.NDBG = 12
        self.dbg_i = 0
        if getattr(self, "debug", False):
            self.dbg_out = nc.dram_tensor("dbg", [self.NDBG, 128, 512], F32, kind="ExternalOutput").ap()
            self.dbg_sem = T("dbgsem")
        with ExitStack() as es:
            sb = lambda name, shape, dt: es.enter_context(nc.sbuf_tensor(name, shape, dt))
            xs = sb("xs", [128, 8 * S], F32)
            self.x3 = xs[:, :].rearrange("p (c s) -> p c s", c=8)
            self.rope = sb("rope_sb", [128, 2 * S], BF16)[:, :].rearrange("p (c s) -> p c s", c=2)
            self.cm = sb("cm_sb", [128, 6 * 128], BF16)[:, :].rearrange("p (c s) -> p c s", c=6)
            self.mtab = sb("mtab_sb", [128, self.MW], BF16)[:, :]
            self.gains = sb("gains_sb", [128, self.NG], F32)[:, :]
            self.esink = sb("esink", [128, 16], F32)[:, :]
            self.epsb = sb("epsb", [128, 1], F32)[:, :]
            self.arena = sb("arena", [128, ARENA], BF16)[:, :]
            wslots = [sb("ws%d" % i, [128, 1024], BF16)[:, :] for i in range(8)]
            wdslots = [sb("wd%d" % i, [128, NFB * 128], BF16)[:, :] for i in range(2)]
            banks = [es.enter_context(nc.psum_tensor("pb%d" % i, [128, 512], F32))[:, :] for i in range(8)]
            if getattr(self, "debug", False):
                self.dbg_stage = [sb("dbgst%d" % i, [128, 512], F32)[:, :] for i in range(self.NDBG)]
            P = Prog(nc)
            self.P = P
            self.bank = [(banks[i], T("bank%d" % i, excl=True)) for i in range(8)]
            self.wrot = Rot([(wslots[i], T("ws%d" % i)) for i in range(8)])
            self.wdrot = Rot([(wdslots[i], T("wd%d" % i)) for i in range(2)])
            self.cT = T("consts")
            self.cmT = T("cm")
            self.xT = [[T("x%d_%d" % (c, t)) for t in range(self.NC)] for c in range(8)]
            xsem = [T("xsem%d" % c) for c in range(8)]
            self.fence = []
            self.ropeT = T("rope")
            P.dma("pool", lambda e: e.dma_start(out=self.rope, in_=rope_h), writes=[self.ropeT])
            P.dma("sp", lambda e: e.dma_start(out=self.gains, in_=g_h), writes=[self.cT])
            P.dma("pool", lambda e: e.dma_start(out=self.cm, in_=cm_h), writes=[self.cmT])
            P.dma("pool", lambda e: e.dma_start(out=self.mtab, in_=mt_h), writes=[self.cmT])
            P.op("dve", lambda e: e.memset(self.epsb, EPS), writes=[self.cT])
            sc = self.glay["sink"]
            self.act(self.esink, self.gains[:, sc:sc + 16], AF.Exp, [self.cT], [self.cT])
            for s in range(NSEQ):
                for c in range(8):
                    P.dma("sp", lambda e, c=c, s=s: e.dma_start(out=self.x3[:, c, :], in_=xin[s, c]),
                          writes=self.xT[c], sem_tile=xsem[c])
                for l in self.layers:
                    kind = l % 3
                    if kind == 0:
                        self.mla(l, l // 3)
                    elif kind == 1:
                        self.swa(l)
                    else:
                        self.dil(l)
                    if self.do_ffn:
                        self.ffn(l)
                for c in range(8):
                    P.dma("sp", lambda e, c=c, s=s: e.dma_start(out=yout[s, c], in_=self.x3[:, c, :]),
                          reads=self.xT[c], sem_tile=xsem[c])
            fin = list(xsem)
            if getattr(self, "debug", False) and self.dbg_i > 0:
                fin.append(self.dbg_sem)
            P.emit(final_tiles=fin)
        return nc


_CACHE = {}


def run_device(xT_cores, inp, S, NSEQ, layers=(0, 1, 2, 3), do_ffn=True):
    key = (S, NSEQ, tuple(layers), do_ffn)
    if key not in _CACHE:
        _CACHE[key] = Builder(S, NSEQ, layers, do_ffn).build()
    nc = _CACHE[key]
    rope, cm, mt = build_consts(S)
    gains = build_gains(inp)
    w = prep_weights(inp)
    in_maps = []
    for xc in xT_cores:
        m = {"xT": xc, "rope": rope, "cmat": cm, "mtab": mt, "gains": gains}
        m.update(w)
        in_maps.append(m)
    res = run_bass_kernel_spmd(nc, in_maps, core_ids=list(range(len(xT_cores))))
    return [r["yT"] for r in res.results]


def kernel(**inputs):
    xp = np.asarray(inputs["x_prompt"], np.float32)
    xs_ = np.asarray(inputs["x_sample"], np.float32)
    B, S, _ = xp.shape
    Bd = xs_.shape[0]
    allx = np.concatenate([xp, xs_], axis=0)
    n = allx.shape[0]
    NSEQ = n // 8
    order = np.arange(n).reshape(8, NSEQ)
    cores = []
    for c in range(8):
        xc = allx[order[c]]
        xT = np.ascontiguousarray(xc.transpose(0, 2, 1)).reshape(NSEQ, 8, 128, S)
        cores.append(xT)
    outs = run_device(cores, inputs, S, NSEQ)
    y = np.empty_like(allx)
    for c in range(8):
        yT = np.asarray(outs[c], np.float32).reshape(NSEQ, D, S)
        y[order[c]] = yT.transpose(0, 2, 1)
    return (np.ascontiguousarray(y[:B]), np.ascontiguousarray(y[B:]))
```
